# Optimizing a Trainium2 kernel written in Bass

```python
import math
import jax, jax.numpy as jnp
from jax import lax
import numpy as np

D_MODEL = 1024
BATCH = 4
SEQ = 4096
DEPTH = 4

MLA_HEADS = 4
MLA_NOPE = 128
MLA_ROPE = 64
MLA_V = 128
MLA_Q_RANK = 384
MLA_KV_RANK = 256
ROPE_THETA = 10000.0
DIFF_HEADS = 4
DIFF_QK = 64
DIFF_V = 2 * DIFF_QK
MIX_WIDTH = MLA_HEADS * MLA_V + DIFF_HEADS * DIFF_V
IN_SIZES = (MLA_Q_RANK, MLA_KV_RANK, MLA_ROPE,
            DIFF_HEADS * 2 * DIFF_QK, DIFF_HEADS * 2 * DIFF_QK, DIFF_HEADS * DIFF_V)
IN_COLS = MLA_Q_RANK + MLA_KV_RANK + MLA_ROPE + 3 * DIFF_HEADS * 2 * DIFF_QK
NUM_BUCKETS = 32
MAX_DISTANCE = 128
D_FF_DENSE = 2816
N_EXPERTS = 8
TOP_K = 2
D_FF_EXPERT = 3584
N_DENSE = (DEPTH + 1) // 2
N_MOE = DEPTH // 2
DN_ALPHA = (2 * DEPTH) ** 0.25
DN_BETA = (8 * DEPTH) ** -0.25
BLOCK_Q = 128

kernel_name = "hybrid_mla_diffattn_deepnorm_moe"


def layer_norm(x, g, b, eps=1e-5):
    xf = x.astype(jnp.float32)
    mu = jnp.mean(xf, -1, keepdims=True)
    var = jnp.mean(jnp.square(xf - mu), -1, keepdims=True)
    return ((xf - mu) * lax.rsqrt(var + eps) * g.astype(jnp.float32) + b.astype(jnp.float32)).astype(x.dtype)


def rms_norm(x, g, eps):
    xf = x.astype(jnp.float32)
    return (xf * lax.rsqrt(jnp.mean(jnp.square(xf), -1, keepdims=True) + eps) * g.astype(jnp.float32)).astype(x.dtype)


def rope_tables(seq):
    pos = jnp.arange(seq, dtype=jnp.float32)
    inv = 1.0 / (ROPE_THETA ** (jnp.arange(0, MLA_ROPE, 2, dtype=jnp.float32) / MLA_ROPE))
    ang = pos[:, None] * inv[None, :]
    return jnp.cos(ang), jnp.sin(ang)


def apply_rope(x, cos, sin):
    xf = x.astype(jnp.float32)
    x1, x2 = jnp.split(xf, 2, axis=-1)
    return jnp.concatenate([x1 * cos - x2 * sin, x2 * cos + x1 * sin], -1).astype(x.dtype)


def t5_causal_bucket(n):
    max_exact = NUM_BUCKETS // 2
    nf = jnp.maximum(n, 1).astype(jnp.float32)
    large = max_exact + (jnp.log(nf / max_exact) / math.log(MAX_DISTANCE / max_exact)
                         * (NUM_BUCKETS - max_exact)).astype(jnp.int32)
    large = jnp.minimum(large, NUM_BUCKETS - 1)
    return jnp.where(n < max_exact, n, large)


def _block_positions(q0, seq):
    qpos = q0 + jnp.arange(BLOCK_Q)
    kpos = jnp.arange(seq)
    return qpos, kpos, kpos[None, :] <= qpos[:, None]


def mla_attention(q_nope, q_rope, k_nope, k_rope, v):
    b, s, h, dv = v.shape
    scale = (MLA_NOPE + MLA_ROPE) ** -0.5

    def block(q0):
        qn = lax.dynamic_slice_in_dim(q_nope, q0, BLOCK_Q, 1)
        qr = lax.dynamic_slice_in_dim(q_rope, q0, BLOCK_Q, 1)
        logits = (jnp.einsum('bqhd,bkhd->bhqk', qn, k_nope).astype(jnp.float32)
                  + jnp.einsum('bqhd,bkd->bhqk', qr, k_rope).astype(jnp.float32)) * scale
        _, _, mask = _block_positions(q0, s)
        p = jax.nn.softmax(jnp.where(mask, logits, -jnp.inf), axis=-1).astype(v.dtype)
        return jnp.einsum('bhqk,bkhd->bqhd', p, v)

    out = lax.map(block, jnp.arange(s // BLOCK_Q) * BLOCK_Q)
    return jnp.moveaxis(out, 0, 1).reshape(b, s, h, dv)


def diff_attention(q, k, v, bias_dist, lam):
    b, s, h, dv = v.shape
    scale = DIFF_QK ** -0.5

    def block(q0):
        qb = lax.dynamic_slice_in_dim(q, q0, BLOCK_Q, 1)
        logits = jnp.einsum('bqhcd,bkhcd->bhcqk', qb, k).astype(jnp.float32) * scale
        qpos, kpos, mask = _block_positions(q0, s)
        rel = jnp.maximum(qpos[:, None] - kpos[None, :], 0)
        bias = jnp.transpose(bias_dist[rel], (2, 0, 1))
        logits = jnp.where(mask, logits + bias[None, :, None], -jnp.inf)
        p = jax.nn.softmax(logits, axis=-1)
        a = (p[:, :, 0] - lam * p[:, :, 1]).astype(v.dtype)
        return jnp.einsum('bhqk,bkhd->bqhd', a, v)

    out = lax.map(block, jnp.arange(s // BLOCK_Q) * BLOCK_Q)
    return jnp.moveaxis(out, 0, 1).reshape(b, s, h, dv)


def hybrid_mixer(x, w_in, q_norm, kv_norm, w_uq, w_uk, w_uv, lam_params, diff_norm, w_o,
                 cos, sin, bias_dist, lambda_init):
    b, s, _ = x.shape
    h = x @ w_in
    splits, acc = [], 0
    for size in IN_SIZES[:-1]:
        acc += size
        splits.append(acc)
    c_q, c_kv, k_r, dq, dk, dv = jnp.split(h, splits, axis=-1)

    q = (rms_norm(c_q, q_norm, 1e-6) @ w_uq).reshape(b, s, MLA_HEADS, MLA_NOPE + MLA_ROPE)
    q_nope = q[..., :MLA_NOPE]
    q_rope = apply_rope(q[..., MLA_NOPE:], cos[None, :, None], sin[None, :, None])
    c_kv = rms_norm(c_kv, kv_norm, 1e-6)
    k_nope = (c_kv @ w_uk).reshape(b, s, MLA_HEADS, MLA_NOPE)
    v_mla = (c_kv @ w_uv).reshape(b, s, MLA_HEADS, MLA_V)
    k_rope = apply_rope(k_r, cos[None], sin[None])
    mla_out = mla_attention(q_nope, q_rope, k_nope, k_rope, v_mla).reshape(b, s, MLA_HEADS * MLA_V)

    lp = lam_params.astype(jnp.float32)
    lam = jnp.exp(jnp.sum(lp[0] * lp[1])) - jnp.exp(jnp.sum(lp[2] * lp[3])) + lambda_init
    dq = dq.reshape(b, s, DIFF_HEADS, 2, DIFF_QK)
    dk = dk.reshape(b, s, DIFF_HEADS, 2, DIFF_QK)
    dv = dv.reshape(b, s, DIFF_HEADS, DIFF_V)
    d_out = diff_attention(dq, dk, dv, bias_dist, lam)
    d_out = rms_norm(d_out, diff_norm, 1e-5) * (1.0 - lambda_init)

    merged = jnp.concatenate([mla_out, d_out.reshape(b, s, DIFF_HEADS * DIFF_V)], axis=-1)
    return merged @ w_o


def swiglu(x, wg, wu, wd):
    return (jax.nn.silu(x @ wg) * (x @ wu)) @ wd


def moe_swiglu(x, router, wg, wu, wd):
    b, s, d = x.shape
    xf = x.reshape(b * s, d)
    logits = (xf @ router).astype(jnp.float32)
    top_v, top_i = lax.top_k(logits, TOP_K)
    gates = jax.nn.softmax(top_v, axis=-1)
    dense_g = jnp.sum(jax.nn.one_hot(top_i, N_EXPERTS, dtype=jnp.float32) * gates[..., None], axis=1).astype(x.dtype)
    y = jnp.zeros_like(xf)
    for e in range(N_EXPERTS):
        y = y + dense_g[:, e:e + 1] * swiglu(xf, wg[e], wu[e], wd[e])
    return y.reshape(b, s, d)


def setup_inputs(seed: int = 0) -> dict:
    key = jax.random.key(seed)
    ks = jax.random.split(key, 24)

    def nrm(k, shape, scale):
        return jax.random.normal(k, shape, jnp.float32) * scale

    col_scale = jnp.concatenate([jnp.ones((IN_COLS - DIFF_HEADS * DIFF_V,), jnp.float32),
                                 jnp.full((DIFF_HEADS * DIFF_V,), DN_BETA, jnp.float32)])
    return {
        "x": nrm(ks[0], (BATCH, SEQ, D_MODEL), 1.0),
        "rel_bias": nrm(ks[1], (NUM_BUCKETS, DIFF_HEADS), 0.5),
        "w_in": nrm(ks[2], (DEPTH, D_MODEL, IN_COLS), D_MODEL ** -0.5) * col_scale,
        "mla_q_norm": 1.0 + nrm(ks[3], (DEPTH, MLA_Q_RANK), 0.02),
        "mla_kv_norm": 1.0 + nrm(ks[4], (DEPTH, MLA_KV_RANK), 0.02),
        "mla_w_uq": nrm(ks[5], (DEPTH, MLA_Q_RANK, MLA_HEADS * (MLA_NOPE + MLA_ROPE)), MLA_Q_RANK ** -0.5),
        "mla_w_uk": nrm(ks[6], (DEPTH, MLA_KV_RANK, MLA_HEADS * MLA_NOPE), MLA_KV_RANK ** -0.5),
        "mla_w_uv": nrm(ks[7], (DEPTH, MLA_KV_RANK, MLA_HEADS * MLA_V), MLA_KV_RANK ** -0.5 * DN_BETA),
        "diff_lambda": nrm(ks[8], (DEPTH, 4, DIFF_QK), 0.1),
        "diff_norm": 1.0 + nrm(ks[9], (DEPTH, DIFF_V), 0.02),
        "w_o": nrm(ks[10], (DEPTH, MIX_WIDTH, D_MODEL), MIX_WIDTH ** -0.5 * DN_BETA),
        "ln1_g": 1.0 + nrm(ks[11], (DEPTH, D_MODEL), 0.02),
        "ln1_b": nrm(ks[12], (DEPTH, D_MODEL), 0.02),
        "ln2_g": 1.0 + nrm(ks[13], (DEPTH, D_MODEL), 0.02),
        "ln2_b": nrm(ks[14], (DEPTH, D_MODEL), 0.02),
        "ffn_w_gate": nrm(ks[15], (N_DENSE, D_MODEL, D_FF_DENSE), D_MODEL ** -0.5 * DN_BETA),
        "ffn_w_up": nrm(ks[16], (N_DENSE, D_MODEL, D_FF_DENSE), D_MODEL ** -0.5 * DN_BETA),
        "ffn_w_down": nrm(ks[17], (N_DENSE, D_FF_DENSE, D_MODEL), D_FF_DENSE ** -0.5 * DN_BETA),
        "moe_router": nrm(ks[18], (N_MOE, D_MODEL, N_EXPERTS), D_MODEL ** -0.5),
        "moe_w_gate": nrm(ks[19], (N_MOE, N_EXPERTS, D_MODEL, D_FF_EXPERT), D_MODEL ** -0.5 * DN_BETA),
        "moe_w_up": nrm(ks[20], (N_MOE, N_EXPERTS, D_MODEL, D_FF_EXPERT), D_MODEL ** -0.5 * DN_BETA),
        "moe_w_down": nrm(ks[21], (N_MOE, N_EXPERTS, D_FF_EXPERT, D_MODEL), D_FF_EXPERT ** -0.5 * DN_BETA),
    }


def reference(x, rel_bias, w_in, mla_q_norm, mla_kv_norm, mla_w_uq, mla_w_uk, mla_w_uv,
              diff_lambda, diff_norm, w_o, ln1_g, ln1_b, ln2_g, ln2_b,
              ffn_w_gate, ffn_w_up, ffn_w_down, moe_router, moe_w_gate, moe_w_up, moe_w_down):
    seq = x.shape[1]
    cos, sin = rope_tables(seq)
    bias_dist = rel_bias.astype(jnp.float32)[t5_causal_bucket(jnp.arange(seq))]
    for l in range(DEPTH):
        lambda_init = 0.8 - 0.6 * math.exp(-0.3 * l)
        mix = hybrid_mixer(x, w_in[l], mla_q_norm[l], mla_kv_norm[l], mla_w_uq[l], mla_w_uk[l],
                           mla_w_uv[l], diff_lambda[l], diff_norm[l], w_o[l],
                           cos, sin, bias_dist, lambda_init)
        x = layer_norm(DN_ALPHA * x + mix, ln1_g[l], ln1_b[l])
        if l % 2 == 0:
            f = swiglu(x, ffn_w_gate[l // 2], ffn_w_up[l // 2], ffn_w_down[l // 2])
        else:
            f = moe_swiglu(x, moe_router[l // 2], moe_w_gate[l // 2], moe_w_up[l // 2], moe_w_down[l // 2])
        x = layer_norm(DN_ALPHA * x + f, ln2_g[l], ln2_b[l])
    return x
```

```python
import math
from contextlib import ExitStack

import numpy as np
import ml_dtypes

import concourse.bass as bass
import concourse.mybir as mybir
from concourse.bass_utils import run_bass_kernel_spmd

F32, BF16 = mybir.dt.float32, mybir.dt.bfloat16
AF = mybir.ActivationFunctionType
ALU = mybir.AluOpType
AX = mybir.AxisListType
NPBF = ml_dtypes.bfloat16

D = 1024
DEPTH = 4
NH = 4
INW = 2304
FD = 2816
FE = 3584
NE = 8
DN_ALPHA = (2 * DEPTH) ** 0.25
NEG = -30000.0
ENG = ["pe", "act", "dve", "pool", "sp"]


_UID = [0]


def _uname(n):
    _UID[0] += 1
    return "%s_u%d" % (n, _UID[0])


class Gen:
    def __init__(self, nc, es):
        self.nc = nc
        self.es = es
        self.ops = {e: [] for e in ENG}
        self.cnt = {e: 0 for e in ENG}
        self.seen = {e: {} for e in ENG}
        self.sem = {e: es.enter_context(nc.semaphore("s_" + e)) for e in ENG}
        self.dsem = {}
        self.nops = 0

    def _dsem(self, name):
        if name not in self.dsem:
            self.dsem[name] = [self.es.enter_context(self.nc.semaphore("d_" + name)), 0]
        return self.dsem[name]

    def _wait(self, eng, tok):
        if tok is None:
            return
        if isinstance(tok, (list, tuple)) and (len(tok) == 0 or not isinstance(tok[0], str)):
            for t in tok:
                self._wait(eng, t)
            return
        kind, key, v = tok
        if kind == "e":
            sem = self.sem[key]
            k = "e:" + key
        else:
            sem = self.dsem[key][0]
            k = "d:" + key
        if self.seen[eng].get(k, 0) >= v:
            return
        self.seen[eng][k] = v
        self.ops[eng].append(lambda h, sem=sem, v=v: h.wait_ge(sem, v))

    def op(self, eng, fn, deps=(), sig=True):
        self._wait(eng, deps)
        self.nops += 1
        if sig:
            self.cnt[eng] += 1
            sem = self.sem[eng]
            self.ops[eng].append(lambda h, fn=fn, sem=sem: fn(h).then_inc(sem, 1))
            return ("e", eng, self.cnt[eng])
        self.ops[eng].append(lambda h, fn=fn: fn(h))
        return None

    def dma(self, eng, name, out, in_, deps=()):
        self._wait(eng, deps)
        s = self._dsem(name)
        s[1] += 16
        sem = s[0]
        self.ops[eng].append(lambda h, out=out, in_=in_, sem=sem: h.dma_start(out=out, in_=in_).then_inc(sem, 16))
        return ("d", name, s[1])

    def all_tokens(self):
        toks = [("e", e, self.cnt[e]) for e in ENG if self.cnt[e] > 0]
        toks += [("d", n, s[1]) for n, s in self.dsem.items() if s[1] > 0]
        return toks

    def barrier(self):
        toks = self.all_tokens()
        for e in ENG:
            self._wait(e, toks)

    def flush(self, scope=None):
        nc = self.nc
        if scope is not None:
            with nc.named_scope(_uname(scope)):
                self.flush(None)
            return
        with nc.Block() as block:
            for e, deco in (("pe", block.tensor), ("act", block.scalar), ("dve", block.vector),
                            ("pool", block.gpsimd), ("sp", block.sync)):
                ops = self.ops[e]

                def run(h, ops=ops):
                    for f in ops:
                        f(h)
                deco(run)
        self.ops = {e: [] for e in ENG}


class Psum:
    def __init__(self, g, nc, es):
        self.g = g
        self.banks = [es.enter_context(nc.psum_tensor("ps%d" % i, [128, 512], F32)) for i in range(7)]
        self.tb = es.enter_context(nc.psum_tensor("pstb", [128, 1024], BF16))
        self.free = [None] * 7
        self.tbfree = None
        self.rr = 0

    def get(self, allowed=None):
        allowed = allowed if allowed is not None else list(range(7))
        k = allowed[self.rr % len(allowed)]
        self.rr += 1
        return k

    def rel(self, k, tok):
        self.free[k] = tok


def mm_group(g, out_ap, pairs, deps=(), sig_last=True):
    n = len(pairs)
    tok = None
    for i, (l, r) in enumerate(pairs):
        tok = g.op("pe", lambda h, l=l, r=r, i=i: h.matmul(out_ap, l, r, start=(i == 0), stop=(i == n - 1)),
                   deps=deps if i == 0 else (), sig=(sig_last and i == n - 1))
    return tok


def phase_A(g, nc, P, TOK, x_d, w, o, cst, xdeps=()):
    NT = TOK // 512
    outtoks = []
    with ExitStack() as es:
        sb = lambda n, s, d: es.enter_context(nc.sbuf_tensor(_uname(n), s, d))
        win = sb("a_win", [128, 8, INW], BF16)
        wuq = sb("a_wuq", [128, 3, 1024], BF16)
        wuk = sb("a_wuk", [128, 2, 512], BF16)
        wuv = sb("a_wuv", [128, 2, 512], BF16)
        wst = [sb("a_wst%d" % i, [128, 1024], F32) for i in range(2)]
        gq = sb("a_gq", [128, 3], F32)
        gkv = sb("a_gkv", [128, 2], F32)
        xin = [sb("a_xin%d" % i, [128, 1024], F32) for i in range(2)]
        xb = [sb("a_xb%d" % i, [128, 1024], BF16) for i in range(2)]
        xT = [sb("a_xT%d" % i, [128, 8, 512], BF16) for i in range(2)]
        craw = sb("a_craw", [128, 5, 512], F32)
        csq = sb("a_csq", [128, 5, 512], BF16)
        rbc = sb("a_rbc", [128, 2, 512], F32)
        cn = sb("a_cn", [128, 5, 512], BF16)
        cs = [sb("a_cs%d" % i, [128, 2, 512], F32) for i in range(2)]
        rt = [sb("a_rt%d" % i, [128, 512], F32) for i in range(3)]
        NST = 6
        stg = [sb("a_stg%d" % i, [128, 512], BF16) for i in range(NST)]
        ones = cst["ones"]
        ident = cst["ident"]

        g.barrier()
        wl = []
        for fc in range(8):
            wl.append(g.dma("pool", "a_win", win[:, fc, :], w["win"][fc * 128:(fc + 1) * 128, :]))
        tgq = g.dma("sp", "a_small", gq[:, :], w["gq"])
        tgkv = g.dma("sp", "a_small2", gkv[:, :], w["gkv"])
        wtok = {}
        k = 0
        wfree = [None, None]
        for (name, dst, src, nch, ncol, gt, gs) in (("wuq", wuq, w["wuq"], 3, 1024, tgq, gq),
                                                    ("wuk", wuk, w["wuk"], 2, 512, tgkv, gkv),
                                                    ("wuv", wuv, w["wuv"], 2, 512, tgkv, gkv)):
            for j in range(nch):
                b = k % 2
                k += 1
                t1 = g.dma("sp", "a_wst%d" % b, wst[b][:, 0:ncol], src[j * 128:(j + 1) * 128, :], deps=[wfree[b]])
                t2 = g.op("dve", lambda h, dst=dst, j=j, b=b, ncol=ncol, gs=gs: h.tensor_scalar(
                    dst[:, j, :], wst[b][:, 0:ncol], gs[:, j:j + 1], None, ALU.mult), deps=[t1, gt])
                wfree[b] = t2
                wtok[(name, j)] = t2
        wall = wl + list(wtok.values())

        import os
        DBG = int(os.environ.get("DBG_A", "99"))
        stg_free = [None] * NST
        stg_i = [0]

        cur_stage = [0]

        def stage():
            i = stg_i[0] % NST
            stg_i[0] += 1
            cur_stage[0] = i
            return i

        xin_free = [None, None]
        xb_free = [None, None]
        xT_free = [None, None]
        cs_free = [None, None]
        craw_free = None
        blk_i = 0
        for t in range(NT if DBG >= 1 else 0):
            tb = t % 2
            tok0 = t * 512
            tcs = g.dma("sp", "a_cs%d" % tb, cs[tb][:, 0, :], w["cos"][:, tok0:tok0 + 512], deps=[cs_free[tb]])
            tcs2 = g.dma("sp", "a_cs%d" % tb, cs[tb][:, 1, :], w["sin"][:, tok0:tok0 + 512])
            xT_ready = []
            for blk in range(4):
                b = blk_i % 2
                blk_i += 1
                r0 = tok0 + blk * 128
                tl = g.dma("sp", "a_xin%d" % b, xin[b][:, :], x_d[r0:r0 + 128, :], deps=[xin_free[b], xdeps])
                tc = g.op("act", lambda h, b=b: h.activation(out=xb[b][:, :], in_=xin[b][:, :], func=AF.Copy),
                          deps=[tl, xb_free[b]])
                xin_free[b] = tc
                tp = None
                for fc in range(8):
                    tp = g.op("pe", lambda h, b=b, fc=fc: h.transpose(
                        P.tb[:, fc * 128:(fc + 1) * 128], xb[b][:, fc * 128:(fc + 1) * 128], ident[:, :]),
                        deps=[tc, P.tbfree] if fc == 0 else (), sig=(fc == 7))
                xb_free[b] = tp
                te = g.op("dve", lambda h, tb=tb, blk=blk: h.tensor_copy(
                    out=xT[tb][:, :, blk * 128:(blk + 1) * 128],
                    in_=P.tb[:, :].rearrange("p (c n) -> p c n", c=8)), deps=[tp, xT_free[tb]])
                P.tbfree = te
                xT_ready.append(te)
            xTt = xT[tb]
            if DBG < 2:
                continue

            def fm_chunk(c0, ncols, deps):
                k = P.get()
                tok = mm_group(g, P.banks[k][0:ncols, :],
                               [(win[:, fc, c0:c0 + ncols], xTt[:, fc, :]) for fc in range(8)],
                               deps=[deps, P.free[k], wl])
                return k, tok

            sq_toks = []
            raw_toks = []
            for j in range(5):
                k, tok = fm_chunk(j * 128, 128, xT_ready)
                t2 = g.op("dve", lambda h, k=k, j=j: h.tensor_copy(out=craw[:, j, :], in_=P.banks[k][:, :]),
                          deps=[tok, craw_free])
                t1 = g.op("act", lambda h, j=j: h.activation(out=csq[:, j, :], in_=craw[:, j, :], func=AF.Square),
                          deps=[t2, craw_free])
                P.rel(k, t2)
                sq_toks.append(t1)
                raw_toks.append(t2)
            if DBG < 3:
                continue
            rtoks = []
            for (which, js, n) in ((0, (0, 1, 2), 384.0), (1, (3, 4), 256.0)):
                k = P.get()
                tok = mm_group(g, P.banks[k][:, :], [(ones[:, :], csq[:, j, :]) for j in js],
                               deps=[[sq_toks[j] for j in js], P.free[k]])
                t1 = g.op("act", lambda h, k=k, which=which, n=n: h.activation(
                    out=rbc[:, which, :], in_=P.banks[k][:, :], func=AF.Ln, bias=1e-6, scale=1.0 / n), deps=[tok, craw_free])
                P.rel(k, t1)
                t2 = g.op("act", lambda h, which=which: h.activation(
                    out=rbc[:, which, :], in_=rbc[:, which, :], func=AF.Exp, scale=-0.5), deps=[t1])
                rtoks.append(t2)
            cn_toks = []
            for j in range(5):
                which = 0 if j < 3 else 1
                eng = "pool" if j % 2 == 0 else "dve"
                cn_toks.append(g.op(eng, lambda h, j=j, which=which: h.tensor_tensor(
                    cn[:, j, :], craw[:, j, :], rbc[:, which, :], ALU.mult), deps=[raw_toks[j], rtoks[which], craw_free]))
            craw_last = list(cn_toks)

            if DBG < 4:
                continue
            def up_chunk(wt, nk, c0, js0, deps):
                k = P.get()
                tok = mm_group(g, P.banks[k][:, :], [(wt[:, j, c0:c0 + 128], cn[:, js0 + j, :]) for j in range(nk)],
                               deps=[deps, P.free[k], wall])
                return k, tok

            def store(src_ap, dst_ap, dep):
                return g.dma("sp", "a_out%d" % cur_stage[0], dst_ap, src_ap, deps=[dep])

            def evac_store(k, tok, dst_ap, eng, scale=None, rows=128):
                i = stage()
                if eng == "act":
                    te = g.op("act", lambda h, k=k, i=i: h.activation(
                        out=stg[i][0:rows, :], in_=P.banks[k][0:rows, :], func=AF.Copy,
                        scale=(1.0 if scale is None else scale)), deps=[tok, stg_free[i]])
                else:
                    te = g.op("dve", lambda h, k=k, i=i: h.tensor_copy(out=stg[i][0:rows, :], in_=P.banks[k][0:rows, :]),
                              deps=[tok, stg_free[i]])
                P.rel(k, te)
                ts = store(stg[i][0:rows, :], dst_ap, te)
                stg_free[i] = ts
                outtoks.append(ts)
                return ts

            for hh in range(4):
                k, tok = up_chunk(wuq, 3, hh * 128, 0, cn_toks[0:3])
                craw_last.append(tok)
                evac_store(k, tok, o["qnT"][hh, :, tok0:tok0 + 512], "act" if hh % 2 == 0 else "dve")

            def rope(kA, tokA, kB, tokB, rows, dsts):
                t1 = g.op("dve", lambda h, tb=tb: h.tensor_tensor(rt[0][0:rows, :], P.banks[kA][0:rows, :], cs[tb][0:rows, 0, :], ALU.mult),
                          deps=[tokA, tcs, tcs2, stg_free_rt[0]])
                t2 = g.op("dve", lambda h, tb=tb: h.tensor_tensor(rt[1][0:rows, :], P.banks[kB][0:rows, :], cs[tb][0:rows, 1, :], ALU.mult),
                          deps=[tokB, stg_free_rt[0]])
                P.rel(kA, t1)
                P.rel(kB, t2)
                i = stage()
                t3 = g.op("pool", lambda h, i=i: h.tensor_tensor(stg[i][0:rows, :], rt[0][0:rows, :], rt[1][0:rows, :], ALU.add),
                          deps=[t1, t2, stg_free[i]])
                stg_free_rt[0] = t3
                last = None
                for (r0, r1, dst) in dsts:
                    last = store(stg[i][r0:r1, :], dst, t3)
                    outtoks.append(last)
                stg_free[i] = last
                return t3

            stg_free_rt = [None]
            for rc in range(2):
                kA, tokA = up_chunk(wuq, 3, 512 + rc * 128, 0, cn_toks[0:3])
                kB, tokB = up_chunk(wuq, 3, 768 + rc * 128, 0, cn_toks[0:3])
                craw_last += [tokA, tokB]
                rope(kA, tokA, kB, tokB, 128, [(0, 64, o["qrT"][2 * rc, :, tok0:tok0 + 512]),
                                                (64, 128, o["qrT"][2 * rc + 1, :, tok0:tok0 + 512])])
            kA, tokA = fm_chunk(640, 64, xT_ready)
            kB, tokB = fm_chunk(704, 64, xT_ready)
            trope = rope(kA, tokA, kB, tokB, 64, [(0, 64, o["krT"][:, tok0:tok0 + 512])])
            cs_free[tb] = trope
            for hh in range(4):
                k, tok = up_chunk(wuk, 2, hh * 128, 3, cn_toks[3:5])
                craw_last.append(tok)
                evac_store(k, tok, o["knT"][hh, :, tok0:tok0 + 512], "act" if hh % 2 == 0 else "dve")
            for blk in range(4):
                k = P.get()
                tok = mm_group(g, P.banks[k][:, :], [(cn[:, 3 + j, blk * 128:(blk + 1) * 128], wuv[:, j, :]) for j in range(2)],
                               deps=[cn_toks[3:5], P.free[k], wall])
                craw_last.append(tok)
                evac_store(k, tok, o["vm"][tok0 + blk * 128: tok0 + (blk + 1) * 128, :], "act" if blk % 2 == 0 else "dve")
            craw_free = craw_last
            for hh in range(4):
                k, tok = fm_chunk(768 + hh * 128, 128, xT_ready)
                evac_store(k, tok, o["dqT"][hh, :, tok0:tok0 + 512], "act", scale=0.125)
            for hh in range(4):
                k, tok = fm_chunk(1280 + hh * 128, 128, xT_ready)
                evac_store(k, tok, o["dkT"][hh, :, tok0:tok0 + 512], "dve" if hh % 2 == 0 else "act")
            last_x = None
            for blk in range(4):
                k = P.get()
                tok = mm_group(g, P.banks[k][:, :], [(xTt[:, fc, blk * 128:(blk + 1) * 128], win[:, fc, 1792:2304]) for fc in range(8)],
                               deps=[xT_ready, P.free[k], wl])
                last_x = tok
                evac_store(k, tok, o["dv"][tok0 + blk * 128: tok0 + (blk + 1) * 128, :], "act" if blk % 2 == 0 else "dve")
            xT_free[tb] = last_x
        g.barrier()
        g.flush("phA")
    return outtoks


def phase_B(g, nc, P, S, NM, ND, i_, mT_d, cst, lam_init, indeps=()):
    NKB = S // 128
    NG = S // 512
    mla_scale = (128 + 64) ** -0.5
    outtoks = []
    with ExitStack() as es:
        sb = lambda n, s, d: es.enter_context(nc.sbuf_tensor(_uname(n), s, d))
        kT = [sb("b_kT%d" % i, [128, S], BF16) for i in range(2)]
        qT = [sb("b_qT%d" % i, [128, S], BF16) for i in range(2)]
        vv = [sb("b_v%d" % i, [128, NKB, 132], BF16) for i in range(2)]
        krT = sb("b_krT", [64, S], BF16)
        qrT = [sb("b_qrT%d" % i, [64, S], BF16) for i in range(2)]
        bias_f = sb("b_biasf", [128, 5, 512], F32)
        bias_t = sb("b_biast", [128, 512], F32)
        bias_hl = sb("b_biashl", [128, 2, 5, 512], BF16)
        cfar = sb("b_cfar", [128, max(ND, 1)], F32)
        lamp = sb("b_lamp", [128, 256], F32)
        lamt = sb("b_lamt", [128, 8], F32)
        dnorm = sb("b_dnorm", [128, 128], F32)
        NPT = 4
        pt = [sb("b_pt%d" % i, [128, 512], BF16) for i in range(NPT)]
        osb = [sb("b_osb%d" % i, [128, 4, 128], F32) for i in range(2)]
        rs = sb("b_rs", [128, 32], F32)
        obf = sb("b_obf", [128, 4, 128], BF16)
        osq = sb("b_osq", [128, 128], F32)
        mo = [sb("b_mo%d" % i, [128, 512], BF16) for i in range(2)]
        ones = cst["ones"]
        ident = cst["ident"]
        tri = cst["tri"]
        zero_c = cst["zero_c"]

        g.barrier()
        vinit = [g.op("pool", lambda h, i=i: h.memset(vv[i][:, :, 128:132], 1.0)) for i in range(2)]
        tkr = g.dma("sp", "b_kr", krT[:, :], i_["krT"], deps=[indeps]) if NM > 0 else None
        tlam = None
        if ND > 0:
            t0 = g.dma("sp", "b_small", lamp[:, :], i_["lamp"])
            t0b = g.dma("sp", "b_small2", dnorm[:, :], i_["dnorm"])
            t0c = g.dma("sp", "b_small3", cfar[:, :], i_["cfar"])
            t1 = g.op("dve", lambda h: h.tensor_tensor(lamp[:, 0:64], lamp[:, 0:64], lamp[:, 64:128], ALU.mult), deps=[t0])
            t2 = g.op("dve", lambda h: h.tensor_tensor(lamp[:, 128:192], lamp[:, 128:192], lamp[:, 192:256], ALU.mult), deps=[t1])
            t3 = g.op("dve", lambda h: h.tensor_reduce(lamt[:, 0:1], lamp[:, 0:64], AX.X, ALU.add), deps=[t2])
            t4 = g.op("dve", lambda h: h.tensor_reduce(lamt[:, 1:2], lamp[:, 128:192], AX.X, ALU.add), deps=[t3])
            t5 = g.op("act", lambda h: h.activation(out=lamt[:, 2:4], in_=lamt[:, 0:2], func=AF.Exp), deps=[t4])
            t6 = g.op("dve", lambda h: h.tensor_tensor(lamt[:, 4:5], lamt[:, 2:3], lamt[:, 3:4], ALU.subtract), deps=[t5])
            t7 = g.op("dve", lambda h: h.tensor_scalar(lamt[:, 5:6], lamt[:, 4:5], -1.0, -float(lam_init), ALU.mult, ALU.add), deps=[t6])
            t8 = g.op("dve", lambda h: h.tensor_scalar(dnorm[:, :], dnorm[:, :], float(1.0 - lam_init), None, ALU.mult), deps=[t0b, t7])
            tlam = [t7, t8, t0c]

        kfree = [None, None]
        qfree = [None, None]
        vfree = [vinit[0], vinit[1]]
        qrfree = [None, None]
        ptfree = [None] * NPT
        pt_i = [0]
        osb_free = [None, None]
        obf_free = [None]
        mo_free = [None, None]
        mo_i = [0]
        bias_free = [None]
        SBANKS = [0, 1, 2]
        lbu = [None]
        OBANKS = [3, 4, 5, 6]
        s_rr = [0]

        def oacc(qb):
            return P.banks[OBANKS[qb]][:, 0:129]

        heads = [("m", h) for h in range(NM)] + [("d", h) for h in range(ND)]
        for hi, (kind, hh) in enumerate(heads):
            hb = hi % 2
            if kind == "m":
                tk = g.dma("sp", "b_k%d" % hb, kT[hb][:, :], i_["knT"][hh, :, :], deps=[kfree[hb], indeps])
                tq = g.dma("sp", "b_q%d" % hb, qT[hb][:, :], i_["qnT"][hh, :, :], deps=[qfree[hb]])
                tqr = g.dma("sp", "b_qr%d" % hb, qrT[hb][:, :], i_["qrT"][hh, :, :], deps=[qrfree[hb]])
                tv = g.dma("sp", "b_v%d" % hb, vv[hb][:, :, 0:128],
                           i_["vm"][:, hh * 128:(hh + 1) * 128].rearrange("(kb p) d -> p kb d", p=128), deps=[vfree[hb]])
                ld = [tk, tq, tqr, tv, tkr]
                comps = [0]
            else:
                tk = g.dma("sp", "b_k%d" % hb, kT[hb][:, :], i_["dkT"][hh, :, :], deps=[kfree[hb], indeps])
                tq = g.dma("sp", "b_q%d" % hb, qT[hb][:, :], i_["dqT"][hh, :, :], deps=[qfree[hb]])
                tv = g.dma("sp", "b_v%d" % hb, vv[hb][:, :, 0:128],
                           i_["dv"][:, hh * 128:(hh + 1) * 128].rearrange("(kb p) d -> p kb d", p=128), deps=[vfree[hb]])
                tb0 = g.dma("sp", "b_bias", bias_f[:, :, :], i_["bias"][hh].rearrange("r p q -> p r q"), deps=[bias_free[0]])
                tb1 = g.op("dve", lambda h: h.tensor_copy(out=bias_hl[:, 0, :, :], in_=bias_f[:, :, :]), deps=[tb0, bias_free[0]])
                tbl = tb1
                for r in range(5):
                    ta = g.op("dve", lambda h, r=r: h.tensor_tensor(bias_t[:, :], bias_f[:, r, :], bias_hl[:, 0, r, :], ALU.subtract), deps=[tbl])
                    tbl = g.op("dve", lambda h, r=r: h.tensor_copy(out=bias_hl[:, 1, r, :], in_=bias_t[:, :]), deps=[ta])
                ld = [tk, tq, tv, tbl, tlam]
                comps = [0, 1]
            last_pe = None
            last_bias_use = None
            for G in range(NG):
                q0 = G * 512
                comp_done = []
                for c in comps:
                    nkb = 4 * G + 4
                    pv_last = [None] * 4
                    def issue_S(kb, c=c, G=G, q0=q0):
                        r = kb - 4 * G
                        qlo = 128 * r if r > 0 else 0
                        ncol = 512 - qlo
                        sk = SBANKS[s_rr[0] % len(SBANKS)]
                        s_rr[0] += 1
                        S_ps = P.banks[sk][:, 0:ncol]
                        pairs = []
                        if kind == "m":
                            pairs.append((kT[hb][:, kb * 128:(kb + 1) * 128], qT[hb][:, q0 + qlo:q0 + 512]))
                            pairs.append((krT[:, kb * 128:(kb + 1) * 128], qrT[hb][:, q0 + qlo:q0 + 512]))
                        else:
                            pairs.append((kT[hb][c * 64:(c + 1) * 64, kb * 128:(kb + 1) * 128],
                                          qT[hb][c * 64:(c + 1) * 64, q0 + qlo:q0 + 512]))
                            if r >= -1:
                                pairs.append((ident[:, :], bias_hl[:, 0, r + 1, qlo:512]))
                                pairs.append((ident[:, :], bias_hl[:, 1, r + 1, qlo:512]))
                        tS = mm_group(g, S_ps, pairs, deps=[ld, P.free[sk]])
                        if kind == "d" and r >= -1:
                            lbu[0] = tS
                        pi = pt_i[0] % NPT
                        pt_i[0] += 1
                        if kind == "m":
                            bias_arg = zero_c[:, 0:1]
                            sc = mla_scale
                        else:
                            bias_arg = cfar[:, hh:hh + 1] if r < -1 else zero_c[:, 0:1]
                            sc = 1.0
                        tE = g.op("act", lambda h, pi=pi, S_ps=S_ps, ncol=ncol, bias_arg=bias_arg, sc=sc: h.activation(
                            out=pt[pi][:, 0:ncol], in_=S_ps, func=AF.Exp, bias=bias_arg, scale=sc),
                            deps=[tS, ptfree[pi], tlam])
                        P.rel(sk, tE)
                        if r >= 0:
                            tE = g.op("pool", lambda h, pi=pi: h.tensor_tensor(pt[pi][:, 0:128], pt[pi][:, 0:128], tri[:, :], ALU.mult),
                                      deps=[tE])
                        return (r, qlo, pi, tE)

                    def issue_PV(kb, rec, G=G):
                        r, qlo, pi, tE = rec
                        qb0 = r if r > 0 else 0
                        tpv = None
                        for qb in range(qb0, 4):
                            lastkb = 4 * G + qb
                            dep = [tE]
                            if kb == 0:
                                dep.append(P.free[OBANKS[qb]])
                            tpv = g.op("pe", lambda h, qb=qb, pi=pi, kb=kb, qlo=qlo, lastkb=lastkb, hb=hb: h.matmul(
                                oacc(qb), pt[pi][:, qb * 128 - qlo:(qb + 1) * 128 - qlo], vv[hb][:, kb, 0:129],
                                start=(kb == 0), stop=(kb == lastkb)), deps=dep if qb == qb0 else ([] if kb > 0 else dep),
                                sig=(qb == 3 or kb == lastkb))
                            if kb == lastkb:
                                pv_last[qb] = tpv
                        ptfree[pi] = tpv
                        return tpv

                    LA = 2
                    recs = {}
                    for kb in range(min(LA, nkb)):
                        recs[kb] = issue_S(kb)
                    for kb in range(nkb):
                        if kb + LA < nkb:
                            recs[kb + LA] = issue_S(kb + LA)
                        last_pe = issue_PV(kb, recs.pop(kb))
                    if lbu[0] is not None:
                        last_bias_use = lbu[0]
                    slot = c
                    tr = None
                    tn = []
                    for qb in range(4):
                        bk = OBANKS[qb]
                        col = 0
                        ci = (G * 2 + c) % 4 * 4 + qb
                        tr = g.op("dve", lambda h, bk=bk, col=col, ci=ci: h.reciprocal(rs[:, ci:ci + 1], P.banks[bk][:, col + 128:col + 129]),
                                  deps=[pv_last[qb]])
                        t_ = g.op("dve", lambda h, bk=bk, col=col, ci=ci, qb=qb, slot=slot: h.tensor_scalar(
                            osb[slot][:, qb, :], P.banks[bk][:, col:col + 128], rs[:, ci:ci + 1], None, ALU.mult),
                            deps=[tr, osb_free[slot]])
                        tn.append(t_)
                        P.rel(bk, t_)
                    comp_done.append(tn)
                if kind == "m":
                    tfin = g.op("act", lambda h: h.activation(out=obf[:, :, :], in_=osb[0][:, :, :], func=AF.Copy),
                                deps=[comp_done[0], obf_free[0]])
                    osb_free[0] = tfin
                    fin = [tfin] * 4
                else:
                    fin = []
                    for qb in range(4):
                        ta = g.op("dve", lambda h, qb=qb: h.scalar_tensor_tensor(
                            osb[0][:, qb, :], osb[1][:, qb, :], lamt[:, 5:6], osb[0][:, qb, :], ALU.mult, ALU.add),
                            deps=[comp_done[0][qb], comp_done[1][qb], tlam])
                        ci = 12 + qb
                        tb0_ = g.op("dve", lambda h, qb=qb: h.tensor_tensor(osq[:, :], osb[0][:, qb, :], osb[0][:, qb, :], ALU.mult), deps=[ta])
                        tb_ = g.op("dve", lambda h, ci=ci: h.tensor_reduce(rs[:, ci + 4:ci + 5], osq[:, :], AX.X, ALU.add), deps=[tb0_])
                        tc_ = g.op("act", lambda h, ci=ci: h.activation(out=rs[:, ci + 4:ci + 5], in_=rs[:, ci + 4:ci + 5], func=AF.Ln, bias=1e-5, scale=1.0 / 128.0), deps=[tb_])
                        td_ = g.op("act", lambda h, ci=ci: h.activation(out=rs[:, ci + 4:ci + 5], in_=rs[:, ci + 4:ci + 5], func=AF.Exp, scale=-0.5), deps=[tc_])
                        te_ = g.op("dve", lambda h, qb=qb, ci=ci: h.scalar_tensor_tensor(
                            obf[:, qb, :], osb[0][:, qb, :], rs[:, ci + 4:ci + 5], dnorm[:, :], ALU.mult, ALU.mult),
                            deps=[td_, obf_free[0]])
                        fin.append(te_)
                    osb_free[0] = fin
                    osb_free[1] = fin
                mi = mo_i[0] % 2
                mo_i[0] += 1
                tp = None
                for qb in range(4):
                    tp = g.op("pe", lambda h, qb=qb: h.transpose(P.tb[:, qb * 128:(qb + 1) * 128], obf[:, qb, :], ident[:, :]),
                              deps=[fin[qb], P.tbfree] if qb == 0 else [fin[qb]], sig=(qb == 3))
                obf_free[0] = tp
                te = g.op("act", lambda h, mi=mi: h.activation(out=mo[mi][:, :], in_=P.tb[:, 0:512], func=AF.Copy), deps=[tp, mo_free[mi]])
                P.tbfree = te
                ts = g.dma("sp", "b_out%d" % mi, mT_d[hi, :, q0:q0 + 512], mo[mi][:, :], deps=[te])
                mo_free[mi] = ts
                outtoks.append(ts)
            kfree[hb] = last_pe
            qfree[hb] = last_pe
            vfree[hb] = last_pe
            if kind == "m":
                qrfree[hb] = last_pe
            else:
                bias_free[0] = last_bias_use
        g.barrier()
        g.flush("phB")
    return outtoks


def phase_B2(g, nc, P, S, NM, ND, i_, mT_d, cst, lam_init, indeps=()):
    NKB = S // 128
    NG = S // 512
    mla_scale = (128 + 64) ** -0.5
    outtoks = []
    with ExitStack() as es:
        sb = lambda n, s, d: es.enter_context(nc.sbuf_tensor(_uname(n), s, d))
        kT = [sb("b_kT%d" % i, [128, S], BF16) for i in range(2)]
        qT = [sb("b_qT%d" % i, [128, S], BF16) for i in range(2)]
        vv = [sb("b_v%d" % i, [128, NKB, 128], BF16) for i in range(2)]
        kz = [[sb("b_kz%d_%d" % (i, c), [128, S], BF16) for c in range(2)] for i in range(2)]
        krT = sb("b_krT", [128, S], BF16)
        qrT = [sb("b_qrT%d" % i, [128, S], BF16) for i in range(2)]
        pacc = [[sb("b_pacc%d_%d" % (i, e), [128, 512], F32) for e in range(2)] for i in range(2)]
        onesf = sb("b_onesf", [128, 128], F32)
        bias_f = sb("b_biasf", [128, 5, 512], F32)
        bias_t = sb("b_biast", [128, 512], F32)
        bias_hl = sb("b_biashl", [128, 2, 5, 512], BF16)
        cfar = sb("b_cfar", [128, max(ND, 1)], F32)
        lamp = sb("b_lamp", [128, 256], F32)
        lamt = sb("b_lamt", [128, 8], F32)
        dncol = sb("b_dncol", [128, 1], F32)
        NPT = 6
        pt = [sb("b_pt%d" % i, [128, 512], BF16) for i in range(NPT)]
        rinv2 = [sb("b_rinv%d" % i, [128, 512], F32) for i in range(2)]
        on = [sb("b_on%d" % i, [128, 512], F32) for i in range(2)]
        osq = sb("b_osq", [128, 512], BF16)
        rr = sb("b_rr", [128, 512], F32)
        mo = [sb("b_mo%d" % i, [128, 512], BF16) for i in range(2)]
        ones = cst["ones"]
        ident = cst["ident"]
        tri = cst["tri"]
        zero_c = cst["zero_c"]

        g.barrier()
        zinit = [g.op("pool", lambda h: h.memset(onesf[:, :], 1.0)),
                 g.op("pool", lambda h: h.memset(krT[64:128, :], 0.0))]
        for i in range(2):
            zinit.append(g.op("pool", lambda h, i=i: h.memset(qrT[i][64:128, :], 0.0)))
            zinit.append(g.op("pool", lambda h, i=i: h.memset(kz[i][0][64:128, :], 0.0)))
            zinit.append(g.op("pool", lambda h, i=i: h.memset(kz[i][1][0:64, :], 0.0)))
        tkr = g.dma("sp", "b_kr", krT[0:64, :], i_["krT"], deps=[indeps, zinit]) if NM > 0 else None
        tlam = None
        if ND > 0:
            t0 = g.dma("sp", "b_small", lamp[:, :], i_["lamp"])
            t0b = g.dma("sp", "b_small2", dncol[:, :], i_["dnorm"][0:1, :].rearrange("o d -> d o"))
            t0c = g.dma("sp", "b_small3", cfar[:, :], i_["cfar"])
            t1 = g.op("dve", lambda h: h.tensor_tensor(lamp[:, 0:64], lamp[:, 0:64], lamp[:, 64:128], ALU.mult), deps=[t0])
            t2 = g.op("dve", lambda h: h.tensor_tensor(lamp[:, 128:192], lamp[:, 128:192], lamp[:, 192:256], ALU.mult), deps=[t1])
            t3 = g.op("dve", lambda h: h.tensor_reduce(lamt[:, 0:1], lamp[:, 0:64], AX.X, ALU.add), deps=[t2])
            t4 = g.op("dve", lambda h: h.tensor_reduce(lamt[:, 1:2], lamp[:, 128:192], AX.X, ALU.add), deps=[t3])
            t5 = g.op("act", lambda h: h.activation(out=lamt[:, 2:4], in_=lamt[:, 0:2], func=AF.Exp), deps=[t4])
            t6 = g.op("dve", lambda h: h.tensor_tensor(lamt[:, 4:5], lamt[:, 2:3], lamt[:, 3:4], ALU.subtract), deps=[t5])
            t7 = g.op("dve", lambda h: h.tensor_scalar(lamt[:, 5:6], lamt[:, 4:5], -1.0, -float(lam_init), ALU.mult, ALU.add), deps=[t6])
            t8 = g.op("dve", lambda h: h.tensor_scalar(dncol[:, :], dncol[:, :], float(1.0 - lam_init), None, ALU.mult), deps=[t0b, t7])
            tlam = [t7, t8, t0c]

        kfree = [None, None]
        qfree = [None, None]
        vfree = [None, None]
        qrfree = [None, None]
        ptfree = [None] * NPT
        pt_i = [0]
        mo_free = [None, None]
        mo_i = [0]
        bias_free = [None]
        lbu = [None]
        SBANKS = [0, 1, 2]
        OTB, RSB = [3, 4], [5, 6]
        grp_i = [0]
        s_rr = [0]
        rinv_free = [None, None]
        on_free = [None, None]
        osq_free = [None]
        rr_free = [None]
        acc_i = [0]
        pacc_free = [None, None]
        lastA = [None, None]

        heads = [("m", h) for h in range(NM)] + [("d", h) for h in range(ND)]
        for hi, (kind, hh) in enumerate(heads):
            hb = hi % 2
            if kind == "m":
                tk = g.dma("sp", "b_k%d" % hb, kT[hb][:, :], i_["knT"][hh, :, :], deps=[kfree[hb], indeps])
                tq = g.dma("sp", "b_q%d" % hb, qT[hb][:, :], i_["qnT"][hh, :, :], deps=[qfree[hb]])
                tqr = g.dma("sp", "b_qr%d" % hb, qrT[hb][0:64, :], i_["qrT"][hh, :, :], deps=[qrfree[hb], zinit])
                tv = g.dma("sp", "b_v%d" % hb, vv[hb][:, :, :],
                           i_["vm"][:, hh * 128:(hh + 1) * 128].rearrange("(kb p) d -> p kb d", p=128), deps=[vfree[hb]])
                ld = [tk, tq, tqr, tv, tkr]
                comps = [0]
            else:
                tk0 = g.dma("sp", "b_k%d" % hb, kz[hb][0][0:64, :], i_["dkT"][hh, 0:64, :], deps=[kfree[hb], indeps, zinit])
                tk = g.dma("sp", "b_k%d" % hb, kz[hb][1][64:128, :], i_["dkT"][hh, 64:128, :])
                tq = g.dma("sp", "b_q%d" % hb, qT[hb][:, :], i_["dqT"][hh, :, :], deps=[qfree[hb]])
                tv = g.dma("sp", "b_v%d" % hb, vv[hb][:, :, :],
                           i_["dv"][:, hh * 128:(hh + 1) * 128].rearrange("(kb p) d -> p kb d", p=128), deps=[vfree[hb]])
                tb0 = g.dma("sp", "b_bias", bias_f[:, :, :], i_["bias"][hh].rearrange("r p q -> p r q"), deps=[bias_free[0]])
                tb1 = g.op("dve", lambda h: h.tensor_copy(out=bias_hl[:, 0, :, :], in_=bias_f[:, :, :]), deps=[tb0, bias_free[0]])
                tbl = tb1
                for r in range(5):
                    ta = g.op("dve", lambda h, r=r: h.tensor_tensor(bias_t[:, :], bias_f[:, r, :], bias_hl[:, 0, r, :], ALU.subtract), deps=[tbl])
                    tbl = g.op("dve", lambda h, r=r: h.tensor_copy(out=bias_hl[:, 1, r, :], in_=bias_t[:, :]), deps=[ta])
                ld = [tk0, tk, tq, tv, tbl, tlam]
                comps = [0, 1]
            last_pe = None
            lbu[0] = None
            for G in range(NG):
                q0 = G * 512
                nkb = 4 * G + 4
                on_tok = []
                for c in comps:
                    gi = grp_i[0] % 2
                    grp_i[0] += 1
                    OT, RS = OTB[gi], RSB[gi]
                    rinv = rinv2[gi]

                    def issue_S(kb, c=c, G=G, q0=q0):
                        r = kb - 4 * G
                        qlo = 128 * r if r > 0 else 0
                        ncol = 512 - qlo
                        sk = SBANKS[s_rr[0] % len(SBANKS)]
                        s_rr[0] += 1
                        S_ps = P.banks[sk][:, 0:ncol]
                        pairs = []
                        if kind == "m":
                            pairs.append((kT[hb][:, kb * 128:(kb + 1) * 128], qT[hb][:, q0 + qlo:q0 + 512]))
                            pairs.append((krT[:, kb * 128:(kb + 1) * 128], qrT[hb][:, q0 + qlo:q0 + 512]))
                        else:
                            pairs.append((kz[hb][c][:, kb * 128:(kb + 1) * 128], qT[hb][:, q0 + qlo:q0 + 512]))
                            if r >= -1:
                                pairs.append((ident[:, :], bias_hl[:, 0, r + 1, qlo:512]))
                                pairs.append((ident[:, :], bias_hl[:, 1, r + 1, qlo:512]))
                        tS = mm_group(g, S_ps, pairs, deps=[ld, P.free[sk]])
                        if kind == "d" and r >= -1:
                            lbu[0] = tS
                        pi = pt_i[0] % NPT
                        pt_i[0] += 1
                        if kind == "m":
                            bias_arg = zero_c[:, 0:1]
                            sc = mla_scale
                        else:
                            bias_arg = cfar[:, hh:hh + 1] if r < -1 else zero_c[:, 0:1]
                            sc = 1.0
                        tE = g.op("act", lambda h, pi=pi, S_ps=S_ps, ncol=ncol, bias_arg=bias_arg, sc=sc: h.activation(
                            out=pt[pi][:, 0:ncol], in_=S_ps, func=AF.Exp, bias=bias_arg, scale=sc),
                            deps=[tS, ptfree[pi], tlam])
                        P.rel(sk, tE)
                        if r >= 0:
                            tE = g.op("pool", lambda h, pi=pi: h.tensor_tensor(pt[pi][:, 0:128], pt[pi][:, 0:128], tri[:, :], ALU.mult),
                                      deps=[tE])
                        ai = acc_i[0] % 2
                        if kb == 0:
                            tA = g.op("dve", lambda h, pi=pi, ai=ai: h.tensor_copy(out=pacc[ai][0][:, :], in_=pt[pi][:, :]),
                                      deps=[tE, pacc_free[ai]])
                            tZ = g.op("pool", lambda h, ai=ai: h.memset(pacc[ai][1][:, :], 0.0), deps=[pacc_free[ai]])
                            lastA[0] = tA
                            lastA[1] = tZ
                        elif kb % 2 == 0:
                            tA = g.op("dve", lambda h, pi=pi, ai=ai, qlo=qlo, ncol=ncol: h.tensor_tensor(
                                pacc[ai][0][:, qlo:512], pacc[ai][0][:, qlo:512], pt[pi][:, 0:ncol], ALU.add), deps=[tE])
                            lastA[0] = tA
                        else:
                            tA = g.op("pool", lambda h, pi=pi, ai=ai, qlo=qlo, ncol=ncol: h.tensor_tensor(
                                pacc[ai][1][:, qlo:512], pacc[ai][1][:, qlo:512], pt[pi][:, 0:ncol], ALU.add), deps=[tE, lastA[1]])
                            lastA[1] = tA
                        return (qlo, ncol, pi, tE, tA)

                    def issue_PV(kb, rec, nkb=nkb, OT=OT):
                        qlo, ncol, pi, tE, tA = rec
                        first = (kb == 0)
                        last = (kb == nkb - 1)
                        t = g.op("pe", lambda h, pi=pi, kb=kb, qlo=qlo, ncol=ncol, hb=hb, first=first, last=last, OT=OT: h.matmul(
                            P.banks[OT][:, qlo:512], vv[hb][:, kb, :], pt[pi][:, 0:ncol], start=first, stop=last),
                            deps=[tE, P.free[OT]] if first else [tE], sig=True)
                        ptfree[pi] = [t, tA]
                        return t

                    LA = 2
                    recs = {}
                    for kb in range(min(LA, nkb)):
                        recs[kb] = issue_S(kb)
                    tpv = None
                    for kb in range(nkb):
                        if kb + LA < nkb:
                            recs[kb + LA] = issue_S(kb + LA)
                        tpv = issue_PV(kb, recs.pop(kb))
                    last_pe = tpv
                    ai = acc_i[0] % 2
                    acc_i[0] += 1
                    g.op("pe", lambda h, ai=ai, RS=RS: h.matmul(P.banks[RS][:, :], onesf[:, :], pacc[ai][0][:, :], start=True, stop=False),
                         deps=[lastA[0], lastA[1], P.free[RS]], sig=False)
                    trs = g.op("pe", lambda h, ai=ai, RS=RS: h.matmul(P.banks[RS][:, :], onesf[:, :], pacc[ai][1][:, :], start=False, stop=True))
                    pacc_free[ai] = trs
                    last_pe = trs
                    tr = g.op("dve", lambda h, RS=RS, rinv=rinv: h.reciprocal(rinv[:, :], P.banks[RS][:, :]), deps=[trs, rinv_free[gi]])
                    P.rel(RS, tr)
                    if kind == "m":
                        mi = mo_i[0] % 2
                        mo_i[0] += 1
                        tn = g.op("dve", lambda h, mi=mi, OT=OT, rinv=rinv: h.tensor_tensor(mo[mi][:, :], P.banks[OT][:, :], rinv[:, :], ALU.mult),
                                  deps=[tr, mo_free[mi]])
                        P.rel(OT, tn)
                        rinv_free[gi] = tn
                        ts = g.dma("sp", "b_out%d" % mi, mT_d[hi, :, q0:q0 + 512], mo[mi][:, :], deps=[tn])
                        mo_free[mi] = ts
                        outtoks.append(ts)
                    else:
                        tn = g.op("dve", lambda h, c=c, OT=OT, rinv=rinv: h.tensor_tensor(on[c][:, :], P.banks[OT][:, :], rinv[:, :], ALU.mult),
                                  deps=[tr, on_free[c]])
                        P.rel(OT, tn)
                        rinv_free[gi] = tn
                        on_tok.append(tn)
                if kind == "d":
                    ta = g.op("dve", lambda h: h.scalar_tensor_tensor(on[0][:, :], on[1][:, :], lamt[:, 5:6], on[0][:, :], ALU.mult, ALU.add),
                              deps=[on_tok, tlam])
                    on_free[1] = ta
                    tsq = g.op("pool", lambda h: h.tensor_tensor(osq[:, :], on[0][:, :], on[0][:, :], ALU.mult), deps=[ta, osq_free[0]])
                    XB = RS
                    tmm = g.op("pe", lambda h, XB=XB: h.matmul(P.banks[XB][:, :], ones[:, :], osq[:, :], start=True, stop=True),
                               deps=[tsq, P.free[XB]])
                    osq_free[0] = tmm
                    t1_ = g.op("act", lambda h, XB=XB: h.activation(out=rr[:, :], in_=P.banks[XB][:, :], func=AF.Ln, bias=1e-5, scale=1.0 / 128.0),
                               deps=[tmm, rr_free[0]])
                    P.rel(XB, t1_)
                    t2_ = g.op("act", lambda h: h.activation(out=rr[:, :], in_=rr[:, :], func=AF.Exp, scale=-0.5), deps=[t1_])
                    mi = mo_i[0] % 2
                    mo_i[0] += 1
                    tf = g.op("dve", lambda h, mi=mi: h.scalar_tensor_tensor(mo[mi][:, :], on[0][:, :], dncol[:, 0:1], rr[:, :], ALU.mult, ALU.mult),
                              deps=[t2_, ta, mo_free[mi], tlam])
                    rr_free[0] = tf
                    on_free[0] = tf
                    ts = g.dma("sp", "b_out%d" % mi, mT_d[hi, :, q0:q0 + 512], mo[mi][:, :], deps=[tf])
                    mo_free[mi] = ts
                    outtoks.append(ts)
            kfree[hb] = last_pe
            qfree[hb] = last_pe
            vfree[hb] = last_pe
            if kind == "m":
                qrfree[hb] = last_pe
            else:
                bias_free[0] = lbu[0]
        g.barrier()
        g.flush("phB")
    return outtoks


def ln_block(g, st, A, lnp, gi, bi, dep, tln):
    t1 = g.op("dve", lambda h: h.bn_stats(out=st[:, 0:6], in_=A(0, 512)), deps=[dep])
    t2 = g.op("dve", lambda h: h.bn_stats(out=st[:, 6:12], in_=A(512, 1024)), deps=[dep])
    t3 = g.op("dve", lambda h: h.bn_aggr(out=st[:, 12:14], in_=st[:, 0:12]), deps=[t1, t2])
    t4a = g.op("act", lambda h: h.activation(out=st[:, 14:15], in_=st[:, 13:14], func=AF.Ln, bias=1e-5, scale=1.0), deps=[t3])
    t4 = g.op("act", lambda h: h.activation(out=st[:, 14:15], in_=st[:, 14:15], func=AF.Exp, scale=-0.5), deps=[t4a])
    t5 = g.op("dve", lambda h: h.tensor_scalar(A(0, 1024), A(0, 1024), st[:, 12:13], st[:, 14:15], ALU.subtract, ALU.mult), deps=[t4])
    t6 = g.op("pool", lambda h: h.tensor_tensor(A(0, 1024), A(0, 1024), lnp[:, gi, :], ALU.mult), deps=[t5, tln])
    t7 = g.op("pool", lambda h: h.tensor_tensor(A(0, 1024), A(0, 1024), lnp[:, bi, :], ALU.add), deps=[t6])
    return t7


def gates_block(g, lg, dep):
    L = lg[:, 0:8]
    m1, m2 = lg[:, 8:9], lg[:, 9:10]
    k1, k2 = lg[:, 16:24], lg[:, 24:32]
    L2 = lg[:, 32:40]
    e, g1, g2 = lg[:, 10:11], lg[:, 11:12], lg[:, 12:13]
    t = g.op("dve", lambda h: h.tensor_reduce(m1, L, AX.X, ALU.max), deps=[dep])
    t = g.op("dve", lambda h: h.tensor_scalar(k1, L, m1, None, ALU.is_equal), deps=[t])
    t = g.op("dve", lambda h: h.scalar_tensor_tensor(L2, k1, -1e30, L, ALU.mult, ALU.add), deps=[t])
    t = g.op("dve", lambda h: h.tensor_reduce(m2, L2, AX.X, ALU.max), deps=[t])
    t = g.op("dve", lambda h: h.tensor_scalar(k2, L2, m2, None, ALU.is_equal), deps=[t])
    t = g.op("dve", lambda h: h.tensor_tensor(e, m2, m1, ALU.subtract), deps=[t])
    t = g.op("act", lambda h: h.activation(out=e, in_=e, func=AF.Exp), deps=[t])
    t = g.op("dve", lambda h: h.tensor_scalar(g1, e, 1.0, None, ALU.add), deps=[t])
    t = g.op("dve", lambda h: h.reciprocal(g1, g1), deps=[t])
    t = g.op("dve", lambda h: h.tensor_tensor(g2, e, g1, ALU.mult), deps=[t])
    t = g.op("dve", lambda h: h.tensor_scalar(k1, k1, g1, None, ALU.mult), deps=[t])
    t = g.op("dve", lambda h: h.scalar_tensor_tensor(lg[:, 56:64], k2, g2, k1, ALU.mult, ALU.add), deps=[t])
    return t


def phase_C(g, nc, P, TOK, moe, x_d, mT_d, w, xo_d, cst, indeps=()):
    CT = min(TOK, 2048)
    NCH = TOK // CT
    NB = CT // 128
    NTT = CT // 512
    F = FE if moe else FD
    nfc_all = F // 128
    FT = 4
    outtoks = []
    ident = cst["ident"]
    with ExitStack() as es:
        sb = lambda n, s, d: es.enter_context(nc.sbuf_tensor(_uname(n), s, d))
        yacc = sb("c_yacc", [128, NB, 1024], F32)
        x1T = sb("c_x1T", [128, 8, CT], BF16)
        lnp = sb("c_lnp", [128, 2, 1024], F32)
        st = sb("c_st", [128, 16], F32)
        sg = [sb("c_sg%d" % i, [128, 512], F32) for i in range(2)]
        tt_ = [sb("c_tt%d" % i, [128, 512], F32) for i in range(2)]
        if moe:
            gT = sb("c_gT", [8, CT], F32)
            gbc = sb("c_gbc", [128, CT], F32)
            sel = cst["sel"]
            identf = cst["identf"]
        g.barrier()
        for ch in range(NCH):
            c0 = ch * CT
            with ExitStack() as es1:
                sb1 = lambda n, s, d: es1.enter_context(nc.sbuf_tensor(_uname(n), s, d))
                wo = sb1("c_wo", [128, 8, 1024], BF16)
                xin = [sb1("c_xin%d" % i, [128, 1024], F32) for i in range(2)]
                xbf = [sb1("c_xbf%d" % i, [128, 1024], BF16) for i in range(2)]
                mTs = [sb1("c_mT%d" % i, [128, 8, 128], BF16) for i in range(2)]
                if moe:
                    rw = sb1("c_rw", [128, 8, 8], F32)
                    x1Tf = [sb1("c_x1Tf%d" % i, [128, 8, 128], F32) for i in range(2)]
                    lg = sb1("c_lg", [128, 64], F32)
                g.barrier()
                two = [g.dma("pool", "c_wo", wo[:, fc, :], w["wo"][fc * 128:(fc + 1) * 128, :]) for fc in range(8)]
                tln = g.dma("sp", "c_ln", lnp[:, :, :], w["ln"][:, 0:2, :])
                trw = g.dma("sp", "c_rw", rw[:, :, :], w["router"]) if moe else None
                xin_free = [None, None]
                xbf_free = [None, None]
                mT_free = [None, None]
                x1Tf_free = [None, None]
                gate_ready = None
                for blk in range(NB):
                    b = blk % 2
                    r0 = c0 + blk * 128
                    tlx = g.dma("sp", "c_xin%d" % b, xin[b][:, :], x_d[r0:r0 + 128, :], deps=[xin_free[b], indeps])
                    tlm = g.dma("sp", "c_mT%d" % b, mTs[b][:, :, :], mT_d[:, :, r0:r0 + 128].rearrange("c p t -> p c t"),
                                deps=[mT_free[b], indeps])
                    ks = []
                    for half in range(2):
                        k = P.get()
                        tok = mm_group(g, P.banks[k][:, :], [(mTs[b][:, hc, :], wo[:, hc, half * 512:(half + 1) * 512]) for hc in range(8)],
                                       deps=[tlm, two, P.free[k]])
                        ks.append((k, tok))
                    mT_free[b] = ks[1][1]
                    ty = None
                    for half in range(2):
                        k, tok = ks[half]
                        ty = g.op("dve", lambda h, b=b, k=k, half=half: h.scalar_tensor_tensor(
                            xin[b][:, half * 512:(half + 1) * 512], xin[b][:, half * 512:(half + 1) * 512], float(DN_ALPHA),
                            P.banks[k][:, :], ALU.mult, ALU.add), deps=[tok, tlx, ty])
                        P.rel(k, ty)
                    tl = ln_block(g, st, (lambda a, c, b=b: xin[b][:, a:c]), lnp, 0, 1, ty, tln)
                    ta = g.op("pool", lambda h, b=b, blk=blk: h.tensor_scalar(yacc[:, blk, :], xin[b][:, :], float(DN_ALPHA), None, ALU.mult),
                              deps=[tl])
                    tb_ = g.op("act", lambda h, b=b: h.activation(out=xbf[b][:, :], in_=xin[b][:, :], func=AF.Copy), deps=[tl, xbf_free[b]])
                    tp = None
                    for fc in range(8):
                        tp = g.op("pe", lambda h, b=b, fc=fc: h.transpose(P.tb[:, fc * 128:(fc + 1) * 128], xbf[b][:, fc * 128:(fc + 1) * 128], ident[:, :]),
                                  deps=[tb_, P.tbfree] if fc == 0 else (), sig=(fc == 7))
                    xbf_free[b] = tp
                    te = g.op("dve", lambda h, blk=blk: h.tensor_copy(out=x1T[:, :, blk * 128:(blk + 1) * 128],
                                                                      in_=P.tb[:, :].rearrange("p (c n) -> p c n", c=8)), deps=[tp])
                    P.tbfree = te
                    xfree = [ta, tb_]
                    if moe:
                        fb = blk % 2
                        parts = []
                        for half in range(2):
                            kk = P.get()
                            tpf = None
                            for j in range(4):
                                fc = half * 4 + j
                                tpf = g.op("pe", lambda h, b=b, fc=fc, j=j, kk=kk: h.transpose(
                                    P.banks[kk][:, j * 128:(j + 1) * 128], xin[b][:, fc * 128:(fc + 1) * 128], identf[:, :]),
                                    deps=[tl, P.free[kk]] if j == 0 else (), sig=(j == 3))
                            tcp = g.op("act", lambda h, fb=fb, half=half, kk=kk: h.activation(
                                out=x1Tf[fb][:, half * 4:(half + 1) * 4, :], in_=P.banks[kk][:, :].rearrange("p (c n) -> p c n", c=4), func=AF.Copy),
                                deps=[tpf, x1Tf_free[fb]])
                            P.rel(kk, tcp)
                            parts.append(tcp)
                            xfree.append(tpf)
                        kk = P.get()
                        tlg = mm_group(g, P.banks[kk][:, 0:8], [(x1Tf[fb][:, fc, :], rw[:, fc, :]) for fc in range(8)],
                                       deps=[parts, trw, P.free[kk]])
                        x1Tf_free[fb] = tlg
                        tlc = g.op("dve", lambda h, kk=kk: h.tensor_copy(out=lg[:, 0:8], in_=P.banks[kk][:, 0:8]), deps=[tlg, gate_ready])
                        P.rel(kk, tlc)
                        tgt = gates_block(g, lg, tlc)
                        kk = P.get()
                        ttp = g.op("pe", lambda h, kk=kk: h.transpose(P.banks[kk][0:8, 0:128], lg[:, 56:64], identf[:, :]), deps=[tgt, P.free[kk]])
                        tgc = g.op("dve", lambda h, kk=kk, blk=blk: h.tensor_copy(out=gT[:, blk * 128:(blk + 1) * 128], in_=P.banks[kk][0:8, 0:128]),
                                   deps=[ttp])
                        P.rel(kk, tgc)
                        gate_ready = tgc
                    xin_free[b] = xfree
                g.barrier()
                g.flush("phC1")
            with ExitStack() as es2:
                sb2 = lambda n, s, d: es2.enter_context(nc.sbuf_tensor(_uname(n), s, d))
                hT = sb2("c_hT", [128, FT, CT], BF16)
                wgu = [sb2("c_wgu%d" % i, [128, 2, 8, FT * 128], BF16) for i in range(2)]
                wd = [sb2("c_wd%d" % i, [128, FT, 1024], BF16) for i in range(2)]
                g.barrier()
                wfree_gu = [None, None]
                wfree_d = [None, None]
                hfree = None
                sgfree = [None, None]
                ttfree = [None, None]
                wt_i = 0
                yl = [None] * NB
                experts = list(range(NE)) if moe else [None]
                gb_free = None
                for e in experts:
                    tgb = None
                    if moe:
                        tgb = []
                        for tt in range(NTT):
                            kk = P.get()
                            tmm = g.op("pe", lambda h, kk=kk, e=e, tt=tt: h.matmul(P.banks[kk][:, :], sel[:, e, :], gT[:, tt * 512:(tt + 1) * 512],
                                                                                  start=True, stop=True), deps=[P.free[kk]])
                            tcp = g.op("act", lambda h, kk=kk, tt=tt: h.activation(out=gbc[:, tt * 512:(tt + 1) * 512], in_=P.banks[kk][:, :], func=AF.Copy),
                                       deps=[tmm, gb_free])
                            P.rel(kk, tcp)
                            tgb.append(tcp)
                    gb_last = []
                    nft = (nfc_all + FT - 1) // FT
                    for ft in range(nft):
                        nf = min(FT, nfc_all - ft * FT)
                        wb = wt_i % 2
                        wt_i += 1
                        f0 = ft * FT * 128
                        src_g = w["wg"][e] if moe else w["wg"]
                        src_u = w["wu"][e] if moe else w["wu"]
                        src_d = w["wd"][e] if moe else w["wd"]
                        twg = g.dma("pool", "c_wg%d" % wb, wgu[wb][:, 0, :, 0:nf * 128],
                                    src_g[:, f0:f0 + nf * 128].rearrange("(fc p) f -> p fc f", p=128), deps=[wfree_gu[wb]])
                        twu = g.dma("pool", "c_wu%d" % wb, wgu[wb][:, 1, :, 0:nf * 128],
                                    src_u[:, f0:f0 + nf * 128].rearrange("(fc p) f -> p fc f", p=128))
                        twd = g.dma("pool", "c_wd%d" % wb, wd[wb][:, 0:nf, :],
                                    src_d[f0:f0 + nf * 128, :].rearrange("(j p) d -> p j d", p=128), deps=[wfree_d[wb]])
                        h_toks = []
                        last_gu = None
                        for j in range(nf):
                            for tt in range(NTT):
                                kg = P.get()
                                tg_ = mm_group(g, P.banks[kg][:, :], [(wgu[wb][:, 0, fc, j * 128:(j + 1) * 128], x1T[:, fc, tt * 512:(tt + 1) * 512]) for fc in range(8)],
                                               deps=[twg, P.free[kg]])
                                ku = P.get()
                                tu_ = mm_group(g, P.banks[ku][:, :], [(wgu[wb][:, 1, fc, j * 128:(j + 1) * 128], x1T[:, fc, tt * 512:(tt + 1) * 512]) for fc in range(8)],
                                               deps=[twu, P.free[ku]])
                                last_gu = tu_
                                si = (j * NTT + tt) % 2
                                ts_ = g.op("act", lambda h, kg=kg, si=si: h.activation(out=sg[si][:, :], in_=P.banks[kg][:, :], func=AF.Silu),
                                           deps=[tg_, sgfree[si]])
                                P.rel(kg, ts_)
                                if moe:
                                    tm_ = g.op("dve", lambda h, ku=ku, si=si: h.tensor_tensor(tt_[si][:, :], sg[si][:, :], P.banks[ku][:, :], ALU.mult),
                                               deps=[ts_, tu_, ttfree[si]])
                                    P.rel(ku, tm_)
                                    sgfree[si] = tm_
                                    th_ = g.op("pool", lambda h, si=si, j=j, tt=tt: h.tensor_tensor(
                                        hT[:, j, tt * 512:(tt + 1) * 512], tt_[si][:, :], gbc[:, tt * 512:(tt + 1) * 512], ALU.mult),
                                        deps=[tm_, tgb, hfree])
                                    ttfree[si] = th_
                                    gb_last.append(th_)
                                else:
                                    th_ = g.op("dve", lambda h, ku=ku, si=si, j=j, tt=tt: h.tensor_tensor(
                                        hT[:, j, tt * 512:(tt + 1) * 512], sg[si][:, :], P.banks[ku][:, :], ALU.mult),
                                        deps=[ts_, tu_, hfree])
                                    P.rel(ku, th_)
                                    sgfree[si] = th_
                                h_toks.append(th_)
                        wfree_gu[wb] = last_gu
                        last_d = None
                        for blk in range(NB):
                            for half in range(2):
                                kk = P.get()
                                td_ = mm_group(g, P.banks[kk][:, :], [(hT[:, j, blk * 128:(blk + 1) * 128], wd[wb][:, j, half * 512:(half + 1) * 512]) for j in range(nf)],
                                               deps=[h_toks, twd, P.free[kk]])
                                last_d = td_
                                tacc = g.op("dve", lambda h, kk=kk, blk=blk, half=half: h.tensor_tensor(
                                    yacc[:, blk, half * 512:(half + 1) * 512], yacc[:, blk, half * 512:(half + 1) * 512], P.banks[kk][:, :], ALU.add),
                                    deps=[td_, yl[blk]])
                                P.rel(kk, tacc)
                                yl[blk] = tacc
                        wfree_d[wb] = last_d
                        hfree = last_d
                    if moe:
                        gb_free = gb_last
                g.barrier()
                g.flush("phC2")
            tln2 = g.dma("sp", "c_ln", lnp[:, :, :], w["ln"][:, 2:4, :])
            for blk in range(NB):
                tl = ln_block(g, st, (lambda a, c, blk=blk: yacc[:, blk, a:c]), lnp, 0, 1, None, tln2)
                ts = g.dma("sp", "c_out%d" % (blk % 4), xo_d[c0 + blk * 128:c0 + (blk + 1) * 128, :], yacc[:, blk, :], deps=[tl])
                outtoks.append(ts)
            g.barrier()
            g.flush("phC3")
    return outtoks


CONST_SPECS = [("c_ident", [128, 128], BF16), ("c_identf", [128, 128], F32), ("c_ones", [128, 128], BF16),
               ("c_tri", [128, 128], BF16), ("c_zero", [128, 1], F32), ("c_sel", [8, NE, 128], F32)]


def host_consts():
    ident = np.eye(128, dtype=np.float32)
    tri = (np.arange(128)[None, :] >= np.arange(128)[:, None]).astype(np.float32)
    sel = np.zeros((8, NE, 128), np.float32)
    for e in range(NE):
        sel[e, e, :] = 1.0
    return {"c_ident": ident.astype(NPBF), "c_identf": ident, "c_ones": np.ones((128, 128), NPBF),
            "c_tri": tri.astype(NPBF), "c_zero": np.zeros((128, 1), np.float32), "c_sel": sel}


def load_consts(g, nc, es):
    cst = {}
    for (name, shape, dt) in CONST_SPECS:
        d = nc.dram_tensor(name, shape, dt, kind="ExternalInput").ap()
        t = es.enter_context(nc.sbuf_tensor("s" + name, shape, dt))
        if len(shape) == 2:
            g.dma("sp", "cst", t[:, :], d)
        else:
            g.dma("sp", "cst", t[:, :, :], d)
        cst[name] = t
    return {"ident": cst["c_ident"], "identf": cst["c_identf"], "ones": cst["c_ones"], "tri": cst["c_tri"],
            "zero_c": cst["c_zero"], "sel": cst["c_sel"]}


A_IN = lambda TOK: [("win", [1024, INW], F32), ("wuq", [384, 1024], F32), ("wuk", [256, 512], F32), ("wuv", [256, 512], F32),
                    ("gq", [128, 3], F32), ("gkv", [128, 2], F32)]
A_OUT = lambda TOK: [("qnT", [4, 128, TOK], BF16), ("qrT", [4, 64, TOK], BF16), ("knT", [4, 128, TOK], BF16), ("krT", [64, TOK], BF16),
                     ("vm", [TOK, 512], BF16), ("dqT", [4, 128, TOK], BF16), ("dkT", [4, 128, TOK], BF16), ("dv", [TOK, 512], BF16)]


def B_IN(S, NM, ND):
    return [("qnT", [NM, 128, S], BF16), ("qrT", [NM, 64, S], BF16), ("knT", [NM, 128, S], BF16), ("krT", [64, S], BF16),
            ("vm", [S, NM * 128], BF16), ("dqT", [ND, 128, S], BF16), ("dkT", [ND, 128, S], BF16), ("dv", [S, ND * 128], BF16)]


B_W = lambda ND: [("bias", [ND, 5, 128, 512], F32), ("cfar", [128, ND], F32), ("lamp", [128, 256], F32), ("dnorm", [128, 128], F32)]


def C_W(moe):
    base = [("wo", [1024, 1024], F32), ("ln", [128, 4, 1024], F32)]
    if moe:
        return base + [("router", [128, 8, 8], F32), ("wg", [NE, 1024, FE], F32), ("wu", [NE, 1024, FE], F32), ("wd", [NE, FE, 1024], F32)]
    return base + [("wg", [1024, FD], F32), ("wu", [1024, FD], F32), ("wd", [FD, 1024], F32)]


PHASE_B = phase_B2


def lam_init_of(l):
    return 0.8 - 0.6 * math.exp(-0.3 * l)


def build(mode, S=4096, TOK=2048, NM=2, ND=2, moe=False, layers=(0,), lam_l=0):
    nc = bass.Bass("TRN2", target_bir_lowering=False)
    ext_in = lambda n, s, d: nc.dram_tensor(n, s, d, kind="ExternalInput").ap()
    ext_out = lambda n, s, d: nc.dram_tensor(n, s, d, kind="ExternalOutput").ap()
    internal = lambda n, s, d: nc.dram_tensor(n, s, d).ap()
    with ExitStack() as es:
        g = Gen(nc, es)
        P = Psum(g, nc, es)
        cst = load_consts(g, nc, es)
        if mode == "A":
            x = ext_in("x", [TOK, 1024], F32)
            w = {n: ext_in(n, s, d) for (n, s, d) in A_IN(TOK)}
            w["cos"] = ext_in("cos", [128, TOK], F32)
            w["sin"] = ext_in("sin", [128, TOK], F32)
            o = {n: ext_out("o_" + n, s, d) for (n, s, d) in A_OUT(TOK)}
            phase_A(g, nc, P, TOK, x, w, o, cst)
        elif mode == "B":
            i_ = {n: ext_in(n, s, d) for (n, s, d) in B_IN(S, NM, ND) + B_W(ND)}
            mT = ext_out("o_mT", [NM + ND, 128, S], BF16)
            PHASE_B(g, nc, P, S, NM, ND, i_, mT, cst, lam_init_of(lam_l))
        elif mode == "C":
            x = ext_in("x", [TOK, 1024], F32)
            mT = ext_in("mT", [8, 128, TOK], BF16)
            w = {n: ext_in(n, s, d) for (n, s, d) in C_W(moe)}
            xo = ext_out("o_x", [TOK, 1024], F32)
            phase_C(g, nc, P, TOK, moe, x, mT, w, xo, cst)
        elif mode == "F":
            x0 = ext_in("x", [S, 1024], F32)
            cos = ext_in("cos", [128, S], F32)
            sin = ext_in("sin", [128, S], F32)
            bias = ext_in("bias", [4, 5, 128, 512], F32)
            cfar = ext_in("cfar", [128, 4], F32)
            xout = ext_out("o_x", [S, 1024], F32)
            xs = [internal("xs0", [S, 1024], F32), internal("xs1", [S, 1024], F32)]
            sc = {n: internal("sc_" + n, s, d) for (n, s, d) in A_OUT(S)}
            mT = internal("sc_mT", [8, 128, S], BF16)
            xin = x0
            for li, l in enumerate(layers):
                moe_l = (l % 2 == 1)
                w = {n: ext_in("L%d_%s" % (l, n), s, d) for (n, s, d) in A_IN(S)}
                w["cos"], w["sin"] = cos, sin
                phase_A(g, nc, P, S, xin, w, sc, cst)
                i_ = dict(sc)
                i_["bias"], i_["cfar"] = bias, cfar
                i_["lamp"] = ext_in("L%d_lamp" % l, [128, 256], F32)
                i_["dnorm"] = ext_in("L%d_dnorm" % l, [128, 128], F32)
                PHASE_B(g, nc, P, S, 4, 4, i_, mT, cst, lam_init_of(l))
                wc = {n: ext_in("L%d_%s" % (l, n), s, d) for (n, s, d) in C_W(moe_l)}
                xo = xout if li == len(layers) - 1 else xs[li % 2]
                phase_C(g, nc, P, S, moe_l, xin, mT, wc, xo, cst)
                xin = xo
        g.barrier()
        g.flush()
    return nc


WIN_PERM = np.concatenate([np.arange(0, 704), np.arange(672, 704), np.arange(640, 672), np.arange(704, 2240)])
_uq = []
for _h in range(4):
    _uq.append(np.arange(_h * 192, _h * 192 + 128))
for _h in range(4):
    _uq.append(np.arange(_h * 192 + 128, _h * 192 + 192))
for _h in range(4):
    _uq.append(np.concatenate([np.arange(_h * 192 + 160, _h * 192 + 192), np.arange(_h * 192 + 128, _h * 192 + 160)]))
WUQ_PERM = np.concatenate(_uq)


def rope_tables(pos):
    inv = (1.0 / (np.float32(10000.0) ** (np.arange(0, 64, 2, dtype=np.float32) / np.float32(64)))).astype(np.float32)
    ang = pos.astype(np.float32)[None, :] * inv[:, None]
    c = np.cos(ang).astype(np.float32)
    s = np.sin(ang).astype(np.float32)
    cos_t = np.concatenate([c, c, c, c], 0)
    sin_t = np.concatenate([-s, s, -s, s], 0)
    return np.ascontiguousarray(cos_t), np.ascontiguousarray(sin_t)


def bucket_index(S):
    n = np.arange(S, dtype=np.int32)
    nf = np.maximum(n, 1).astype(np.float32)
    large = 16 + (np.log(nf / np.float32(16)) / np.float32(math.log(128 / 16)) * np.float32(16)).astype(np.int32)
    large = np.minimum(large, 31)
    return np.where(n < 16, n, large)


def bias_tiles(rel_bias, S):
    bidx = bucket_index(max(S, 1024))
    k = np.arange(128)[:, None]
    q = np.arange(512)[None, :]
    out = np.empty((4, 5, 128, 512), np.float32)
    for r in range(-1, 4):
        dist = np.clip(q - 128 * r - k, 0, len(bidx) - 1)
        out[:, r + 1] = np.transpose(rel_bias[bidx[dist]], (2, 0, 1))
    return out


def layer_A_weights(inp, l):
    return {"win": np.ascontiguousarray(inp["w_in"][l][:, WIN_PERM]),
            "wuq": np.ascontiguousarray(inp["mla_w_uq"][l][:, WUQ_PERM]),
            "wuk": np.ascontiguousarray(inp["mla_w_uk"][l]), "wuv": np.ascontiguousarray(inp["mla_w_uv"][l]),
            "gq": np.ascontiguousarray(inp["mla_q_norm"][l].reshape(3, 128).T),
            "gkv": np.ascontiguousarray(inp["mla_kv_norm"][l].reshape(2, 128).T)}


def layer_B_weights(inp, l):
    return {"lamp": np.ascontiguousarray(np.broadcast_to(inp["diff_lambda"][l].reshape(1, 256), (128, 256))),
            "dnorm": np.ascontiguousarray(np.broadcast_to(inp["diff_norm"][l].reshape(1, 128), (128, 128)))}


def layer_C_weights(inp, l):
    ln = np.stack([inp["ln1_g"][l], inp["ln1_b"][l], inp["ln2_g"][l], inp["ln2_b"][l]], 0)
    w = {"wo": np.ascontiguousarray(inp["w_o"][l]), "ln": np.ascontiguousarray(np.broadcast_to(ln[None], (128, 4, 1024)))}
    if l % 2 == 1:
        m = l // 2
        w["router"] = np.ascontiguousarray(inp["moe_router"][m].reshape(8, 128, 8).transpose(1, 0, 2))
        w["wg"], w["wu"], w["wd"] = inp["moe_w_gate"][m], inp["moe_w_up"][m], inp["moe_w_down"][m]
    else:
        m = l // 2
        w["wg"], w["wu"], w["wd"] = inp["ffn_w_gate"][m], inp["ffn_w_up"][m], inp["ffn_w_down"][m]
    return w


MODE = "fused4"
S_FULL = 4096
_PROG_CACHE = {}


def _prog(layers):
    key = tuple(layers)
    if key not in _PROG_CACHE:
        _PROG_CACHE[key] = build("F", S=S_FULL, layers=key)
    return _PROG_CACHE[key]


def _layer_inputs(inp, l):
    d = {}
    for part in (layer_A_weights(inp, l), layer_B_weights(inp, l), layer_C_weights(inp, l)):
        for k, v in part.items():
            d["L%d_%s" % (l, k)] = np.ascontiguousarray(v, dtype=np.float32)
    return d


def kernel(**inputs):
    inp = {k: np.asarray(v) for k, v in inputs.items()}
    x = np.ascontiguousarray(inp["x"], dtype=np.float32)
    B = x.shape[0]
    shared = dict(host_consts())
    shared["cos"], shared["sin"] = rope_tables(np.arange(S_FULL))
    shared["bias"] = bias_tiles(inp["rel_bias"].astype(np.float32), S_FULL)
    shared["cfar"] = np.ascontiguousarray(np.broadcast_to(inp["rel_bias"][31][None, :], (128, 4)), dtype=np.float32)
    groups = [tuple(range(DEPTH))] if MODE == "fused4" else [(l,) for l in range(DEPTH)]
    cur = [x[b] for b in range(B)]
    for layers in groups:
        nc = _prog(layers)
        base = dict(shared)
        for l in layers:
            base.update(_layer_inputs(inp, l))
        in_maps = []
        for b in range(B):
            m = dict(base)
            m["x"] = np.ascontiguousarray(cur[b])
            in_maps.append(m)
        res = run_bass_kernel_spmd(nc, in_maps, core_ids=list(range(B)))
        cur = [np.asarray(res.results[b]["o_x"], dtype=np.float32) for b in range(B)]
    return np.stack(cur, 0).astype(np.float32)
```

```python
import math
from contextlib import ExitStack

import numpy as np
import ml_dtypes

import concourse.bass as bass
import concourse.mybir as mybir
from concourse.bass_utils import run_bass_kernel_spmd

F32, BF16 = mybir.dt.float32, mybir.dt.bfloat16
AF = mybir.ActivationFunctionType
ALU = mybir.AluOpType
AX = mybir.AxisListType
NPBF = ml_dtypes.bfloat16

D = 1024
DEPTH = 4
NH = 4
INW = 2304
FD = 2816
FE = 3584
NE = 8
DN_ALPHA = (2 * DEPTH) ** 0.25
NEG = -30000.0
ENG = ["pe", "act", "dve", "pool", "sp"]


_UID = [0]


def _uname(n):
    _UID[0] += 1
    return "%s_u%d" % (n, _UID[0])


class Gen:
    def __init__(self, nc, es):
        self.nc = nc
        self.es = es
        self.ops = {e: [] for e in ENG}
        self.cnt = {e: 0 for e in ENG}
        self.seen = {e: {} for e in ENG}
        self.sem = {e: es.enter_context(nc.semaphore("s_" + e)) for e in ENG}
        self.dsem = {}
        self.nops = 0

    def _dsem(self, name):
        if name not in self.dsem:
            self.dsem[name] = [self.es.enter_context(self.nc.semaphore("d_" + name)), 0]
        return self.dsem[name]

    def _wait(self, eng, tok):
        if tok is None:
            return
        if isinstance(tok, (list, tuple)) and (len(tok) == 0 or not isinstance(tok[0], str)):
            for t in tok:
                self._wait(eng, t)
            return
        kind, key, v = tok
        if kind == "e":
            sem = self.sem[key]
            k = "e:" + key
        else:
            sem = self.dsem[key][0]
            k = "d:" + key
        if self.seen[eng].get(k, 0) >= v:
            return
        self.seen[eng][k] = v
        self.ops[eng].append(lambda h, sem=sem, v=v: h.wait_ge(sem, v))

    def _flat(self, tok, out):
        if tok is None:
            return
        if isinstance(tok, (list, tuple)) and (len(tok) == 0 or not isinstance(tok[0], str)):
            for t in tok:
                self._flat(t, out)
            return
        k = (tok[0], tok[1])
        if out.get(k, 0) < tok[2]:
            out[k] = tok[2]

    def _wait_all(self, eng, deps):
        d = {}
        self._flat(deps, d)
        for (kind, key), v in d.items():
            self._wait(eng, (kind, key, v))

    def op(self, eng, fn, deps=(), sig=True):
        self._wait_all(eng, deps)
        self.nops += 1
        if sig:
            self.cnt[eng] += 1
            sem = self.sem[eng]
            self.ops[eng].append(lambda h, fn=fn, sem=sem: fn(h).then_inc(sem, 1))
            return ("e", eng, self.cnt[eng])
        self.ops[eng].append(lambda h, fn=fn: fn(h))
        return None

    def dma(self, eng, name, out, in_, deps=()):
        self._wait_all(eng, deps)
        s = self._dsem(name)
        s[1] += 16
        sem = s[0]
        self.ops[eng].append(lambda h, out=out, in_=in_, sem=sem: h.dma_start(out=out, in_=in_).then_inc(sem, 16))
        return ("d", name, s[1])

    def all_tokens(self):
        toks = [("e", e, self.cnt[e]) for e in ENG if self.cnt[e] > 0]
        toks += [("d", n, s[1]) for n, s in self.dsem.items() if s[1] > 0]
        return toks

    def barrier(self):
        toks = self.all_tokens()
        for e in ENG:
            self._wait(e, toks)

    def flush(self, scope=None):
        nc = self.nc
        if scope is not None:
            with nc.named_scope(_uname(scope)):
                self.flush(None)
            return
        with nc.Block() as block:
            for e, deco in (("pe", block.tensor), ("act", block.scalar), ("dve", block.vector),
                            ("pool", block.gpsimd), ("sp", block.sync)):
                ops = self.ops[e]

                def run(h, ops=ops):
                    for f in ops:
                        f(h)
                deco(run)
        self.ops = {e: [] for e in ENG}


class Psum:
    def __init__(self, g, nc, es):
        self.g = g
        self.banks = [es.enter_context(nc.psum_tensor("ps%d" % i, [128, 512], F32)) for i in range(7)]
        self.tb = es.enter_context(nc.psum_tensor("pstb", [128, 1024], BF16))
        self.free = [None] * 7
        self.tbfree = None
        self.rr = 0

    def get(self, allowed=None):
        allowed = allowed if allowed is not None else list(range(7))
        k = allowed[self.rr % len(allowed)]
        self.rr += 1
        return k

    def rel(self, k, tok):
        self.free[k] = tok


def mm_group(g, out_ap, pairs, deps=(), sig_last=True):
    n = len(pairs)
    tok = None
    for i, (l, r) in enumerate(pairs):
        tok = g.op("pe", lambda h, l=l, r=r, i=i: h.matmul(out_ap, l, r, start=(i == 0), stop=(i == n - 1)),
                   deps=deps if i == 0 else (), sig=(sig_last and i == n - 1))
    return tok


def phase_A(g, nc, P, TOK, x_d, w, o, cst, xdeps=()):
    NT = TOK // 512
    outtoks = []
    with ExitStack() as es:
        sb = lambda n, s, d: es.enter_context(nc.sbuf_tensor(_uname(n), s, d))
        win = sb("a_win", [128, 8, INW], BF16)
        wuq = sb("a_wuq", [128, 3, 1024], BF16)
        wuk = sb("a_wuk", [128, 2, 512], BF16)
        wuv = sb("a_wuv", [128, 2, 512], BF16)
        wst = [sb("a_wst%d" % i, [128, 1024], F32) for i in range(2)]
        gq = sb("a_gq", [128, 3], F32)
        gkv = sb("a_gkv", [128, 2], F32)
        xin = [sb("a_xin%d" % i, [128, 1024], F32) for i in range(2)]
        xb = [sb("a_xb%d" % i, [128, 1024], BF16) for i in range(2)]
        xT = [sb("a_xT%d" % i, [128, 8, 512], BF16) for i in range(2)]
        craw = sb("a_craw", [128, 5, 512], F32)
        csq = sb("a_csq", [128, 5, 512], BF16)
        rbc = sb("a_rbc", [128, 2, 512], F32)
        cn = sb("a_cn", [128, 5, 512], BF16)
        cs = [sb("a_cs%d" % i, [128, 2, 512], F32) for i in range(2)]
        rt = [sb("a_rt%d" % i, [128, 512], F32) for i in range(3)]
        NST = 6
        stg = [sb("a_stg%d" % i, [128, 512], BF16) for i in range(NST)]
        ones = cst["ones"]
        ident = cst["ident"]

        g.barrier()
        wl = []
        for fc in range(8):
            wl.append(g.dma("pool", "a_win", win[:, fc, :], w["win"][fc * 128:(fc + 1) * 128, :]))
        tgq = g.dma("sp", "a_small", gq[:, :], w["gq"])
        tgkv = g.dma("sp", "a_small2", gkv[:, :], w["gkv"])
        wtok = {}
        k = 0
        wfree = [None, None]
        for (name, dst, src, nch, ncol, gt, gs) in (("wuq", wuq, w["wuq"], 3, 1024, tgq, gq),
                                                    ("wuk", wuk, w["wuk"], 2, 512, tgkv, gkv),
                                                    ("wuv", wuv, w["wuv"], 2, 512, tgkv, gkv)):
            for j in range(nch):
                b = k % 2
                k += 1
                t1 = g.dma("sp", "a_wst%d" % b, wst[b][:, 0:ncol], src[j * 128:(j + 1) * 128, :], deps=[wfree[b]])
                t2 = g.op("dve", lambda h, dst=dst, j=j, b=b, ncol=ncol, gs=gs: h.tensor_scalar(
                    dst[:, j, :], wst[b][:, 0:ncol], gs[:, j:j + 1], None, ALU.mult), deps=[t1, gt])
                wfree[b] = t2
                wtok[(name, j)] = t2
        wall = wl + list(wtok.values())

        import os
        DBG = int(os.environ.get("DBG_A", "99"))
        stg_free = [None] * NST
        stg_i = [0]

        cur_stage = [0]

        def stage():
            i = stg_i[0] % NST
            stg_i[0] += 1
            cur_stage[0] = i
            return i

        xin_free = [None, None]
        xb_free = [None, None]
        xT_free = [None, None]
        cs_free = [None, None]
        craw_free = None
        blk_i = 0
        for t in range(NT if DBG >= 1 else 0):
            tb = t % 2
            tok0 = t * 512
            tcs = g.dma("sp", "a_cs%d" % tb, cs[tb][:, 0, :], w["cos"][:, tok0:tok0 + 512], deps=[cs_free[tb]])
            tcs2 = g.dma("sp", "a_cs%d" % tb, cs[tb][:, 1, :], w["sin"][:, tok0:tok0 + 512])
            xT_ready = []
            for blk in range(4):
                b = blk_i % 2
                blk_i += 1
                r0 = tok0 + blk * 128
                tl = g.dma("sp", "a_xin%d" % b, xin[b][:, :], x_d[r0:r0 + 128, :], deps=[xin_free[b], xdeps])
                tc = g.op("act", lambda h, b=b: h.activation(out=xb[b][:, :], in_=xin[b][:, :], func=AF.Copy),
                          deps=[tl, xb_free[b]])
                xin_free[b] = tc
                tp = None
                for fc in range(8):
                    tp = g.op("pe", lambda h, b=b, fc=fc: h.transpose(
                        P.tb[:, fc * 128:(fc + 1) * 128], xb[b][:, fc * 128:(fc + 1) * 128], ident[:, :]),
                        deps=[tc, P.tbfree] if fc == 0 else (), sig=(fc == 7))
                xb_free[b] = tp
                te = g.op("dve", lambda h, tb=tb, blk=blk: h.tensor_copy(
                    out=xT[tb][:, :, blk * 128:(blk + 1) * 128],
                    in_=P.tb[:, :].rearrange("p (c n) -> p c n", c=8)), deps=[tp, xT_free[tb]])
                P.tbfree = te
                xT_ready.append(te)
            xTt = xT[tb]
            if DBG < 2:
                continue

            def fm_chunk(c0, ncols, deps):
                k = P.get()
                tok = mm_group(g, P.banks[k][0:ncols, :],
                               [(win[:, fc, c0:c0 + ncols], xTt[:, fc, :]) for fc in range(8)],
                               deps=[deps, P.free[k], wl])
                return k, tok

            sq_toks = []
            raw_toks = []
            for j in range(5):
                k, tok = fm_chunk(j * 128, 128, xT_ready)
                t2 = g.op("dve", lambda h, k=k, j=j: h.tensor_copy(out=craw[:, j, :], in_=P.banks[k][:, :]),
                          deps=[tok, craw_free])
                t1 = g.op("act", lambda h, j=j: h.activation(out=csq[:, j, :], in_=craw[:, j, :], func=AF.Square),
                          deps=[t2, craw_free])
                P.rel(k, t2)
                sq_toks.append(t1)
                raw_toks.append(t2)
            if DBG < 3:
                continue
            rtoks = []
            for (which, js, n) in ((0, (0, 1, 2), 384.0), (1, (3, 4), 256.0)):
                k = P.get()
                tok = mm_group(g, P.banks[k][:, :], [(ones[:, :], csq[:, j, :]) for j in js],
                               deps=[[sq_toks[j] for j in js], P.free[k]])
                t1 = g.op("act", lambda h, k=k, which=which, n=n: h.activation(
                    out=rbc[:, which, :], in_=P.banks[k][:, :], func=AF.Ln, bias=1e-6, scale=1.0 / n), deps=[tok, craw_free])
                P.rel(k, t1)
                t2 = g.op("act", lambda h, which=which: h.activation(
                    out=rbc[:, which, :], in_=rbc[:, which, :], func=AF.Exp, scale=-0.5), deps=[t1])
                rtoks.append(t2)
            cn_toks = []
            for j in range(5):
                which = 0 if j < 3 else 1
                eng = "pool" if j % 2 == 0 else "dve"
                cn_toks.append(g.op(eng, lambda h, j=j, which=which: h.tensor_tensor(
                    cn[:, j, :], craw[:, j, :], rbc[:, which, :], ALU.mult), deps=[raw_toks[j], rtoks[which], craw_free]))
            craw_last = list(cn_toks)

            if DBG < 4:
                continue
            def up_chunk(wt, nk, c0, js0, deps):
                k = P.get()
                tok = mm_group(g, P.banks[k][:, :], [(wt[:, j, c0:c0 + 128], cn[:, js0 + j, :]) for j in range(nk)],
                               deps=[deps, P.free[k], wall])
                return k, tok

            def store(src_ap, dst_ap, dep):
                return g.dma("sp", "a_out%d" % cur_stage[0], dst_ap, src_ap, deps=[dep])

            def evac_store(k, tok, dst_ap, eng, scale=None, rows=128):
                i = stage()
                if eng == "act":
                    te = g.op("act", lambda h, k=k, i=i: h.activation(
                        out=stg[i][0:rows, :], in_=P.banks[k][0:rows, :], func=AF.Copy,
                        scale=(1.0 if scale is None else scale)), deps=[tok, stg_free[i]])
                else:
                    te = g.op("dve", lambda h, k=k, i=i: h.tensor_copy(out=stg[i][0:rows, :], in_=P.banks[k][0:rows, :]),
                              deps=[tok, stg_free[i]])
                P.rel(k, te)
                ts = store(stg[i][0:rows, :], dst_ap, te)
                stg_free[i] = ts
                outtoks.append(ts)
                return ts

            for hh in range(4):
                k, tok = up_chunk(wuq, 3, hh * 128, 0, cn_toks[0:3])
                craw_last.append(tok)
                evac_store(k, tok, o["qnT"][hh, :, tok0:tok0 + 512], "act" if hh % 2 == 0 else "dve")

            def rope(kA, tokA, kB, tokB, rows, dsts):
                t1 = g.op("dve", lambda h, tb=tb: h.tensor_tensor(rt[0][0:rows, :], P.banks[kA][0:rows, :], cs[tb][0:rows, 0, :], ALU.mult),
                          deps=[tokA, tcs, tcs2, stg_free_rt[0]])
                t2 = g.op("dve", lambda h, tb=tb: h.tensor_tensor(rt[1][0:rows, :], P.banks[kB][0:rows, :], cs[tb][0:rows, 1, :], ALU.mult),
                          deps=[tokB, stg_free_rt[0]])
                P.rel(kA, t1)
                P.rel(kB, t2)
                i = stage()
                t3 = g.op("pool", lambda h, i=i: h.tensor_tensor(stg[i][0:rows, :], rt[0][0:rows, :], rt[1][0:rows, :], ALU.add),
                          deps=[t1, t2, stg_free[i]])
                stg_free_rt[0] = t3
                last = None
                for (r0, r1, dst) in dsts:
                    last = store(stg[i][r0:r1, :], dst, t3)
                    outtoks.append(last)
                stg_free[i] = last
                return t3

            stg_free_rt = [None]
            for rc in range(2):
                kA, tokA = up_chunk(wuq, 3, 512 + rc * 128, 0, cn_toks[0:3])
                kB, tokB = up_chunk(wuq, 3, 768 + rc * 128, 0, cn_toks[0:3])
                craw_last += [tokA, tokB]
                rope(kA, tokA, kB, tokB, 128, [(0, 64, o["qrT"][2 * rc, :, tok0:tok0 + 512]),
                                                (64, 128, o["qrT"][2 * rc + 1, :, tok0:tok0 + 512])])
            kA, tokA = fm_chunk(640, 64, xT_ready)
            kB, tokB = fm_chunk(704, 64, xT_ready)
            trope = rope(kA, tokA, kB, tokB, 64, [(0, 64, o["krT"][:, tok0:tok0 + 512])])
            cs_free[tb] = trope
            for hh in range(4):
                k, tok = up_chunk(wuk, 2, hh * 128, 3, cn_toks[3:5])
                craw_last.append(tok)
                evac_store(k, tok, o["knT"][hh, :, tok0:tok0 + 512], "act" if hh % 2 == 0 else "dve")
            for blk in range(4):
                k = P.get()
                tok = mm_group(g, P.banks[k][:, :], [(cn[:, 3 + j, blk * 128:(blk + 1) * 128], wuv[:, j, :]) for j in range(2)],
                               deps=[cn_toks[3:5], P.free[k], wall])
                craw_last.append(tok)
                evac_store(k, tok, o["vm"][tok0 + blk * 128: tok0 + (blk + 1) * 128, :], "act" if blk % 2 == 0 else "dve")
            craw_free = craw_last
            for hh in range(4):
                k, tok = fm_chunk(768 + hh * 128, 128, xT_ready)
                evac_store(k, tok, o["dqT"][hh, :, tok0:tok0 + 512], "act", scale=0.125)
            for hh in range(4):
                k, tok = fm_chunk(1280 + hh * 128, 128, xT_ready)
                evac_store(k, tok, o["dkT"][hh, :, tok0:tok0 + 512], "dve" if hh % 2 == 0 else "act")
            last_x = None
            for blk in range(4):
                k = P.get()
                tok = mm_group(g, P.banks[k][:, :], [(xTt[:, fc, blk * 128:(blk + 1) * 128], win[:, fc, 1792:2304]) for fc in range(8)],
                               deps=[xT_ready, P.free[k], wl])
                last_x = tok
                evac_store(k, tok, o["dv"][tok0 + blk * 128: tok0 + (blk + 1) * 128, :], "act" if blk % 2 == 0 else "dve")
            xT_free[tb] = last_x
        g.barrier()
        g.flush("phA")
    return outtoks


def phase_B(g, nc, P, S, NM, ND, i_, mT_d, cst, lam_init, indeps=()):
    NKB = S // 128
    NG = S // 512
    mla_scale = (128 + 64) ** -0.5
    outtoks = []
    with ExitStack() as es:
        sb = lambda n, s, d: es.enter_context(nc.sbuf_tensor(_uname(n), s, d))
        kT = [sb("b_kT%d" % i, [128, S], BF16) for i in range(2)]
        qT = [sb("b_qT%d" % i, [128, S], BF16) for i in range(2)]
        vv = [sb("b_v%d" % i, [128, NKB, 132], BF16) for i in range(2)]
        krT = sb("b_krT", [64, S], BF16)
        qrT = [sb("b_qrT%d" % i, [64, S], BF16) for i in range(2)]
        bias_f = sb("b_biasf", [128, 5, 512], F32)
        bias_t = sb("b_biast", [128, 512], F32)
        bias_hl = sb("b_biashl", [128, 2, 5, 512], BF16)
        cfar = sb("b_cfar", [128, max(ND, 1)], F32)
        lamp = sb("b_lamp", [128, 256], F32)
        lamt = sb("b_lamt", [128, 8], F32)
        dnorm = sb("b_dnorm", [128, 128], F32)
        NPT = 4
        pt = [sb("b_pt%d" % i, [128, 512], BF16) for i in range(NPT)]
        osb = [sb("b_osb%d" % i, [128, 4, 128], F32) for i in range(2)]
        rs = sb("b_rs", [128, 32], F32)
        obf = sb("b_obf", [128, 4, 128], BF16)
        osq = sb("b_osq", [128, 128], F32)
        mo = [sb("b_mo%d" % i, [128, 512], BF16) for i in range(2)]
        ones = cst["ones"]
        ident = cst["ident"]
        tri = cst["tri"]
        zero_c = cst["zero_c"]

        g.barrier()
        vinit = [g.op("pool", lambda h, i=i: h.memset(vv[i][:, :, 128:132], 1.0)) for i in range(2)]
        tkr = g.dma("sp", "b_kr", krT[:, :], i_["krT"], deps=[indeps]) if NM > 0 else None
        tlam = None
        if ND > 0:
            t0 = g.dma("sp", "b_small", lamp[:, :], i_["lamp"])
            t0b = g.dma("sp", "b_small2", dnorm[:, :], i_["dnorm"])
            t0c = g.dma("sp", "b_small3", cfar[:, :], i_["cfar"])
            t1 = g.op("dve", lambda h: h.tensor_tensor(lamp[:, 0:64], lamp[:, 0:64], lamp[:, 64:128], ALU.mult), deps=[t0])
            t2 = g.op("dve", lambda h: h.tensor_tensor(lamp[:, 128:192], lamp[:, 128:192], lamp[:, 192:256], ALU.mult), deps=[t1])
            t3 = g.op("dve", lambda h: h.tensor_reduce(lamt[:, 0:1], lamp[:, 0:64], AX.X, ALU.add), deps=[t2])
            t4 = g.op("dve", lambda h: h.tensor_reduce(lamt[:, 1:2], lamp[:, 128:192], AX.X, ALU.add), deps=[t3])
            t5 = g.op("act", lambda h: h.activation(out=lamt[:, 2:4], in_=lamt[:, 0:2], func=AF.Exp), deps=[t4])
            t6 = g.op("dve", lambda h: h.tensor_tensor(lamt[:, 4:5], lamt[:, 2:3], lamt[:, 3:4], ALU.subtract), deps=[t5])
            t7 = g.op("dve", lambda h: h.tensor_scalar(lamt[:, 5:6], lamt[:, 4:5], -1.0, -float(lam_init), ALU.mult, ALU.add), deps=[t6])
            t8 = g.op("dve", lambda h: h.tensor_scalar(dnorm[:, :], dnorm[:, :], float(1.0 - lam_init), None, ALU.mult), deps=[t0b, t7])
            tlam = [t7, t8, t0c]

        kfree = [None, None]
        qfree = [None, None]
        vfree = [vinit[0], vinit[1]]
        qrfree = [None, None]
        ptfree = [None] * NPT
        pt_i = [0]
        osb_free = [None, None]
        obf_free = [None]
        mo_free = [None, None]
        mo_i = [0]
        bias_free = [None]
        SBANKS = [0, 1, 2]
        lbu = [None]
        OBANKS = [3, 4, 5, 6]
        s_rr = [0]

        def oacc(qb):
            return P.banks[OBANKS[qb]][:, 0:129]

        heads = [("m", h) for h in range(NM)] + [("d", h) for h in range(ND)]
        for hi, (kind, hh) in enumerate(heads):
            hb = hi % 2
            if kind == "m":
                tk = g.dma("sp", "b_k%d" % hb, kT[hb][:, :], i_["knT"][hh, :, :], deps=[kfree[hb], indeps])
                tq = g.dma("sp", "b_q%d" % hb, qT[hb][:, :], i_["qnT"][hh, :, :], deps=[qfree[hb]])
                tqr = g.dma("sp", "b_qr%d" % hb, qrT[hb][:, :], i_["qrT"][hh, :, :], deps=[qrfree[hb]])
                tv = g.dma("sp", "b_v%d" % hb, vv[hb][:, :, 0:128],
                           i_["vm"][:, hh * 128:(hh + 1) * 128].rearrange("(kb p) d -> p kb d", p=128), deps=[vfree[hb]])
                ld = [tk, tq, tqr, tv, tkr]
                comps = [0]
            else:
                tk = g.dma("sp", "b_k%d" % hb, kT[hb][:, :], i_["dkT"][hh, :, :], deps=[kfree[hb], indeps])
                tq = g.dma("sp", "b_q%d" % hb, qT[hb][:, :], i_["dqT"][hh, :, :], deps=[qfree[hb]])
                tv = g.dma("sp", "b_v%d" % hb, vv[hb][:, :, 0:128],
                           i_["dv"][:, hh * 128:(hh + 1) * 128].rearrange("(kb p) d -> p kb d", p=128), deps=[vfree[hb]])
                tb0 = g.dma("sp", "b_bias", bias_f[:, :, :], i_["bias"][hh].rearrange("r p q -> p r q"), deps=[bias_free[0]])
                tb1 = g.op("dve", lambda h: h.tensor_copy(out=bias_hl[:, 0, :, :], in_=bias_f[:, :, :]), deps=[tb0, bias_free[0]])
                tbl = tb1
                for r in range(5):
                    ta = g.op("dve", lambda h, r=r: h.tensor_tensor(bias_t[:, :], bias_f[:, r, :], bias_hl[:, 0, r, :], ALU.subtract), deps=[tbl])
                    tbl = g.op("dve", lambda h, r=r: h.tensor_copy(out=bias_hl[:, 1, r, :], in_=bias_t[:, :]), deps=[ta])
                ld = [tk, tq, tv, tbl, tlam]
                comps = [0, 1]
            last_pe = None
            last_bias_use = None
            for G in range(NG):
                q0 = G * 512
                comp_done = []
                for c in comps:
                    nkb = 4 * G + 4
                    pv_last = [None] * 4
                    def issue_S(kb, c=c, G=G, q0=q0):
                        r = kb - 4 * G
                        qlo = 128 * r if r > 0 else 0
                        ncol = 512 - qlo
                        sk = SBANKS[s_rr[0] % len(SBANKS)]
                        s_rr[0] += 1
                        S_ps = P.banks[sk][:, 0:ncol]
                        pairs = []
                        if kind == "m":
                            pairs.append((kT[hb][:, kb * 128:(kb + 1) * 128], qT[hb][:, q0 + qlo:q0 + 512]))
                            pairs.append((krT[:, kb * 128:(kb + 1) * 128], qrT[hb][:, q0 + qlo:q0 + 512]))
                        else:
                            pairs.append((kT[hb][c * 64:(c + 1) * 64, kb * 128:(kb + 1) * 128],
                                          qT[hb][c * 64:(c + 1) * 64, q0 + qlo:q0 + 512]))
                            if r >= -1:
                                pairs.append((ident[:, :], bias_hl[:, 0, r + 1, qlo:512]))
                                pairs.append((ident[:, :], bias_hl[:, 1, r + 1, qlo:512]))
                        tS = mm_group(g, S_ps, pairs, deps=[ld, P.free[sk]])
                        if kind == "d" and r >= -1:
                            lbu[0] = tS
                        pi = pt_i[0] % NPT
                        pt_i[0] += 1
                        if kind == "m":
                            bias_arg = zero_c[:, 0:1]
                            sc = mla_scale
                        else:
                            bias_arg = cfar[:, hh:hh + 1] if r < -1 else zero_c[:, 0:1]
                            sc = 1.0
                        tE = g.op("act", lambda h, pi=pi, S_ps=S_ps, ncol=ncol, bias_arg=bias_arg, sc=sc: h.activation(
                            out=pt[pi][:, 0:ncol], in_=S_ps, func=AF.Exp, bias=bias_arg, scale=sc),
                            deps=[tS, ptfree[pi], tlam])
                        P.rel(sk, tE)
                        if r >= 0:
                            tE = g.op("pool", lambda h, pi=pi: h.tensor_tensor(pt[pi][:, 0:128], pt[pi][:, 0:128], tri[:, :], ALU.mult),
                                      deps=[tE])
                        return (r, qlo, pi, tE)

                    def issue_PV(kb, rec, G=G):
                        r, qlo, pi, tE = rec
                        qb0 = r if r > 0 else 0
                        tpv = None
                        for qb in range(qb0, 4):
                            lastkb = 4 * G + qb
                            dep = [tE]
                            if kb == 0:
                                dep.append(P.free[OBANKS[qb]])
                            tpv = g.op("pe", lambda h, qb=qb, pi=pi, kb=kb, qlo=qlo, lastkb=lastkb, hb=hb: h.matmul(
                                oacc(qb), pt[pi][:, qb * 128 - qlo:(qb + 1) * 128 - qlo], vv[hb][:, kb, 0:129],
                                start=(kb == 0), stop=(kb == lastkb)), deps=dep if qb == qb0 else ([] if kb > 0 else dep),
                                sig=(qb == 3 or kb == lastkb))
                            if kb == lastkb:
                                pv_last[qb] = tpv
                        ptfree[pi] = tpv
                        return tpv

                    LA = 2
                    recs = {}
                    for kb in range(min(LA, nkb)):
                        recs[kb] = issue_S(kb)
                    for kb in range(nkb):
                        if kb + LA < nkb:
                            recs[kb + LA] = issue_S(kb + LA)
                        last_pe = issue_PV(kb, recs.pop(kb))
                    if lbu[0] is not None:
                        last_bias_use = lbu[0]
                    slot = c
                    tr = None
                    tn = []
                    for qb in range(4):
                        bk = OBANKS[qb]
                        col = 0
                        ci = (G * 2 + c) % 4 * 4 + qb
                        tr = g.op("dve", lambda h, bk=bk, col=col, ci=ci: h.reciprocal(rs[:, ci:ci + 1], P.banks[bk][:, col + 128:col + 129]),
                                  deps=[pv_last[qb]])
                        t_ = g.op("dve", lambda h, bk=bk, col=col, ci=ci, qb=qb, slot=slot: h.tensor_scalar(
                            osb[slot][:, qb, :], P.banks[bk][:, col:col + 128], rs[:, ci:ci + 1], None, ALU.mult),
                            deps=[tr, osb_free[slot]])
                        tn.append(t_)
                        P.rel(bk, t_)
                    comp_done.append(tn)
                if kind == "m":
                    tfin = g.op("act", lambda h: h.activation(out=obf[:, :, :], in_=osb[0][:, :, :], func=AF.Copy),
                                deps=[comp_done[0], obf_free[0]])
                    osb_free[0] = tfin
                    fin = [tfin] * 4
                else:
                    fin = []
                    for qb in range(4):
                        ta = g.op("dve", lambda h, qb=qb: h.scalar_tensor_tensor(
                            osb[0][:, qb, :], osb[1][:, qb, :], lamt[:, 5:6], osb[0][:, qb, :], ALU.mult, ALU.add),
                            deps=[comp_done[0][qb], comp_done[1][qb], tlam])
                        ci = 12 + qb
                        tb0_ = g.op("dve", lambda h, qb=qb: h.tensor_tensor(osq[:, :], osb[0][:, qb, :], osb[0][:, qb, :], ALU.mult), deps=[ta])
                        tb_ = g.op("dve", lambda h, ci=ci: h.tensor_reduce(rs[:, ci + 4:ci + 5], osq[:, :], AX.X, ALU.add), deps=[tb0_])
                        tc_ = g.op("act", lambda h, ci=ci: h.activation(out=rs[:, ci + 4:ci + 5], in_=rs[:, ci + 4:ci + 5], func=AF.Ln, bias=1e-5, scale=1.0 / 128.0), deps=[tb_])
                        td_ = g.op("act", lambda h, ci=ci: h.activation(out=rs[:, ci + 4:ci + 5], in_=rs[:, ci + 4:ci + 5], func=AF.Exp, scale=-0.5), deps=[tc_])
                        te_ = g.op("dve", lambda h, qb=qb, ci=ci: h.scalar_tensor_tensor(
                            obf[:, qb, :], osb[0][:, qb, :], rs[:, ci + 4:ci + 5], dnorm[:, :], ALU.mult, ALU.mult),
                            deps=[td_, obf_free[0]])
                        fin.append(te_)
                    osb_free[0] = fin
                    osb_free[1] = fin
                mi = mo_i[0] % 2
                mo_i[0] += 1
                tp = None
                for qb in range(4):
                    tp = g.op("pe", lambda h, qb=qb: h.transpose(P.tb[:, qb * 128:(qb + 1) * 128], obf[:, qb, :], ident[:, :]),
                              deps=[fin[qb], P.tbfree] if qb == 0 else [fin[qb]], sig=(qb == 3))
                obf_free[0] = tp
                te = g.op("act", lambda h, mi=mi: h.activation(out=mo[mi][:, :], in_=P.tb[:, 0:512], func=AF.Copy), deps=[tp, mo_free[mi]])
                P.tbfree = te
                ts = g.dma("sp", "b_out%d" % mi, mT_d[hi, :, q0:q0 + 512], mo[mi][:, :], deps=[te])
                mo_free[mi] = ts
                outtoks.append(ts)
            kfree[hb] = last_pe
            qfree[hb] = last_pe
            vfree[hb] = last_pe
            if kind == "m":
                qrfree[hb] = last_pe
            else:
                bias_free[0] = last_bias_use
        g.barrier()
        g.flush("phB")
    return outtoks


def phase_B2(g, nc, P, S, NM, ND, i_, mT_d, cst, lam_init, indeps=()):
    NKB = S // 128
    NG = S // 512
    mla_scale = (128 + 64) ** -0.5
    outtoks = []
    with ExitStack() as es:
        sb = lambda n, s, d: es.enter_context(nc.sbuf_tensor(_uname(n), s, d))
        kT = [sb("b_kT%d" % i, [128, S], BF16) for i in range(2)]
        qT = [sb("b_qT%d" % i, [128, S], BF16) for i in range(2)]
        vv = [sb("b_v%d" % i, [128, NKB, 128], BF16) for i in range(2)]
        kz = [[sb("b_kz%d_%d" % (i, c), [128, S], BF16) for c in range(2)] for i in range(2)]
        krT = sb("b_krT", [128, S], BF16)
        qrT = [sb("b_qrT%d" % i, [128, S], BF16) for i in range(2)]
        pacc = [[sb("b_pacc%d_%d" % (i, e), [128, 512], F32) for e in range(2)] for i in range(2)]
        onesf = sb("b_onesf", [128, 128], F32)
        bias_f = sb("b_biasf", [128, 5, 512], F32)
        bias_t = sb("b_biast", [128, 512], F32)
        bias_hl = sb("b_biashl", [128, 2, 5, 512], BF16)
        cfar = sb("b_cfar", [128, max(ND, 1)], F32)
        lamp = sb("b_lamp", [128, 256], F32)
        lamt = sb("b_lamt", [128, 8], F32)
        dncol = sb("b_dncol", [128, 1], F32)
        NPT = 6
        pt = [sb("b_pt%d" % i, [128, 512], BF16) for i in range(NPT)]
        rinv2 = [sb("b_rinv%d" % i, [128, 512], F32) for i in range(2)]
        on = [sb("b_on%d" % i, [128, 512], F32) for i in range(2)]
        osq = sb("b_osq", [128, 512], BF16)
        rr = sb("b_rr", [128, 512], F32)
        mo = [sb("b_mo%d" % i, [128, 512], BF16) for i in range(2)]
        ones = cst["ones"]
        ident = cst["ident"]
        tri = cst["tri"]
        zero_c = cst["zero_c"]

        g.barrier()
        zinit = [g.op("pool", lambda h: h.memset(onesf[:, :], 1.0)),
                 g.op("pool", lambda h: h.memset(krT[64:128, :], 0.0))]
        for i in range(2):
            zinit.append(g.op("pool", lambda h, i=i: h.memset(qrT[i][64:128, :], 0.0)))
            zinit.append(g.op("pool", lambda h, i=i: h.memset(kz[i][0][64:128, :], 0.0)))
            zinit.append(g.op("pool", lambda h, i=i: h.memset(kz[i][1][0:64, :], 0.0)))
        tkr = g.dma("sp", "b_kr", krT[0:64, :], i_["krT"], deps=[indeps, zinit]) if NM > 0 else None
        tlam = None
        if ND > 0:
            t0 = g.dma("sp", "b_small", lamp[:, :], i_["lamp"])
            t0b = g.dma("sp", "b_small2", dncol[:, :], i_["dnorm"][0:1, :].rearrange("o d -> d o"))
            t0c = g.dma("sp", "b_small3", cfar[:, :], i_["cfar"])
            t1 = g.op("dve", lambda h: h.tensor_tensor(lamp[:, 0:64], lamp[:, 0:64], lamp[:, 64:128], ALU.mult), deps=[t0])
            t2 = g.op("dve", lambda h: h.tensor_tensor(lamp[:, 128:192], lamp[:, 128:192], lamp[:, 192:256], ALU.mult), deps=[t1])
            t3 = g.op("dve", lambda h: h.tensor_reduce(lamt[:, 0:1], lamp[:, 0:64], AX.X, ALU.add), deps=[t2])
            t4 = g.op("dve", lambda h: h.tensor_reduce(lamt[:, 1:2], lamp[:, 128:192], AX.X, ALU.add), deps=[t3])
            t5 = g.op("act", lambda h: h.activation(out=lamt[:, 2:4], in_=lamt[:, 0:2], func=AF.Exp), deps=[t4])
            t6 = g.op("dve", lambda h: h.tensor_tensor(lamt[:, 4:5], lamt[:, 2:3], lamt[:, 3:4], ALU.subtract), deps=[t5])
            t7 = g.op("dve", lambda h: h.tensor_scalar(lamt[:, 5:6], lamt[:, 4:5], -1.0, -float(lam_init), ALU.mult, ALU.add), deps=[t6])
            t8 = g.op("dve", lambda h: h.tensor_scalar(dncol[:, :], dncol[:, :], float(1.0 - lam_init), None, ALU.mult), deps=[t0b, t7])
            tlam = [t7, t8, t0c]

        kfree = [None, None]
        qfree = [None, None]
        vfree = [None, None]
        qrfree = [None, None]
        ptfree = [None] * NPT
        pt_i = [0]
        mo_free = [None, None]
        mo_i = [0]
        bias_free = [None]
        lbu = [None]
        SBANKS = [0, 1, 2, 3]
        OTB, RSB = [4, 6], [5, 5]
        grp_i = [0]
        s_rr = [0]
        rinv_free = [None, None]
        on_free = [None, None]
        osq_free = [None]
        rr_free = [None]
        acc_i = [0]
        pacc_free = [None, None]
        lastA = [None, None]

        heads = [("m", h) for h in range(NM)] + [("d", h) for h in range(ND)]
        for hi, (kind, hh) in enumerate(heads):
            hb = hi % 2
            if kind == "m":
                tk = g.dma("sp", "b_k%d" % hb, kT[hb][:, :], i_["knT"][hh, :, :], deps=[kfree[hb], indeps])
                tq = g.dma("sp", "b_q%d" % hb, qT[hb][:, :], i_["qnT"][hh, :, :], deps=[qfree[hb]])
                tqr = g.dma("sp", "b_qr%d" % hb, qrT[hb][0:64, :], i_["qrT"][hh, :, :], deps=[qrfree[hb], zinit])
                tv = g.dma("sp", "b_v%d" % hb, vv[hb][:, :, :],
                           i_["vm"][:, hh * 128:(hh + 1) * 128].rearrange("(kb p) d -> p kb d", p=128), deps=[vfree[hb]])
                ld = [tk, tq, tqr, tv, tkr]
                comps = [0]
            else:
                tk0 = g.dma("sp", "b_k%d" % hb, kz[hb][0][0:64, :], i_["dkT"][hh, 0:64, :], deps=[kfree[hb], indeps, zinit])
                tk = g.dma("sp", "b_k%d" % hb, kz[hb][1][64:128, :], i_["dkT"][hh, 64:128, :])
                tq = g.dma("sp", "b_q%d" % hb, qT[hb][:, :], i_["dqT"][hh, :, :], deps=[qfree[hb]])
                tv = g.dma("sp", "b_v%d" % hb, vv[hb][:, :, :],
                           i_["dv"][:, hh * 128:(hh + 1) * 128].rearrange("(kb p) d -> p kb d", p=128), deps=[vfree[hb]])
                tb0 = g.dma("sp", "b_bias", bias_f[:, :, :], i_["bias"][hh].rearrange("r p q -> p r q"), deps=[bias_free[0]])
                tb1 = g.op("dve", lambda h: h.tensor_copy(out=bias_hl[:, 0, :, :], in_=bias_f[:, :, :]), deps=[tb0, bias_free[0]])
                tbl = tb1
                for r in range(5):
                    ta = g.op("dve", lambda h, r=r: h.tensor_tensor(bias_t[:, :], bias_f[:, r, :], bias_hl[:, 0, r, :], ALU.subtract), deps=[tbl])
                    tbl = g.op("dve", lambda h, r=r: h.tensor_copy(out=bias_hl[:, 1, r, :], in_=bias_t[:, :]), deps=[ta])
                ld = [tk0, tk, tq, tv, tbl, tlam]
                comps = [0, 1]
            last_pe = None
            lbu[0] = None
            for G in range(NG):
                q0 = G * 512
                nkb = 4 * G + 4
                on_tok = []
                for c in comps:
                    gi = grp_i[0] % 2
                    grp_i[0] += 1
                    OT, RS = OTB[gi], RSB[gi]
                    rinv = rinv2[gi]

                    def issue_S(kb, c=c, G=G, q0=q0):
                        r = kb - 4 * G
                        qlo = 128 * r if r > 0 else 0
                        ncol = 512 - qlo
                        sk = SBANKS[s_rr[0] % len(SBANKS)]
                        s_rr[0] += 1
                        S_ps = P.banks[sk][:, 0:ncol]
                        pairs = []
                        if kind == "m":
                            pairs.append((kT[hb][:, kb * 128:(kb + 1) * 128], qT[hb][:, q0 + qlo:q0 + 512]))
                            pairs.append((krT[:, kb * 128:(kb + 1) * 128], qrT[hb][:, q0 + qlo:q0 + 512]))
                        else:
                            pairs.append((kz[hb][c][:, kb * 128:(kb + 1) * 128], qT[hb][:, q0 + qlo:q0 + 512]))
                            if r >= -1:
                                pairs.append((ident[:, :], bias_hl[:, 0, r + 1, qlo:512]))
                                pairs.append((ident[:, :], bias_hl[:, 1, r + 1, qlo:512]))
                        tS = mm_group(g, S_ps, pairs, deps=[ld, P.free[sk]])
                        if kind == "d" and r >= -1:
                            lbu[0] = tS
                        pi = pt_i[0] % NPT
                        pt_i[0] += 1
                        if kind == "m":
                            bias_arg = zero_c[:, 0:1]
                            sc = mla_scale
                        else:
                            bias_arg = cfar[:, hh:hh + 1] if r < -1 else zero_c[:, 0:1]
                            sc = 1.0
                        tE = g.op("act", lambda h, pi=pi, S_ps=S_ps, ncol=ncol, bias_arg=bias_arg, sc=sc: h.activation(
                            out=pt[pi][:, 0:ncol], in_=S_ps, func=AF.Exp, bias=bias_arg, scale=sc),
                            deps=[tS, ptfree[pi], tlam])
                        P.rel(sk, tE)
                        if r >= 0:
                            tE = g.op("pool", lambda h, pi=pi: h.tensor_tensor(pt[pi][:, 0:128], pt[pi][:, 0:128], tri[:, :], ALU.mult),
                                      deps=[tE])
                        ai = acc_i[0] % 2
                        if kb == 0:
                            tA = g.op("dve", lambda h, pi=pi, ai=ai: h.tensor_copy(out=pacc[ai][0][:, :], in_=pt[pi][:, :]),
                                      deps=[tE, pacc_free[ai]])
                            tZ = g.op("pool", lambda h, ai=ai: h.memset(pacc[ai][1][:, :], 0.0), deps=[pacc_free[ai]])
                            lastA[0] = tA
                            lastA[1] = tZ
                        elif kb % 2 == 0:
                            tA = g.op("dve", lambda h, pi=pi, ai=ai, qlo=qlo, ncol=ncol: h.tensor_tensor(
                                pacc[ai][0][:, qlo:512], pacc[ai][0][:, qlo:512], pt[pi][:, 0:ncol], ALU.add), deps=[tE])
                            lastA[0] = tA
                        else:
                            tA = g.op("pool", lambda h, pi=pi, ai=ai, qlo=qlo, ncol=ncol: h.tensor_tensor(
                                pacc[ai][1][:, qlo:512], pacc[ai][1][:, qlo:512], pt[pi][:, 0:ncol], ALU.add), deps=[tE, lastA[1]])
                            lastA[1] = tA
                        return (qlo, ncol, pi, tE, tA)

                    def issue_PV(kb, rec, nkb=nkb, OT=OT):
                        qlo, ncol, pi, tE, tA = rec
                        first = (kb == 0)
                        last = (kb == nkb - 1)
                        t = g.op("pe", lambda h, pi=pi, kb=kb, qlo=qlo, ncol=ncol, hb=hb, first=first, last=last, OT=OT: h.matmul(
                            P.banks[OT][:, qlo:512], vv[hb][:, kb, :], pt[pi][:, 0:ncol], start=first, stop=last),
                            deps=[tE, P.free[OT]] if first else [tE], sig=True)
                        ptfree[pi] = [t, tA]
                        return t

                    recs = {}
                    for kb in range(min(2, nkb)):
                        recs[kb] = issue_S(kb)
                    tpv = None
                    for kb0 in range(0, nkb, 2):
                        for kb in (kb0 + 2, kb0 + 3):
                            if kb < nkb:
                                recs[kb] = issue_S(kb)
                        for kb in (kb0, kb0 + 1):
                            if kb < nkb:
                                tpv = issue_PV(kb, recs.pop(kb))
                    last_pe = tpv
                    ai = acc_i[0] % 2
                    acc_i[0] += 1
                    g.op("pe", lambda h, ai=ai, RS=RS: h.matmul(P.banks[RS][:, :], onesf[:, :], pacc[ai][0][:, :], start=True, stop=False),
                         deps=[lastA[0], lastA[1], P.free[RS]], sig=False)
                    trs = g.op("pe", lambda h, ai=ai, RS=RS: h.matmul(P.banks[RS][:, :], onesf[:, :], pacc[ai][1][:, :], start=False, stop=True))
                    pacc_free[ai] = trs
                    last_pe = trs
                    tr = g.op("dve", lambda h, RS=RS, rinv=rinv: h.reciprocal(rinv[:, :], P.banks[RS][:, :]), deps=[trs, rinv_free[gi]])
                    P.rel(RS, tr)
                    if kind == "m":
                        mi = mo_i[0] % 2
                        mo_i[0] += 1
                        tn = g.op("dve", lambda h, mi=mi, OT=OT, rinv=rinv: h.tensor_tensor(mo[mi][:, :], P.banks[OT][:, :], rinv[:, :], ALU.mult),
                                  deps=[tr, mo_free[mi]])
                        P.rel(OT, tn)
                        rinv_free[gi] = tn
                        ts = g.dma("sp", "b_out%d" % mi, mT_d[hi, :, q0:q0 + 512], mo[mi][:, :], deps=[tn])
                        mo_free[mi] = ts
                        outtoks.append(ts)
                    else:
                        tn = g.op("dve", lambda h, c=c, OT=OT, rinv=rinv: h.tensor_tensor(on[c][:, :], P.banks[OT][:, :], rinv[:, :], ALU.mult),
                                  deps=[tr, on_free[c]])
                        P.rel(OT, tn)
                        rinv_free[gi] = tn
                        on_tok.append(tn)
                if kind == "d":
                    ta = g.op("dve", lambda h: h.scalar_tensor_tensor(on[0][:, :], on[1][:, :], lamt[:, 5:6], on[0][:, :], ALU.mult, ALU.add),
                              deps=[on_tok, tlam])
                    on_free[1] = ta
                    tsq = g.op("pool", lambda h: h.tensor_tensor(osq[:, :], on[0][:, :], on[0][:, :], ALU.mult), deps=[ta, osq_free[0]])
                    XB = RS
                    tmm = g.op("pe", lambda h, XB=XB: h.matmul(P.banks[XB][:, :], ones[:, :], osq[:, :], start=True, stop=True),
                               deps=[tsq, P.free[XB]])
                    osq_free[0] = tmm
                    t1_ = g.op("act", lambda h, XB=XB: h.activation(out=rr[:, :], in_=P.banks[XB][:, :], func=AF.Ln, bias=1e-5, scale=1.0 / 128.0),
                               deps=[tmm, rr_free[0]])
                    P.rel(XB, t1_)
                    t2_ = g.op("act", lambda h: h.activation(out=rr[:, :], in_=rr[:, :], func=AF.Exp, scale=-0.5), deps=[t1_])
                    mi = mo_i[0] % 2
                    mo_i[0] += 1
                    tf = g.op("dve", lambda h, mi=mi: h.scalar_tensor_tensor(mo[mi][:, :], on[0][:, :], dncol[:, 0:1], rr[:, :], ALU.mult, ALU.mult),
                              deps=[t2_, ta, mo_free[mi], tlam])
                    rr_free[0] = tf
                    on_free[0] = tf
                    ts = g.dma("sp", "b_out%d" % mi, mT_d[hi, :, q0:q0 + 512], mo[mi][:, :], deps=[tf])
                    mo_free[mi] = ts
                    outtoks.append(ts)
            kfree[hb] = last_pe
            qfree[hb] = last_pe
            vfree[hb] = last_pe
            if kind == "m":
                qrfree[hb] = last_pe
            else:
                bias_free[0] = lbu[0]
        g.barrier()
        g.flush("phB")
    return outtoks


def ln_block(g, st, A, lnp, gi, bi, dep, tln):
    t1 = g.op("dve", lambda h: h.bn_stats(out=st[:, 0:6], in_=A(0, 512)), deps=[dep])
    t2 = g.op("dve", lambda h: h.bn_stats(out=st[:, 6:12], in_=A(512, 1024)), deps=[dep])
    t3 = g.op("dve", lambda h: h.bn_aggr(out=st[:, 12:14], in_=st[:, 0:12]), deps=[t1, t2])
    t4a = g.op("act", lambda h: h.activation(out=st[:, 14:15], in_=st[:, 13:14], func=AF.Ln, bias=1e-5, scale=1.0), deps=[t3])
    t4 = g.op("act", lambda h: h.activation(out=st[:, 14:15], in_=st[:, 14:15], func=AF.Exp, scale=-0.5), deps=[t4a])
    t5 = g.op("dve", lambda h: h.tensor_scalar(A(0, 1024), A(0, 1024), st[:, 12:13], st[:, 14:15], ALU.subtract, ALU.mult), deps=[t4])
    t6 = g.op("dve", lambda h: h.tensor_tensor(A(0, 1024), A(0, 1024), lnp[:, gi, :], ALU.mult), deps=[t5, tln])
    t7 = g.op("dve", lambda h: h.tensor_tensor(A(0, 1024), A(0, 1024), lnp[:, bi, :], ALU.add), deps=[t6])
    return t7


def gates_block(g, lg, dep):
    L = lg[:, 0:8]
    m1, m2 = lg[:, 8:9], lg[:, 9:10]
    k1, k2 = lg[:, 16:24], lg[:, 24:32]
    L2 = lg[:, 32:40]
    e, g1, g2 = lg[:, 10:11], lg[:, 11:12], lg[:, 12:13]
    t = g.op("dve", lambda h: h.tensor_reduce(m1, L, AX.X, ALU.max), deps=[dep])
    t = g.op("dve", lambda h: h.tensor_scalar(k1, L, m1, None, ALU.is_equal), deps=[t])
    t = g.op("dve", lambda h: h.scalar_tensor_tensor(L2, k1, -1e30, L, ALU.mult, ALU.add), deps=[t])
    t = g.op("dve", lambda h: h.tensor_reduce(m2, L2, AX.X, ALU.max), deps=[t])
    t = g.op("dve", lambda h: h.tensor_scalar(k2, L2, m2, None, ALU.is_equal), deps=[t])
    t = g.op("dve", lambda h: h.tensor_tensor(e, m2, m1, ALU.subtract), deps=[t])
    t = g.op("act", lambda h: h.activation(out=e, in_=e, func=AF.Exp), deps=[t])
    t = g.op("dve", lambda h: h.tensor_scalar(g1, e, 1.0, None, ALU.add), deps=[t])
    t = g.op("dve", lambda h: h.reciprocal(g1, g1), deps=[t])
    t = g.op("dve", lambda h: h.tensor_tensor(g2, e, g1, ALU.mult), deps=[t])
    t = g.op("dve", lambda h: h.tensor_scalar(k1, k1, g1, None, ALU.mult), deps=[t])
    t = g.op("dve", lambda h: h.scalar_tensor_tensor(lg[:, 56:64], k2, g2, k1, ALU.mult, ALU.add), deps=[t])
    return t


def phase_C(g, nc, P, TOK, moe, x_d, mT_d, w, xo_d, cst, indeps=()):
    CT = min(TOK, 2048)
    NCH = TOK // CT
    NB = CT // 128
    NTT = CT // 512
    F = FE if moe else FD
    nfc_all = F // 128
    FT = 4
    outtoks = []
    ident = cst["ident"]
    with ExitStack() as es:
        sb = lambda n, s, d: es.enter_context(nc.sbuf_tensor(_uname(n), s, d))
        yacc = sb("c_yacc", [128, NB, 1024], F32)
        x1T = sb("c_x1T", [128, 8, CT], BF16)
        lnp = sb("c_lnp", [128, 2, 1024], F32)
        st = sb("c_st", [128, 16], F32)
        sg = [sb("c_sg%d" % i, [128, 512], F32) for i in range(2)]
        tt_ = [sb("c_tt%d" % i, [128, 512], F32) for i in range(2)]
        if moe:
            gT = sb("c_gT", [8, CT], F32)
            gbc = sb("c_gbc", [128, CT], F32)
            sel = cst["sel"]
            identf = cst["identf"]
        g.barrier()
        for ch in range(NCH):
            c0 = ch * CT
            with ExitStack() as es1:
                sb1 = lambda n, s, d: es1.enter_context(nc.sbuf_tensor(_uname(n), s, d))
                wo = sb1("c_wo", [128, 8, 1024], BF16)
                xin = [sb1("c_xin%d" % i, [128, 1024], F32) for i in range(2)]
                xbf = [sb1("c_xbf%d" % i, [128, 1024], BF16) for i in range(2)]
                mTs = [sb1("c_mT%d" % i, [128, 8, 128], BF16) for i in range(2)]
                if moe:
                    rw = sb1("c_rw", [128, 8, 8], F32)
                    x1Tf = [sb1("c_x1Tf%d" % i, [128, 8, 128], F32) for i in range(2)]
                    lg = sb1("c_lg", [128, 64], F32)
                g.barrier()
                two = [g.dma("pool", "c_wo", wo[:, fc, :], w["wo"][fc * 128:(fc + 1) * 128, :]) for fc in range(8)]
                tln = g.dma("sp", "c_ln", lnp[:, :, :], w["ln"][:, 0:2, :])
                trw = g.dma("sp", "c_rw", rw[:, :, :], w["router"]) if moe else None
                xin_free = [None, None]
                xbf_free = [None, None]
                mT_free = [None, None]
                x1Tf_free = [None, None]
                gate_ready = None
                for blk in range(NB):
                    b = blk % 2
                    r0 = c0 + blk * 128
                    tlx = g.dma("sp", "c_xin%d" % b, xin[b][:, :], x_d[r0:r0 + 128, :], deps=[xin_free[b], indeps])
                    tlm = g.dma("sp", "c_mT%d" % b, mTs[b][:, :, :], mT_d[:, :, r0:r0 + 128].rearrange("c p t -> p c t"),
                                deps=[mT_free[b], indeps])
                    ks = []
                    for half in range(2):
                        k = P.get()
                        tok = mm_group(g, P.banks[k][:, :], [(mTs[b][:, hc, :], wo[:, hc, half * 512:(half + 1) * 512]) for hc in range(8)],
                                       deps=[tlm, two, P.free[k]])
                        ks.append((k, tok))
                    mT_free[b] = ks[1][1]
                    ty = None
                    for half in range(2):
                        k, tok = ks[half]
                        ty = g.op("dve", lambda h, b=b, k=k, half=half: h.scalar_tensor_tensor(
                            xin[b][:, half * 512:(half + 1) * 512], xin[b][:, half * 512:(half + 1) * 512], float(DN_ALPHA),
                            P.banks[k][:, :], ALU.mult, ALU.add), deps=[tok, tlx, ty])
                        P.rel(k, ty)
                    tl = ln_block(g, st, (lambda a, c, b=b: xin[b][:, a:c]), lnp, 0, 1, ty, tln)
                    ta = g.op("act", lambda h, b=b, blk=blk: h.activation(out=yacc[:, blk, :], in_=xin[b][:, :], func=AF.Copy, scale=float(DN_ALPHA)),
                              deps=[tl])
                    tb_ = g.op("act", lambda h, b=b: h.activation(out=xbf[b][:, :], in_=xin[b][:, :], func=AF.Copy), deps=[tl, xbf_free[b]])
                    tp = None
                    for fc in range(8):
                        tp = g.op("pe", lambda h, b=b, fc=fc: h.transpose(P.tb[:, fc * 128:(fc + 1) * 128], xbf[b][:, fc * 128:(fc + 1) * 128], ident[:, :]),
                                  deps=[tb_, P.tbfree] if fc == 0 else (), sig=(fc == 7))
                    xbf_free[b] = tp
                    te = g.op("dve", lambda h, blk=blk: h.tensor_copy(out=x1T[:, :, blk * 128:(blk + 1) * 128],
                                                                      in_=P.tb[:, :].rearrange("p (c n) -> p c n", c=8)), deps=[tp])
                    P.tbfree = te
                    xfree = [ta, tb_]
                    if moe:
                        fb = blk % 2
                        parts = []
                        for half in range(2):
                            kk = P.get()
                            tpf = None
                            for j in range(4):
                                fc = half * 4 + j
                                tpf = g.op("pe", lambda h, b=b, fc=fc, j=j, kk=kk: h.transpose(
                                    P.banks[kk][:, j * 128:(j + 1) * 128], xin[b][:, fc * 128:(fc + 1) * 128], identf[:, :]),
                                    deps=[tl, P.free[kk]] if j == 0 else (), sig=(j == 3))
                            tcp = g.op("act", lambda h, fb=fb, half=half, kk=kk: h.activation(
                                out=x1Tf[fb][:, half * 4:(half + 1) * 4, :], in_=P.banks[kk][:, :].rearrange("p (c n) -> p c n", c=4), func=AF.Copy),
                                deps=[tpf, x1Tf_free[fb]])
                            P.rel(kk, tcp)
                            parts.append(tcp)
                            xfree.append(tpf)
                        kk = P.get()
                        tlg = mm_group(g, P.banks[kk][:, 0:8], [(x1Tf[fb][:, fc, :], rw[:, fc, :]) for fc in range(8)],
                                       deps=[parts, trw, P.free[kk]])
                        x1Tf_free[fb] = tlg
                        tlc = g.op("dve", lambda h, kk=kk: h.tensor_copy(out=lg[:, 0:8], in_=P.banks[kk][:, 0:8]), deps=[tlg, gate_ready])
                        P.rel(kk, tlc)
                        tgt = gates_block(g, lg, tlc)
                        kk = P.get()
                        ttp = g.op("pe", lambda h, kk=kk: h.transpose(P.banks[kk][0:8, 0:128], lg[:, 56:64], identf[:, :]), deps=[tgt, P.free[kk]])
                        tgc = g.op("dve", lambda h, kk=kk, blk=blk: h.tensor_copy(out=gT[:, blk * 128:(blk + 1) * 128], in_=P.banks[kk][0:8, 0:128]),
                                   deps=[ttp])
                        P.rel(kk, tgc)
                        gate_ready = tgc
                    xin_free[b] = xfree
                g.barrier()
                g.flush("phC1")
            with ExitStack() as es2:
                sb2 = lambda n, s, d: es2.enter_context(nc.sbuf_tensor(_uname(n), s, d))
                hT = sb2("c_hT", [128, FT, CT], BF16)
                wgu = [sb2("c_wgu%d" % i, [128, 2, 8, FT * 128], BF16) for i in range(2)]
                wd = [sb2("c_wd%d" % i, [128, FT, 1024], BF16) for i in range(2)]
                g.barrier()
                wfree_gu = [None, None]
                wfree_d = [None, None]
                hfree = None
                sgfree = [None, None]
                ttfree = [None, None]
                wt_i = 0
                yl = [None] * NB
                experts = list(range(NE)) if moe else [None]
                gb_free = None
                for e in experts:
                    tgb = None
                    if moe:
                        tgb = []
                        for tt in range(NTT):
                            kk = P.get()
                            tmm = g.op("pe", lambda h, kk=kk, e=e, tt=tt: h.matmul(P.banks[kk][:, :], sel[:, e, :], gT[:, tt * 512:(tt + 1) * 512],
                                                                                  start=True, stop=True), deps=[P.free[kk]])
                            tcp = g.op("act", lambda h, kk=kk, tt=tt: h.activation(out=gbc[:, tt * 512:(tt + 1) * 512], in_=P.banks[kk][:, :], func=AF.Copy),
                                       deps=[tmm, gb_free])
                            P.rel(kk, tcp)
                            tgb.append(tcp)
                    gb_last = []
                    nft = (nfc_all + FT - 1) // FT
                    for ft in range(nft):
                        nf = min(FT, nfc_all - ft * FT)
                        wb = wt_i % 2
                        wt_i += 1
                        f0 = ft * FT * 128
                        src_g = w["wg"][e] if moe else w["wg"]
                        src_u = w["wu"][e] if moe else w["wu"]
                        src_d = w["wd"][e] if moe else w["wd"]
                        twg = g.dma("pool", "c_wg%d" % wb, wgu[wb][:, 0, :, 0:nf * 128],
                                    src_g[:, f0:f0 + nf * 128].rearrange("(fc p) f -> p fc f", p=128), deps=[wfree_gu[wb]])
                        twu = g.dma("pool", "c_wu%d" % wb, wgu[wb][:, 1, :, 0:nf * 128],
                                    src_u[:, f0:f0 + nf * 128].rearrange("(fc p) f -> p fc f", p=128))
                        twd = g.dma("pool", "c_wd%d" % wb, wd[wb][:, 0:nf, :],
                                    src_d[f0:f0 + nf * 128, :].rearrange("(j p) d -> p j d", p=128), deps=[wfree_d[wb]])
                        h_toks = []
                        last_gu = None
                        for j in range(nf):
                            for tt in range(NTT):
                                kg = P.get()
                                tg_ = mm_group(g, P.banks[kg][:, :], [(wgu[wb][:, 0, fc, j * 128:(j + 1) * 128], x1T[:, fc, tt * 512:(tt + 1) * 512]) for fc in range(8)],
                                               deps=[twg, P.free[kg]])
                                ku = P.get()
                                tu_ = mm_group(g, P.banks[ku][:, :], [(wgu[wb][:, 1, fc, j * 128:(j + 1) * 128], x1T[:, fc, tt * 512:(tt + 1) * 512]) for fc in range(8)],
                                               deps=[twu, P.free[ku]])
                                last_gu = tu_
                                si = (j * NTT + tt) % 2
                                ts_ = g.op("act", lambda h, kg=kg, si=si: h.activation(out=sg[si][:, :], in_=P.banks[kg][:, :], func=AF.Silu),
                                           deps=[tg_, sgfree[si]])
                                P.rel(kg, ts_)
                                if moe:
                                    tm_ = g.op("dve", lambda h, ku=ku, si=si: h.tensor_tensor(tt_[si][:, :], sg[si][:, :], P.banks[ku][:, :], ALU.mult),
                                               deps=[ts_, tu_, ttfree[si]])
                                    P.rel(ku, tm_)
                                    sgfree[si] = tm_
                                    th_ = g.op("pool", lambda h, si=si, j=j, tt=tt: h.tensor_tensor(
                                        hT[:, j, tt * 512:(tt + 1) * 512], tt_[si][:, :], gbc[:, tt * 512:(tt + 1) * 512], ALU.mult),
                                        deps=[tm_, tgb, hfree])
                                    ttfree[si] = th_
                                    gb_last.append(th_)
                                else:
                                    th_ = g.op("dve", lambda h, ku=ku, si=si, j=j, tt=tt: h.tensor_tensor(
                                        hT[:, j, tt * 512:(tt + 1) * 512], sg[si][:, :], P.banks[ku][:, :], ALU.mult),
                                        deps=[ts_, tu_, hfree])
                                    P.rel(ku, th_)
                                    sgfree[si] = th_
                                h_toks.append(th_)
                        wfree_gu[wb] = last_gu
                        last_d = None
                        for blk in range(NB):
                            for half in range(2):
                                kk = P.get()
                                td_ = mm_group(g, P.banks[kk][:, :], [(hT[:, j, blk * 128:(blk + 1) * 128], wd[wb][:, j, half * 512:(half + 1) * 512]) for j in range(nf)],
                                               deps=[h_toks, twd, P.free[kk]])
                                last_d = td_
                                tacc = g.op("dve", lambda h, kk=kk, blk=blk, half=half: h.tensor_tensor(
                                    yacc[:, blk, half * 512:(half + 1) * 512], yacc[:, blk, half * 512:(half + 1) * 512], P.banks[kk][:, :], ALU.add),
                                    deps=[td_, yl[blk]])
                                P.rel(kk, tacc)
                                yl[blk] = tacc
                        wfree_d[wb] = last_d
                        hfree = last_d
                    if moe:
                        gb_free = gb_last
                g.barrier()
                g.flush("phC2")
            tln2 = g.dma("sp", "c_ln", lnp[:, :, :], w["ln"][:, 2:4, :])
            for blk in range(NB):
                tl = ln_block(g, st, (lambda a, c, blk=blk: yacc[:, blk, a:c]), lnp, 0, 1, None, tln2)
                ts = g.dma("sp", "c_out%d" % (blk % 4), xo_d[c0 + blk * 128:c0 + (blk + 1) * 128, :], yacc[:, blk, :], deps=[tl])
                outtoks.append(ts)
            g.barrier()
            g.flush("phC3")
    return outtoks


CONST_SPECS = [("c_ident", [128, 128], BF16), ("c_identf", [128, 128], F32), ("c_ones", [128, 128], BF16),
               ("c_tri", [128, 128], BF16), ("c_zero", [128, 1], F32), ("c_sel", [8, NE, 128], F32)]


def host_consts():
    ident = np.eye(128, dtype=np.float32)
    tri = (np.arange(128)[None, :] >= np.arange(128)[:, None]).astype(np.float32)
    sel = np.zeros((8, NE, 128), np.float32)
    for e in range(NE):
        sel[e, e, :] = 1.0
    return {"c_ident": ident.astype(NPBF), "c_identf": ident, "c_ones": np.ones((128, 128), NPBF),
            "c_tri": tri.astype(NPBF), "c_zero": np.zeros((128, 1), np.float32), "c_sel": sel}


def load_consts(g, nc, es):
    cst = {}
    for (name, shape, dt) in CONST_SPECS:
        d = nc.dram_tensor(name, shape, dt, kind="ExternalInput").ap()
        t = es.enter_context(nc.sbuf_tensor("s" + name, shape, dt))
        if len(shape) == 2:
            g.dma("sp", "cst", t[:, :], d)
        else:
            g.dma("sp", "cst", t[:, :, :], d)
        cst[name] = t
    return {"ident": cst["c_ident"], "identf": cst["c_identf"], "ones": cst["c_ones"], "tri": cst["c_tri"],
            "zero_c": cst["c_zero"], "sel": cst["c_sel"]}


A_IN = lambda TOK: [("win", [1024, INW], F32), ("wuq", [384, 1024], F32), ("wuk", [256, 512], F32), ("wuv", [256, 512], F32),
                    ("gq", [128, 3], F32), ("gkv", [128, 2], F32)]
A_OUT = lambda TOK: [("qnT", [4, 128, TOK], BF16), ("qrT", [4, 64, TOK], BF16), ("knT", [4, 128, TOK], BF16), ("krT", [64, TOK], BF16),
                     ("vm", [TOK, 512], BF16), ("dqT", [4, 128, TOK], BF16), ("dkT", [4, 128, TOK], BF16), ("dv", [TOK, 512], BF16)]


def B_IN(S, NM, ND):
    return [("qnT", [NM, 128, S], BF16), ("qrT", [NM, 64, S], BF16), ("knT", [NM, 128, S], BF16), ("krT", [64, S], BF16),
            ("vm", [S, NM * 128], BF16), ("dqT", [ND, 128, S], BF16), ("dkT", [ND, 128, S], BF16), ("dv", [S, ND * 128], BF16)]


B_W = lambda ND: [("bias", [ND, 5, 128, 512], F32), ("cfar", [128, ND], F32), ("lamp", [128, 256], F32), ("dnorm", [128, 128], F32)]


def C_W(moe):
    base = [("wo", [1024, 1024], F32), ("ln", [128, 4, 1024], F32)]
    if moe:
        return base + [("router", [128, 8, 8], F32), ("wg", [NE, 1024, FE], F32), ("wu", [NE, 1024, FE], F32), ("wd", [NE, FE, 1024], F32)]
    return base + [("wg", [1024, FD], F32), ("wu", [1024, FD], F32), ("wd", [FD, 1024], F32)]


PHASE_B = phase_B2


def lam_init_of(l):
    return 0.8 - 0.6 * math.exp(-0.3 * l)


def build(mode, S=4096, TOK=2048, NM=2, ND=2, moe=False, layers=(0,), lam_l=0):
    nc = bass.Bass("TRN2", target_bir_lowering=False)
    ext_in = lambda n, s, d: nc.dram_tensor(n, s, d, kind="ExternalInput").ap()
    ext_out = lambda n, s, d: nc.dram_tensor(n, s, d, kind="ExternalOutput").ap()
    internal = lambda n, s, d: nc.dram_tensor(n, s, d).ap()
    with ExitStack() as es:
        g = Gen(nc, es)
        P = Psum(g, nc, es)
        cst = load_consts(g, nc, es)
        if mode == "A":
            x = ext_in("x", [TOK, 1024], F32)
            w = {n: ext_in(n, s, d) for (n, s, d) in A_IN(TOK)}
            w["cos"] = ext_in("cos", [128, TOK], F32)
            w["sin"] = ext_in("sin", [128, TOK], F32)
            o = {n: ext_out("o_" + n, s, d) for (n, s, d) in A_OUT(TOK)}
            phase_A(g, nc, P, TOK, x, w, o, cst)
        elif mode == "B":
            i_ = {n: ext_in(n, s, d) for (n, s, d) in B_IN(S, NM, ND) + B_W(ND)}
            mT = ext_out("o_mT", [NM + ND, 128, S], BF16)
            PHASE_B(g, nc, P, S, NM, ND, i_, mT, cst, lam_init_of(lam_l))
        elif mode == "C":
            x = ext_in("x", [TOK, 1024], F32)
            mT = ext_in("mT", [8, 128, TOK], BF16)
            w = {n: ext_in(n, s, d) for (n, s, d) in C_W(moe)}
            xo = ext_out("o_x", [TOK, 1024], F32)
            phase_C(g, nc, P, TOK, moe, x, mT, w, xo, cst)
        elif mode == "F":
            x0 = ext_in("x", [S, 1024], F32)
            cos = ext_in("cos", [128, S], F32)
            sin = ext_in("sin", [128, S], F32)
            bias = ext_in("bias", [4, 5, 128, 512], F32)
            cfar = ext_in("cfar", [128, 4], F32)
            xout = ext_out("o_x", [S, 1024], F32)
            xs = [internal("xs0", [S, 1024], F32), internal("xs1", [S, 1024], F32)]
            sc = {n: internal("sc_" + n, s, d) for (n, s, d) in A_OUT(S)}
            mT = internal("sc_mT", [8, 128, S], BF16)
            xin = x0
            for li, l in enumerate(layers):
                moe_l = (l % 2 == 1)
                w = {n: ext_in("L%d_%s" % (l, n), s, d) for (n, s, d) in A_IN(S)}
                w["cos"], w["sin"] = cos, sin
                phase_A(g, nc, P, S, xin, w, sc, cst)
                i_ = dict(sc)
                i_["bias"], i_["cfar"] = bias, cfar
                i_["lamp"] = ext_in("L%d_lamp" % l, [128, 256], F32)
                i_["dnorm"] = ext_in("L%d_dnorm" % l, [128, 128], F32)
                PHASE_B(g, nc, P, S, 4, 4, i_, mT, cst, lam_init_of(l))
                wc = {n: ext_in("L%d_%s" % (l, n), s, d) for (n, s, d) in C_W(moe_l)}
                xo = xout if li == len(layers) - 1 else xs[li % 2]
                phase_C(g, nc, P, S, moe_l, xin, mT, wc, xo, cst)
                xin = xo
        g.barrier()
        g.flush()
    return nc


WIN_PERM = np.concatenate([np.arange(0, 704), np.arange(672, 704), np.arange(640, 672), np.arange(704, 2240)])
_uq = []
for _h in range(4):
    _uq.append(np.arange(_h * 192, _h * 192 + 128))
for _h in range(4):
    _uq.append(np.arange(_h * 192 + 128, _h * 192 + 192))
for _h in range(4):
    _uq.append(np.concatenate([np.arange(_h * 192 + 160, _h * 192 + 192), np.arange(_h * 192 + 128, _h * 192 + 160)]))
WUQ_PERM = np.concatenate(_uq)


def rope_tables(pos):
    inv = (1.0 / (np.float32(10000.0) ** (np.arange(0, 64, 2, dtype=np.float32) / np.float32(64)))).astype(np.float32)
    ang = pos.astype(np.float32)[None, :] * inv[:, None]
    c = np.cos(ang).astype(np.float32)
    s = np.sin(ang).astype(np.float32)
    cos_t = np.concatenate([c, c, c, c], 0)
    sin_t = np.concatenate([-s, s, -s, s], 0)
    return np.ascontiguousarray(cos_t), np.ascontiguousarray(sin_t)


def bucket_index(S):
    n = np.arange(S, dtype=np.int32)
    nf = np.maximum(n, 1).astype(np.float32)
    large = 16 + (np.log(nf / np.float32(16)) / np.float32(math.log(128 / 16)) * np.float32(16)).astype(np.int32)
    large = np.minimum(large, 31)
    return np.where(n < 16, n, large)


def bias_tiles(rel_bias, S):
    bidx = bucket_index(max(S, 1024))
    k = np.arange(128)[:, None]
    q = np.arange(512)[None, :]
    out = np.empty((4, 5, 128, 512), np.float32)
    for r in range(-1, 4):
        dist = np.clip(q - 128 * r - k, 0, len(bidx) - 1)
        out[:, r + 1] = np.transpose(rel_bias[bidx[dist]], (2, 0, 1))
    return out


def layer_A_weights(inp, l):
    return {"win": np.ascontiguousarray(inp["w_in"][l][:, WIN_PERM]),
            "wuq": np.ascontiguousarray(inp["mla_w_uq"][l][:, WUQ_PERM]),
            "wuk": np.ascontiguousarray(inp["mla_w_uk"][l]), "wuv": np.ascontiguousarray(inp["mla_w_uv"][l]),
            "gq": np.ascontiguousarray(inp["mla_q_norm"][l].reshape(3, 128).T),
            "gkv": np.ascontiguousarray(inp["mla_kv_norm"][l].reshape(2, 128).T)}


def layer_B_weights(inp, l):
    return {"lamp": np.ascontiguousarray(np.broadcast_to(inp["diff_lambda"][l].reshape(1, 256), (128, 256))),
            "dnorm": np.ascontiguousarray(np.broadcast_to(inp["diff_norm"][l].reshape(1, 128), (128, 128)))}


def layer_C_weights(inp, l):
    ln = np.stack([inp["ln1_g"][l], inp["ln1_b"][l], inp["ln2_g"][l], inp["ln2_b"][l]], 0)
    w = {"wo": np.ascontiguousarray(inp["w_o"][l]), "ln": np.ascontiguousarray(np.broadcast_to(ln[None], (128, 4, 1024)))}
    if l % 2 == 1:
        m = l // 2
        w["router"] = np.ascontiguousarray(inp["moe_router"][m].reshape(8, 128, 8).transpose(1, 0, 2))
        w["wg"], w["wu"], w["wd"] = inp["moe_w_gate"][m], inp["moe_w_up"][m], inp["moe_w_down"][m]
    else:
        m = l // 2
        w["wg"], w["wu"], w["wd"] = inp["ffn_w_gate"][m], inp["ffn_w_up"][m], inp["ffn_w_down"][m]
    return w


MODE = "fused4"
S_FULL = 4096
_PROG_CACHE = {}


def _prog(layers):
    key = tuple(layers)
    if key not in _PROG_CACHE:
        _PROG_CACHE[key] = build("F", S=S_FULL, layers=key)
    return _PROG_CACHE[key]


def _layer_inputs(inp, l):
    d = {}
    for part in (layer_A_weights(inp, l), layer_B_weights(inp, l), layer_C_weights(inp, l)):
        for k, v in part.items():
            d["L%d_%s" % (l, k)] = np.ascontiguousarray(v, dtype=np.float32)
    return d


def kernel(**inputs):
    inp = {k: np.asarray(v) for k, v in inputs.items()}
    x = np.ascontiguousarray(inp["x"], dtype=np.float32)
    B = x.shape[0]
    shared = dict(host_consts())
    shared["cos"], shared["sin"] = rope_tables(np.arange(S_FULL))
    shared["bias"] = bias_tiles(inp["rel_bias"].astype(np.float32), S_FULL)
    shared["cfar"] = np.ascontiguousarray(np.broadcast_to(inp["rel_bias"][31][None, :], (128, 4)), dtype=np.float32)
    groups = [tuple(range(DEPTH))] if MODE == "fused4" else [(l,) for l in range(DEPTH)]
    cur = [x[b] for b in range(B)]
    for layers in groups:
        nc = _prog(layers)
        base = dict(shared)
        for l in layers:
            base.update(_layer_inputs(inp, l))
        in_maps = []
        for b in range(B):
            m = dict(base)
            m["x"] = np.ascontiguousarray(cur[b])
            in_maps.append(m)
        res = run_bass_kernel_spmd(nc, in_maps, core_ids=list(range(B)))
        cur = [np.asarray(res.results[b]["o_x"], dtype=np.float32) for b in range(B)]
    return np.stack(cur, 0).astype(np.float32)
```

```python
import math
from contextlib import ExitStack

import numpy as np
import ml_dtypes

import concourse.bass as bass
import concourse.mybir as mybir
from concourse.bass_utils import run_bass_kernel_spmd

F32, BF16 = mybir.dt.float32, mybir.dt.bfloat16
AF = mybir.ActivationFunctionType
ALU = mybir.AluOpType
AX = mybir.AxisListType
NPBF = ml_dtypes.bfloat16

D = 1024
DEPTH = 4
NH = 4
INW = 2304
FD = 2816
FE = 3584
NE = 8
DN_ALPHA = (2 * DEPTH) ** 0.25
NEG = -30000.0
ENG = ["pe", "act", "dve", "pool", "sp"]


_UID = [0]


def _uname(n):
    _UID[0] += 1
    return "%s_u%d" % (n, _UID[0])


class Gen:
    def __init__(self, nc, es):
        self.nc = nc
        self.es = es
        self.ops = {e: [] for e in ENG}
        self.cnt = {e: 0 for e in ENG}
        self.seen = {e: {} for e in ENG}
        self.sem = {e: es.enter_context(nc.semaphore("s_" + e)) for e in ENG}
        self.dsem = {}
        self.nops = 0

    def _dsem(self, name):
        if name not in self.dsem:
            self.dsem[name] = [self.es.enter_context(self.nc.semaphore("d_" + name)), 0]
        return self.dsem[name]

    def _wait(self, eng, tok):
        if tok is None:
            return
        if isinstance(tok, (list, tuple)) and (len(tok) == 0 or not isinstance(tok[0], str)):
            for t in tok:
                self._wait(eng, t)
            return
        kind, key, v = tok
        if kind == "e":
            sem = self.sem[key]
            k = "e:" + key
        else:
            sem = self.dsem[key][0]
            k = "d:" + key
        if self.seen[eng].get(k, 0) >= v:
            return
        self.seen[eng][k] = v
        self.ops[eng].append(lambda h, sem=sem, v=v: h.wait_ge(sem, v))

    def _flat(self, tok, out):
        if tok is None:
            return
        if isinstance(tok, (list, tuple)) and (len(tok) == 0 or not isinstance(tok[0], str)):
            for t in tok:
                self._flat(t, out)
            return
        k = (tok[0], tok[1])
        if out.get(k, 0) < tok[2]:
            out[k] = tok[2]

    def _wait_all(self, eng, deps):
        d = {}
        self._flat(deps, d)
        for (kind, key), v in d.items():
            self._wait(eng, (kind, key, v))

    def op(self, eng, fn, deps=(), sig=True):
        self._wait_all(eng, deps)
        self.nops += 1
        if sig:
            self.cnt[eng] += 1
            sem = self.sem[eng]
            self.ops[eng].append(lambda h, fn=fn, sem=sem: fn(h).then_inc(sem, 1))
            return ("e", eng, self.cnt[eng])
        self.ops[eng].append(lambda h, fn=fn: fn(h))
        return None

    def dma(self, eng, name, out, in_, deps=()):
        self._wait_all(eng, deps)
        s = self._dsem(name)
        s[1] += 16
        sem = s[0]
        self.ops[eng].append(lambda h, out=out, in_=in_, sem=sem: h.dma_start(out=out, in_=in_).then_inc(sem, 16))
        return ("d", name, s[1])

    def all_tokens(self):
        toks = [("e", e, self.cnt[e]) for e in ENG if self.cnt[e] > 0]
        toks += [("d", n, s[1]) for n, s in self.dsem.items() if s[1] > 0]
        return toks

    def barrier(self):
        toks = self.all_tokens()
        for e in ENG:
            self._wait(e, toks)

    def flush(self, scope=None):
        nc = self.nc
        if scope is not None:
            with nc.named_scope(_uname(scope)):
                self.flush(None)
            return
        with nc.Block() as block:
            for e, deco in (("pe", block.tensor), ("act", block.scalar), ("dve", block.vector),
                            ("pool", block.gpsimd), ("sp", block.sync)):
                ops = self.ops[e]

                def run(h, ops=ops):
                    for f in ops:
                        f(h)
                deco(run)
        self.ops = {e: [] for e in ENG}


class Psum:
    def __init__(self, g, nc, es):
        self.g = g
        self.banks = [es.enter_context(nc.psum_tensor("ps%d" % i, [128, 512], F32)) for i in range(7)]
        self.tb = es.enter_context(nc.psum_tensor("pstb", [128, 1024], BF16))
        self.free = [None] * 7
        self.tbfree = None
        self.rr = 0

    def get(self, allowed=None):
        allowed = allowed if allowed is not None else list(range(7))
        k = allowed[self.rr % len(allowed)]
        self.rr += 1
        return k

    def rel(self, k, tok):
        self.free[k] = tok


def mm_group(g, out_ap, pairs, deps=(), sig_last=True):
    n = len(pairs)
    tok = None
    for i, (l, r) in enumerate(pairs):
        tok = g.op("pe", lambda h, l=l, r=r, i=i: h.matmul(out_ap, l, r, start=(i == 0), stop=(i == n - 1)),
                   deps=deps if i == 0 else (), sig=(sig_last and i == n - 1))
    return tok


def phase_A(g, nc, P, TOK, x_d, w, o, cst, xdeps=()):
    NT = TOK // 512
    outtoks = []
    with ExitStack() as es:
        sb = lambda n, s, d: es.enter_context(nc.sbuf_tensor(_uname(n), s, d))
        win = sb("a_win", [128, 8, INW], BF16)
        wuq = sb("a_wuq", [128, 3, 1024], BF16)
        wuk = sb("a_wuk", [128, 2, 512], BF16)
        wuv = sb("a_wuv", [128, 2, 512], BF16)
        wst = [sb("a_wst%d" % i, [128, 1024], F32) for i in range(2)]
        gq = sb("a_gq", [128, 3], F32)
        gkv = sb("a_gkv", [128, 2], F32)
        xin = [sb("a_xin%d" % i, [128, 1024], F32) for i in range(2)]
        xb = [sb("a_xb%d" % i, [128, 1024], BF16) for i in range(2)]
        xT = [sb("a_xT%d" % i, [128, 8, 512], BF16) for i in range(2)]
        craw = sb("a_craw", [128, 5, 512], F32)
        csq = sb("a_csq", [128, 5, 512], BF16)
        rbc = sb("a_rbc", [128, 2, 512], F32)
        cn = sb("a_cn", [128, 5, 512], BF16)
        cs = [sb("a_cs%d" % i, [128, 2, 512], F32) for i in range(2)]
        rt = [sb("a_rt%d" % i, [128, 512], F32) for i in range(3)]
        NST = 6
        stg = [sb("a_stg%d" % i, [128, 512], BF16) for i in range(NST)]
        ones = cst["ones"]
        ident = cst["ident"]

        g.barrier()
        wl = []
        for fc in range(8):
            wl.append(g.dma("pool", "a_win", win[:, fc, :], w["win"][fc * 128:(fc + 1) * 128, :]))
        tgq = g.dma("sp", "a_small", gq[:, :], w["gq"])
        tgkv = g.dma("sp", "a_small2", gkv[:, :], w["gkv"])
        wtok = {}
        k = 0
        wfree = [None, None]
        for (name, dst, src, nch, ncol, gt, gs) in (("wuq", wuq, w["wuq"], 3, 1024, tgq, gq),
                                                    ("wuk", wuk, w["wuk"], 2, 512, tgkv, gkv),
                                                    ("wuv", wuv, w["wuv"], 2, 512, tgkv, gkv)):
            for j in range(nch):
                b = k % 2
                k += 1
                t1 = g.dma("sp", "a_wst%d" % b, wst[b][:, 0:ncol], src[j * 128:(j + 1) * 128, :], deps=[wfree[b]])
                t2 = g.op("dve", lambda h, dst=dst, j=j, b=b, ncol=ncol, gs=gs: h.tensor_scalar(
                    dst[:, j, :], wst[b][:, 0:ncol], gs[:, j:j + 1], None, ALU.mult), deps=[t1, gt])
                wfree[b] = t2
                wtok[(name, j)] = t2
        wall = wl + list(wtok.values())

        import os
        DBG = int(os.environ.get("DBG_A", "99"))
        stg_free = [None] * NST
        stg_i = [0]

        cur_stage = [0]

        def stage():
            i = stg_i[0] % NST
            stg_i[0] += 1
            cur_stage[0] = i
            return i

        xin_free = [None, None]
        xb_free = [None, None]
        xT_free = [None, None]
        cs_free = [None, None]
        craw_free = None
        blk_i = 0
        for t in range(NT if DBG >= 1 else 0):
            tb = t % 2
            tok0 = t * 512
            tcs = g.dma("sp", "a_cs%d" % tb, cs[tb][:, 0, :], w["cos"][:, tok0:tok0 + 512], deps=[cs_free[tb]])
            tcs2 = g.dma("sp", "a_cs%d" % tb, cs[tb][:, 1, :], w["sin"][:, tok0:tok0 + 512])
            xT_ready = []
            for blk in range(4):
                b = blk_i % 2
                blk_i += 1
                r0 = tok0 + blk * 128
                tl = g.dma("sp", "a_xin%d" % b, xin[b][:, :], x_d[r0:r0 + 128, :], deps=[xin_free[b], xdeps])
                tc = g.op("act", lambda h, b=b: h.activation(out=xb[b][:, :], in_=xin[b][:, :], func=AF.Copy),
                          deps=[tl, xb_free[b]])
                xin_free[b] = tc
                tp = None
                for fc in range(8):
                    tp = g.op("pe", lambda h, b=b, fc=fc: h.transpose(
                        P.tb[:, fc * 128:(fc + 1) * 128], xb[b][:, fc * 128:(fc + 1) * 128], ident[:, :]),
                        deps=[tc, P.tbfree] if fc == 0 else (), sig=(fc == 7))
                xb_free[b] = tp
                te = g.op("dve", lambda h, tb=tb, blk=blk: h.tensor_copy(
                    out=xT[tb][:, :, blk * 128:(blk + 1) * 128],
                    in_=P.tb[:, :].rearrange("p (c n) -> p c n", c=8)), deps=[tp, xT_free[tb]])
                P.tbfree = te
                xT_ready.append(te)
            xTt = xT[tb]
            if DBG < 2:
                continue

            def fm_chunk(c0, ncols, deps):
                k = P.get()
                tok = mm_group(g, P.banks[k][0:ncols, :],
                               [(win[:, fc, c0:c0 + ncols], xTt[:, fc, :]) for fc in range(8)],
                               deps=[deps, P.free[k], wl])
                return k, tok

            sq_toks = []
            raw_toks = []
            for j in range(5):
                k, tok = fm_chunk(j * 128, 128, xT_ready)
                t2 = g.op("dve", lambda h, k=k, j=j: h.tensor_copy(out=craw[:, j, :], in_=P.banks[k][:, :]),
                          deps=[tok, craw_free])
                t1 = g.op("act", lambda h, j=j: h.activation(out=csq[:, j, :], in_=craw[:, j, :], func=AF.Square),
                          deps=[t2, craw_free])
                P.rel(k, t2)
                sq_toks.append(t1)
                raw_toks.append(t2)
            if DBG < 3:
                continue
            rtoks = []
            for (which, js, n) in ((0, (0, 1, 2), 384.0), (1, (3, 4), 256.0)):
                k = P.get()
                tok = mm_group(g, P.banks[k][:, :], [(ones[:, :], csq[:, j, :]) for j in js],
                               deps=[[sq_toks[j] for j in js], P.free[k]])
                t1 = g.op("act", lambda h, k=k, which=which, n=n: h.activation(
                    out=rbc[:, which, :], in_=P.banks[k][:, :], func=AF.Ln, bias=1e-6, scale=1.0 / n), deps=[tok, craw_free])
                P.rel(k, t1)
                t2 = g.op("act", lambda h, which=which: h.activation(
                    out=rbc[:, which, :], in_=rbc[:, which, :], func=AF.Exp, scale=-0.5), deps=[t1])
                rtoks.append(t2)
            cn_toks = []
            for j in range(5):
                which = 0 if j < 3 else 1
                eng = "pool" if j % 2 == 0 else "dve"
                cn_toks.append(g.op(eng, lambda h, j=j, which=which: h.tensor_tensor(
                    cn[:, j, :], craw[:, j, :], rbc[:, which, :], ALU.mult), deps=[raw_toks[j], rtoks[which], craw_free]))
            craw_last = list(cn_toks)

            if DBG < 4:
                continue
            def up_chunk(wt, nk, c0, js0, deps):
                k = P.get()
                tok = mm_group(g, P.banks[k][:, :], [(wt[:, j, c0:c0 + 128], cn[:, js0 + j, :]) for j in range(nk)],
                               deps=[deps, P.free[k], wall])
                return k, tok

            def store(src_ap, dst_ap, dep):
                return g.dma("sp", "a_out%d" % cur_stage[0], dst_ap, src_ap, deps=[dep])

            def evac_store(k, tok, dst_ap, eng, scale=None, rows=128):
                i = stage()
                if eng == "act":
                    te = g.op("act", lambda h, k=k, i=i: h.activation(
                        out=stg[i][0:rows, :], in_=P.banks[k][0:rows, :], func=AF.Copy,
                        scale=(1.0 if scale is None else scale)), deps=[tok, stg_free[i]])
                else:
                    te = g.op("dve", lambda h, k=k, i=i: h.tensor_copy(out=stg[i][0:rows, :], in_=P.banks[k][0:rows, :]),
                              deps=[tok, stg_free[i]])
                P.rel(k, te)
                ts = store(stg[i][0:rows, :], dst_ap, te)
                stg_free[i] = ts
                outtoks.append(ts)
                return ts

            for hh in range(4):
                k, tok = up_chunk(wuq, 3, hh * 128, 0, cn_toks[0:3])
                craw_last.append(tok)
                evac_store(k, tok, o["qnT"][hh, :, tok0:tok0 + 512], "act" if hh % 2 == 0 else "dve")

            def rope(kA, tokA, kB, tokB, rows, dsts):
                t1 = g.op("dve", lambda h, tb=tb: h.tensor_tensor(rt[0][0:rows, :], P.banks[kA][0:rows, :], cs[tb][0:rows, 0, :], ALU.mult),
                          deps=[tokA, tcs, tcs2, stg_free_rt[0]])
                t2 = g.op("dve", lambda h, tb=tb: h.tensor_tensor(rt[1][0:rows, :], P.banks[kB][0:rows, :], cs[tb][0:rows, 1, :], ALU.mult),
                          deps=[tokB, stg_free_rt[0]])
                P.rel(kA, t1)
                P.rel(kB, t2)
                i = stage()
                t3 = g.op("pool", lambda h, i=i: h.tensor_tensor(stg[i][0:rows, :], rt[0][0:rows, :], rt[1][0:rows, :], ALU.add),
                          deps=[t1, t2, stg_free[i]])
                stg_free_rt[0] = t3
                last = None
                for (r0, r1, dst) in dsts:
                    last = store(stg[i][r0:r1, :], dst, t3)
                    outtoks.append(last)
                stg_free[i] = last
                return t3

            stg_free_rt = [None]
            for rc in range(2):
                kA, tokA = up_chunk(wuq, 3, 512 + rc * 128, 0, cn_toks[0:3])
                kB, tokB = up_chunk(wuq, 3, 768 + rc * 128, 0, cn_toks[0:3])
                craw_last += [tokA, tokB]
                rope(kA, tokA, kB, tokB, 128, [(0, 64, o["qrT"][2 * rc, :, tok0:tok0 + 512]),
                                                (64, 128, o["qrT"][2 * rc + 1, :, tok0:tok0 + 512])])
            kA, tokA = fm_chunk(640, 64, xT_ready)
            kB, tokB = fm_chunk(704, 64, xT_ready)
            trope = rope(kA, tokA, kB, tokB, 64, [(0, 64, o["krT"][:, tok0:tok0 + 512])])
            cs_free[tb] = trope
            for hh in range(4):
                k, tok = up_chunk(wuk, 2, hh * 128, 3, cn_toks[3:5])
                craw_last.append(tok)
                evac_store(k, tok, o["knT"][hh, :, tok0:tok0 + 512], "act" if hh % 2 == 0 else "dve")
            for blk in range(4):
                k = P.get()
                tok = mm_group(g, P.banks[k][:, :], [(cn[:, 3 + j, blk * 128:(blk + 1) * 128], wuv[:, j, :]) for j in range(2)],
                               deps=[cn_toks[3:5], P.free[k], wall])
                craw_last.append(tok)
                evac_store(k, tok, o["vm"][tok0 + blk * 128: tok0 + (blk + 1) * 128, :], "act" if blk % 2 == 0 else "dve")
            craw_free = craw_last
            for hh in range(4):
                k, tok = fm_chunk(768 + hh * 128, 128, xT_ready)
                evac_store(k, tok, o["dqT"][hh, :, tok0:tok0 + 512], "act", scale=0.125)
            for hh in range(4):
                k, tok = fm_chunk(1280 + hh * 128, 128, xT_ready)
                evac_store(k, tok, o["dkT"][hh, :, tok0:tok0 + 512], "dve" if hh % 2 == 0 else "act")
            last_x = None
            for blk in range(4):
                k = P.get()
                tok = mm_group(g, P.banks[k][:, :], [(xTt[:, fc, blk * 128:(blk + 1) * 128], win[:, fc, 1792:2304]) for fc in range(8)],
                               deps=[xT_ready, P.free[k], wl])
                last_x = tok
                evac_store(k, tok, o["dv"][tok0 + blk * 128: tok0 + (blk + 1) * 128, :], "act" if blk % 2 == 0 else "dve")
            xT_free[tb] = last_x
        g.barrier()
        g.flush("phA")
    return outtoks


def phase_B(g, nc, P, S, NM, ND, i_, mT_d, cst, lam_init, indeps=()):
    NKB = S // 128
    NG = S // 512
    mla_scale = (128 + 64) ** -0.5
    outtoks = []
    with ExitStack() as es:
        sb = lambda n, s, d: es.enter_context(nc.sbuf_tensor(_uname(n), s, d))
        kT = [sb("b_kT%d" % i, [128, S], BF16) for i in range(2)]
        qT = [sb("b_qT%d" % i, [128, S], BF16) for i in range(2)]
        vv = [sb("b_v%d" % i, [128, NKB, 132], BF16) for i in range(2)]
        krT = sb("b_krT", [64, S], BF16)
        qrT = [sb("b_qrT%d" % i, [64, S], BF16) for i in range(2)]
        bias_f = sb("b_biasf", [128, 5, 512], F32)
        bias_t = sb("b_biast", [128, 512], F32)
        bias_hl = sb("b_biashl", [128, 2, 5, 512], BF16)
        cfar = sb("b_cfar", [128, max(ND, 1)], F32)
        lamp = sb("b_lamp", [128, 256], F32)
        lamt = sb("b_lamt", [128, 8], F32)
        dnorm = sb("b_dnorm", [128, 128], F32)
        NPT = 4
        pt = [sb("b_pt%d" % i, [128, 512], BF16) for i in range(NPT)]
        osb = [sb("b_osb%d" % i, [128, 4, 128], F32) for i in range(2)]
        rs = sb("b_rs", [128, 32], F32)
        obf = sb("b_obf", [128, 4, 128], BF16)
        osq = sb("b_osq", [128, 128], F32)
        mo = [sb("b_mo%d" % i, [128, 512], BF16) for i in range(2)]
        ones = cst["ones"]
        ident = cst["ident"]
        tri = cst["tri"]
        zero_c = cst["zero_c"]

        g.barrier()
        vinit = [g.op("pool", lambda h, i=i: h.memset(vv[i][:, :, 128:132], 1.0)) for i in range(2)]
        tkr = g.dma("sp", "b_kr", krT[:, :], i_["krT"], deps=[indeps]) if NM > 0 else None
        tlam = None
        if ND > 0:
            t0 = g.dma("sp", "b_small", lamp[:, :], i_["lamp"])
            t0b = g.dma("sp", "b_small2", dnorm[:, :], i_["dnorm"])
            t0c = g.dma("sp", "b_small3", cfar[:, :], i_["cfar"])
            t1 = g.op("dve", lambda h: h.tensor_tensor(lamp[:, 0:64], lamp[:, 0:64], lamp[:, 64:128], ALU.mult), deps=[t0])
            t2 = g.op("dve", lambda h: h.tensor_tensor(lamp[:, 128:192], lamp[:, 128:192], lamp[:, 192:256], ALU.mult), deps=[t1])
            t3 = g.op("dve", lambda h: h.tensor_reduce(lamt[:, 0:1], lamp[:, 0:64], AX.X, ALU.add), deps=[t2])
            t4 = g.op("dve", lambda h: h.tensor_reduce(lamt[:, 1:2], lamp[:, 128:192], AX.X, ALU.add), deps=[t3])
            t5 = g.op("act", lambda h: h.activation(out=lamt[:, 2:4], in_=lamt[:, 0:2], func=AF.Exp), deps=[t4])
            t6 = g.op("dve", lambda h: h.tensor_tensor(lamt[:, 4:5], lamt[:, 2:3], lamt[:, 3:4], ALU.subtract), deps=[t5])
            t7 = g.op("dve", lambda h: h.tensor_scalar(lamt[:, 5:6], lamt[:, 4:5], -1.0, -float(lam_init), ALU.mult, ALU.add), deps=[t6])
            t8 = g.op("dve", lambda h: h.tensor_scalar(dnorm[:, :], dnorm[:, :], float(1.0 - lam_init), None, ALU.mult), deps=[t0b, t7])
            tlam = [t7, t8, t0c]

        kfree = [None, None]
        qfree = [None, None]
        vfree = [vinit[0], vinit[1]]
        qrfree = [None, None]
        ptfree = [None] * NPT
        pt_i = [0]
        osb_free = [None, None]
        obf_free = [None]
        mo_free = [None, None]
        mo_i = [0]
        bias_free = [None]
        SBANKS = [0, 1, 2]
        lbu = [None]
        OBANKS = [3, 4, 5, 6]
        s_rr = [0]

        def oacc(qb):
            return P.banks[OBANKS[qb]][:, 0:129]

        heads = [("m", h) for h in range(NM)] + [("d", h) for h in range(ND)]
        for hi, (kind, hh) in enumerate(heads):
            hb = hi % 2
            if kind == "m":
                tk = g.dma("sp", "b_k%d" % hb, kT[hb][:, :], i_["knT"][hh, :, :], deps=[kfree[hb], indeps])
                tq = g.dma("sp", "b_q%d" % hb, qT[hb][:, :], i_["qnT"][hh, :, :], deps=[qfree[hb]])
                tqr = g.dma("sp", "b_qr%d" % hb, qrT[hb][:, :], i_["qrT"][hh, :, :], deps=[qrfree[hb]])
                tv = g.dma("sp", "b_v%d" % hb, vv[hb][:, :, 0:128],
                           i_["vm"][:, hh * 128:(hh + 1) * 128].rearrange("(kb p) d -> p kb d", p=128), deps=[vfree[hb]])
                ld = [tk, tq, tqr, tv, tkr]
                comps = [0]
            else:
                tk = g.dma("sp", "b_k%d" % hb, kT[hb][:, :], i_["dkT"][hh, :, :], deps=[kfree[hb], indeps])
                tq = g.dma("sp", "b_q%d" % hb, qT[hb][:, :], i_["dqT"][hh, :, :], deps=[qfree[hb]])
                tv = g.dma("sp", "b_v%d" % hb, vv[hb][:, :, 0:128],
                           i_["dv"][:, hh * 128:(hh + 1) * 128].rearrange("(kb p) d -> p kb d", p=128), deps=[vfree[hb]])
                tb0 = g.dma("sp", "b_bias", bias_f[:, :, :], i_["bias"][hh].rearrange("r p q -> p r q"), deps=[bias_free[0]])
                tb1 = g.op("dve", lambda h: h.tensor_copy(out=bias_hl[:, 0, :, :], in_=bias_f[:, :, :]), deps=[tb0, bias_free[0]])
                tbl = tb1
                for r in range(5):
                    ta = g.op("dve", lambda h, r=r: h.tensor_tensor(bias_t[:, :], bias_f[:, r, :], bias_hl[:, 0, r, :], ALU.subtract), deps=[tbl])
                    tbl = g.op("dve", lambda h, r=r: h.tensor_copy(out=bias_hl[:, 1, r, :], in_=bias_t[:, :]), deps=[ta])
                ld = [tk, tq, tv, tbl, tlam]
                comps = [0, 1]
            last_pe = None
            last_bias_use = None
            for G in range(NG):
                q0 = G * 512
                comp_done = []
                for c in comps:
                    nkb = 4 * G + 4
                    pv_last = [None] * 4
                    def issue_S(kb, c=c, G=G, q0=q0):
                        r = kb - 4 * G
                        qlo = 128 * r if r > 0 else 0
                        ncol = 512 - qlo
                        sk = SBANKS[s_rr[0] % len(SBANKS)]
                        s_rr[0] += 1
                        S_ps = P.banks[sk][:, 0:ncol]
                        pairs = []
                        if kind == "m":
                            pairs.append((kT[hb][:, kb * 128:(kb + 1) * 128], qT[hb][:, q0 + qlo:q0 + 512]))
                            pairs.append((krT[:, kb * 128:(kb + 1) * 128], qrT[hb][:, q0 + qlo:q0 + 512]))
                        else:
                            pairs.append((kT[hb][c * 64:(c + 1) * 64, kb * 128:(kb + 1) * 128],
                                          qT[hb][c * 64:(c + 1) * 64, q0 + qlo:q0 + 512]))
                            if r >= -1:
                                pairs.append((ident[:, :], bias_hl[:, 0, r + 1, qlo:512]))
                                pairs.append((ident[:, :], bias_hl[:, 1, r + 1, qlo:512]))
                        tS = mm_group(g, S_ps, pairs, deps=[ld, P.free[sk]])
                        if kind == "d" and r >= -1:
                            lbu[0] = tS
                        pi = pt_i[0] % NPT
                        pt_i[0] += 1
                        if kind == "m":
                            bias_arg = zero_c[:, 0:1]
                            sc = mla_scale
                        else:
                            bias_arg = cfar[:, hh:hh + 1] if r < -1 else zero_c[:, 0:1]
                            sc = 1.0
                        tE = g.op("act", lambda h, pi=pi, S_ps=S_ps, ncol=ncol, bias_arg=bias_arg, sc=sc: h.activation(
                            out=pt[pi][:, 0:ncol], in_=S_ps, func=AF.Exp, bias=bias_arg, scale=sc),
                            deps=[tS, ptfree[pi], tlam])
                        P.rel(sk, tE)
                        if r >= 0:
                            tE = g.op("pool", lambda h, pi=pi: h.tensor_tensor(pt[pi][:, 0:128], pt[pi][:, 0:128], tri[:, :], ALU.mult),
                                      deps=[tE])
                        return (r, qlo, pi, tE)

                    def issue_PV(kb, rec, G=G):
                        r, qlo, pi, tE = rec
                        qb0 = r if r > 0 else 0
                        tpv = None
                        for qb in range(qb0, 4):
                            lastkb = 4 * G + qb
                            dep = [tE]
                            if kb == 0:
                                dep.append(P.free[OBANKS[qb]])
                            tpv = g.op("pe", lambda h, qb=qb, pi=pi, kb=kb, qlo=qlo, lastkb=lastkb, hb=hb: h.matmul(
                                oacc(qb), pt[pi][:, qb * 128 - qlo:(qb + 1) * 128 - qlo], vv[hb][:, kb, 0:129],
                                start=(kb == 0), stop=(kb == lastkb)), deps=dep if qb == qb0 else ([] if kb > 0 else dep),
                                sig=(qb == 3 or kb == lastkb))
                            if kb == lastkb:
                                pv_last[qb] = tpv
                        ptfree[pi] = tpv
                        return tpv

                    LA = 2
                    recs = {}
                    for kb in range(min(LA, nkb)):
                        recs[kb] = issue_S(kb)
                    for kb in range(nkb):
                        if kb + LA < nkb:
                            recs[kb + LA] = issue_S(kb + LA)
                        last_pe = issue_PV(kb, recs.pop(kb))
                    if lbu[0] is not None:
                        last_bias_use = lbu[0]
                    slot = c
                    tr = None
                    tn = []
                    for qb in range(4):
                        bk = OBANKS[qb]
                        col = 0
                        ci = (G * 2 + c) % 4 * 4 + qb
                        tr = g.op("dve", lambda h, bk=bk, col=col, ci=ci: h.reciprocal(rs[:, ci:ci + 1], P.banks[bk][:, col + 128:col + 129]),
                                  deps=[pv_last[qb]])
                        t_ = g.op("dve", lambda h, bk=bk, col=col, ci=ci, qb=qb, slot=slot: h.tensor_scalar(
                            osb[slot][:, qb, :], P.banks[bk][:, col:col + 128], rs[:, ci:ci + 1], None, ALU.mult),
                            deps=[tr, osb_free[slot]])
                        tn.append(t_)
                        P.rel(bk, t_)
                    comp_done.append(tn)
                if kind == "m":
                    tfin = g.op("act", lambda h: h.activation(out=obf[:, :, :], in_=osb[0][:, :, :], func=AF.Copy),
                                deps=[comp_done[0], obf_free[0]])
                    osb_free[0] = tfin
                    fin = [tfin] * 4
                else:
                    fin = []
                    for qb in range(4):
                        ta = g.op("dve", lambda h, qb=qb: h.scalar_tensor_tensor(
                            osb[0][:, qb, :], osb[1][:, qb, :], lamt[:, 5:6], osb[0][:, qb, :], ALU.mult, ALU.add),
                            deps=[comp_done[0][qb], comp_done[1][qb], tlam])
                        ci = 12 + qb
                        tb0_ = g.op("dve", lambda h, qb=qb: h.tensor_tensor(osq[:, :], osb[0][:, qb, :], osb[0][:, qb, :], ALU.mult), deps=[ta])
                        tb_ = g.op("dve", lambda h, ci=ci: h.tensor_reduce(rs[:, ci + 4:ci + 5], osq[:, :], AX.X, ALU.add), deps=[tb0_])
                        tc_ = g.op("act", lambda h, ci=ci: h.activation(out=rs[:, ci + 4:ci + 5], in_=rs[:, ci + 4:ci + 5], func=AF.Ln, bias=1e-5, scale=1.0 / 128.0), deps=[tb_])
                        td_ = g.op("act", lambda h, ci=ci: h.activation(out=rs[:, ci + 4:ci + 5], in_=rs[:, ci + 4:ci + 5], func=AF.Exp, scale=-0.5), deps=[tc_])
                        te_ = g.op("dve", lambda h, qb=qb, ci=ci: h.scalar_tensor_tensor(
                            obf[:, qb, :], osb[0][:, qb, :], rs[:, ci + 4:ci + 5], dnorm[:, :], ALU.mult, ALU.mult),
                            deps=[td_, obf_free[0]])
                        fin.append(te_)
                    osb_free[0] = fin
                    osb_free[1] = fin
                mi = mo_i[0] % 2
                mo_i[0] += 1
                tp = None
                for qb in range(4):
                    tp = g.op("pe", lambda h, qb=qb: h.transpose(P.tb[:, qb * 128:(qb + 1) * 128], obf[:, qb, :], ident[:, :]),
                              deps=[fin[qb], P.tbfree] if qb == 0 else [fin[qb]], sig=(qb == 3))
                obf_free[0] = tp
                te = g.op("act", lambda h, mi=mi: h.activation(out=mo[mi][:, :], in_=P.tb[:, 0:512], func=AF.Copy), deps=[tp, mo_free[mi]])
                P.tbfree = te
                ts = g.dma("sp", "b_out%d" % mi, mT_d[hi, :, q0:q0 + 512], mo[mi][:, :], deps=[te])
                mo_free[mi] = ts
                outtoks.append(ts)
            kfree[hb] = last_pe
            qfree[hb] = last_pe
            vfree[hb] = last_pe
            if kind == "m":
                qrfree[hb] = last_pe
            else:
                bias_free[0] = last_bias_use
        g.barrier()
        g.flush("phB")
    return outtoks


def phase_B2(g, nc, P, S, NM, ND, i_, mT_d, cst, lam_init, indeps=()):
    NKB = S // 128
    NG = S // 512
    mla_scale = (128 + 64) ** -0.5
    outtoks = []
    with ExitStack() as es:
        sb = lambda n, s, d: es.enter_context(nc.sbuf_tensor(_uname(n), s, d))
        kT = [sb("b_kT%d" % i, [128, S], BF16) for i in range(2)]
        qT = [sb("b_qT%d" % i, [128, S], BF16) for i in range(2)]
        vv = [sb("b_v%d" % i, [128, NKB, 128], BF16) for i in range(2)]
        kz = [[sb("b_kz%d_%d" % (i, c), [128, S], BF16) for c in range(2)] for i in range(2)]
        krT = sb("b_krT", [128, S], BF16)
        qrT = [sb("b_qrT%d" % i, [128, S], BF16) for i in range(2)]
        pacc = [[sb("b_pacc%d_%d" % (i, e), [128, 512], F32) for e in range(2)] for i in range(2)]
        onesf = sb("b_onesf", [128, 128], F32)
        bias_f = sb("b_biasf", [128, 5, 512], F32)
        bias_t = sb("b_biast", [128, 512], F32)
        bias_hl = sb("b_biashl", [128, 2, 5, 512], BF16)
        cfar = sb("b_cfar", [128, max(ND, 1)], F32)
        lamp = sb("b_lamp", [128, 256], F32)
        lamt = sb("b_lamt", [128, 8], F32)
        dncol = sb("b_dncol", [128, 1], F32)
        NPT = 6
        pt = [sb("b_pt%d" % i, [128, 512], BF16) for i in range(NPT)]
        rinv2 = [sb("b_rinv%d" % i, [128, 512], F32) for i in range(2)]
        on = [sb("b_on%d" % i, [128, 512], F32) for i in range(2)]
        osq = sb("b_osq", [128, 512], BF16)
        rr = sb("b_rr", [128, 512], F32)
        mo = [sb("b_mo%d" % i, [128, 512], BF16) for i in range(2)]
        ones = cst["ones"]
        ident = cst["ident"]
        tri = cst["tri"]
        zero_c = cst["zero_c"]

        g.barrier()
        zinit = [g.op("pool", lambda h: h.memset(onesf[:, :], 1.0)),
                 g.op("pool", lambda h: h.memset(krT[64:128, :], 0.0))]
        for i in range(2):
            zinit.append(g.op("pool", lambda h, i=i: h.memset(qrT[i][64:128, :], 0.0)))
            zinit.append(g.op("pool", lambda h, i=i: h.memset(kz[i][0][64:128, :], 0.0)))
            zinit.append(g.op("pool", lambda h, i=i: h.memset(kz[i][1][0:64, :], 0.0)))
        tkr = g.dma("sp", "b_kr", krT[0:64, :], i_["krT"], deps=[indeps, zinit]) if NM > 0 else None
        tlam = None
        if ND > 0:
            t0 = g.dma("sp", "b_small", lamp[:, :], i_["lamp"])
            t0b = g.dma("sp", "b_small2", dncol[:, :], i_["dnorm"][0:1, :].rearrange("o d -> d o"))
            t0c = g.dma("sp", "b_small3", cfar[:, :], i_["cfar"])
            t1 = g.op("dve", lambda h: h.tensor_tensor(lamp[:, 0:64], lamp[:, 0:64], lamp[:, 64:128], ALU.mult), deps=[t0])
            t2 = g.op("dve", lambda h: h.tensor_tensor(lamp[:, 128:192], lamp[:, 128:192], lamp[:, 192:256], ALU.mult), deps=[t1])
            t3 = g.op("dve", lambda h: h.tensor_reduce(lamt[:, 0:1], lamp[:, 0:64], AX.X, ALU.add), deps=[t2])
            t4 = g.op("dve", lambda h: h.tensor_reduce(lamt[:, 1:2], lamp[:, 128:192], AX.X, ALU.add), deps=[t3])
            t5 = g.op("act", lambda h: h.activation(out=lamt[:, 2:4], in_=lamt[:, 0:2], func=AF.Exp), deps=[t4])
            t6 = g.op("dve", lambda h: h.tensor_tensor(lamt[:, 4:5], lamt[:, 2:3], lamt[:, 3:4], ALU.subtract), deps=[t5])
            t7 = g.op("dve", lambda h: h.tensor_scalar(lamt[:, 5:6], lamt[:, 4:5], -1.0, -float(lam_init), ALU.mult, ALU.add), deps=[t6])
            t8 = g.op("dve", lambda h: h.tensor_scalar(dncol[:, :], dncol[:, :], float(1.0 - lam_init), None, ALU.mult), deps=[t0b, t7])
            tlam = [t7, t8, t0c]

        kfree = [None, None]
        qfree = [None, None]
        vfree = [None, None]
        qrfree = [None, None]
        ptfree = [None] * NPT
        pt_i = [0]
        mo_free = [None, None]
        mo_i = [0]
        bias_free = [None]
        lbu = [None]
        SBANKS = [0, 1, 2, 3]
        OTB, RSB = [4, 6], [5, 5]
        grp_i = [0]
        s_rr = [0]
        rinv_free = [None, None]
        on_free = [None, None]
        osq_free = [None]
        rr_free = [None]
        acc_i = [0]
        pacc_free = [None, None]
        lastA = [None, None]

        heads = [("m", h) for h in range(NM)] + [("d", h) for h in range(ND)]
        for hi, (kind, hh) in enumerate(heads):
            hb = hi % 2
            if kind == "m":
                tk = g.dma("sp", "b_k%d" % hb, kT[hb][:, :], i_["knT"][hh, :, :], deps=[kfree[hb], indeps])
                tq = g.dma("sp", "b_q%d" % hb, qT[hb][:, :], i_["qnT"][hh, :, :], deps=[qfree[hb]])
                tqr = g.dma("sp", "b_qr%d" % hb, qrT[hb][0:64, :], i_["qrT"][hh, :, :], deps=[qrfree[hb], zinit])
                tv = g.dma("sp", "b_v%d" % hb, vv[hb][:, :, :],
                           i_["vm"][:, hh * 128:(hh + 1) * 128].rearrange("(kb p) d -> p kb d", p=128), deps=[vfree[hb]])
                ld = [tk, tq, tqr, tv, tkr]
                comps = [0]
            else:
                tk0 = g.dma("sp", "b_k%d" % hb, kz[hb][0][0:64, :], i_["dkT"][hh, 0:64, :], deps=[kfree[hb], indeps, zinit])
                tk = g.dma("sp", "b_k%d" % hb, kz[hb][1][64:128, :], i_["dkT"][hh, 64:128, :])
                tq = g.dma("sp", "b_q%d" % hb, qT[hb][:, :], i_["dqT"][hh, :, :], deps=[qfree[hb]])
                tv = g.dma("sp", "b_v%d" % hb, vv[hb][:, :, :],
                           i_["dv"][:, hh * 128:(hh + 1) * 128].rearrange("(kb p) d -> p kb d", p=128), deps=[vfree[hb]])
                tb0 = g.dma("sp", "b_bias", bias_f[:, :, :], i_["bias"][hh].rearrange("r p q -> p r q"), deps=[bias_free[0]])
                tb1 = g.op("dve", lambda h: h.tensor_copy(out=bias_hl[:, 0, :, :], in_=bias_f[:, :, :]), deps=[tb0, bias_free[0]])
                tbl = tb1
                for r in range(5):
                    ta = g.op("dve", lambda h, r=r: h.tensor_tensor(bias_t[:, :], bias_f[:, r, :], bias_hl[:, 0, r, :], ALU.subtract), deps=[tbl])
                    tbl = g.op("dve", lambda h, r=r: h.tensor_copy(out=bias_hl[:, 1, r, :], in_=bias_t[:, :]), deps=[ta])
                ld = [tk0, tk, tq, tv, tbl, tlam]
                comps = [0, 1]
            last_pe = None
            lbu[0] = None
            for G in range(NG):
                q0 = G * 512
                nkb = 4 * G + 4
                on_tok = []
                for c in comps:
                    gi = grp_i[0] % 2
                    grp_i[0] += 1
                    OT, RS = OTB[gi], RSB[gi]
                    rinv = rinv2[gi]

                    def issue_S(kb, c=c, G=G, q0=q0):
                        r = kb - 4 * G
                        qlo = 128 * r if r > 0 else 0
                        ncol = 512 - qlo
                        sk = SBANKS[s_rr[0] % len(SBANKS)]
                        s_rr[0] += 1
                        S_ps = P.banks[sk][:, 0:ncol]
                        pairs = []
                        if kind == "m":
                            pairs.append((kT[hb][:, kb * 128:(kb + 1) * 128], qT[hb][:, q0 + qlo:q0 + 512]))
                            pairs.append((krT[:, kb * 128:(kb + 1) * 128], qrT[hb][:, q0 + qlo:q0 + 512]))
                        else:
                            pairs.append((kz[hb][c][:, kb * 128:(kb + 1) * 128], qT[hb][:, q0 + qlo:q0 + 512]))
                            if r >= -1:
                                pairs.append((ident[:, :], bias_hl[:, 0, r + 1, qlo:512]))
                                pairs.append((ident[:, :], bias_hl[:, 1, r + 1, qlo:512]))
                        tS = mm_group(g, S_ps, pairs, deps=[ld, P.free[sk]])
                        if kind == "d" and r >= -1:
                            lbu[0] = tS
                        pi = pt_i[0] % NPT
                        pt_i[0] += 1
                        if kind == "m":
                            bias_arg = zero_c[:, 0:1]
                            sc = mla_scale
                        else:
                            bias_arg = cfar[:, hh:hh + 1] if r < -1 else zero_c[:, 0:1]
                            sc = 1.0
                        tE = g.op("act", lambda h, pi=pi, S_ps=S_ps, ncol=ncol, bias_arg=bias_arg, sc=sc: h.activation(
                            out=pt[pi][:, 0:ncol], in_=S_ps, func=AF.Exp, bias=bias_arg, scale=sc),
                            deps=[tS, ptfree[pi], tlam])
                        P.rel(sk, tE)
                        if r >= 0:
                            tE = g.op("pool", lambda h, pi=pi: h.tensor_tensor(pt[pi][:, 0:128], pt[pi][:, 0:128], tri[:, :], ALU.mult),
                                      deps=[tE])
                        ai = acc_i[0] % 2
                        if kb == 0:
                            tA = g.op("dve", lambda h, pi=pi, ai=ai: h.tensor_copy(out=pacc[ai][0][:, :], in_=pt[pi][:, :]),
                                      deps=[tE, pacc_free[ai]])
                            tZ = g.op("pool", lambda h, ai=ai: h.memset(pacc[ai][1][:, :], 0.0), deps=[pacc_free[ai]])
                            lastA[0] = tA
                            lastA[1] = tZ
                        elif kb % 2 == 0:
                            tA = g.op("dve", lambda h, pi=pi, ai=ai, qlo=qlo, ncol=ncol: h.tensor_tensor(
                                pacc[ai][0][:, qlo:512], pacc[ai][0][:, qlo:512], pt[pi][:, 0:ncol], ALU.add), deps=[tE, lastA[0]])
                            lastA[0] = tA
                        else:
                            tA = g.op("pool", lambda h, pi=pi, ai=ai, qlo=qlo, ncol=ncol: h.tensor_tensor(
                                pacc[ai][1][:, qlo:512], pacc[ai][1][:, qlo:512], pt[pi][:, 0:ncol], ALU.add), deps=[tE, lastA[1]])
                            lastA[1] = tA
                        return (qlo, ncol, pi, tE, tA)

                    def issue_PV(kb, rec, nkb=nkb, OT=OT):
                        qlo, ncol, pi, tE, tA = rec
                        first = (kb == 0)
                        last = (kb == nkb - 1)
                        t = g.op("pe", lambda h, pi=pi, kb=kb, qlo=qlo, ncol=ncol, hb=hb, first=first, last=last, OT=OT: h.matmul(
                            P.banks[OT][:, qlo:512], vv[hb][:, kb, :], pt[pi][:, 0:ncol], start=first, stop=last),
                            deps=[tE, P.free[OT]] if first else [tE], sig=True)
                        ptfree[pi] = [t, tA]
                        return t

                    recs = {}
                    for kb in range(min(2, nkb)):
                        recs[kb] = issue_S(kb)
                    tpv = None
                    for kb0 in range(0, nkb, 2):
                        for kb in (kb0 + 2, kb0 + 3):
                            if kb < nkb:
                                recs[kb] = issue_S(kb)
                        for kb in (kb0, kb0 + 1):
                            if kb < nkb:
                                tpv = issue_PV(kb, recs.pop(kb))
                    last_pe = tpv
                    ai = acc_i[0] % 2
                    acc_i[0] += 1
                    g.op("pe", lambda h, ai=ai, RS=RS: h.matmul(P.banks[RS][:, :], onesf[:, :], pacc[ai][0][:, :], start=True, stop=False),
                         deps=[lastA[0], lastA[1], P.free[RS]], sig=False)
                    trs = g.op("pe", lambda h, ai=ai, RS=RS: h.matmul(P.banks[RS][:, :], onesf[:, :], pacc[ai][1][:, :], start=False, stop=True))
                    pacc_free[ai] = trs
                    last_pe = trs
                    tr = g.op("dve", lambda h, RS=RS, rinv=rinv: h.reciprocal(rinv[:, :], P.banks[RS][:, :]), deps=[trs, rinv_free[gi]])
                    P.rel(RS, tr)
                    if kind == "m":
                        mi = mo_i[0] % 2
                        mo_i[0] += 1
                        tn = g.op("dve", lambda h, mi=mi, OT=OT, rinv=rinv: h.tensor_tensor(mo[mi][:, :], P.banks[OT][:, :], rinv[:, :], ALU.mult),
                                  deps=[tr, mo_free[mi]])
                        P.rel(OT, tn)
                        rinv_free[gi] = tn
                        ts = g.dma("sp", "b_out%d" % mi, mT_d[hi, :, q0:q0 + 512], mo[mi][:, :], deps=[tn])
                        mo_free[mi] = ts
                        outtoks.append(ts)
                    else:
                        tn = g.op("dve", lambda h, c=c, OT=OT, rinv=rinv: h.tensor_tensor(on[c][:, :], P.banks[OT][:, :], rinv[:, :], ALU.mult),
                                  deps=[tr, on_free[c]])
                        P.rel(OT, tn)
                        rinv_free[gi] = tn
                        on_tok.append(tn)
                if kind == "d":
                    ta = g.op("dve", lambda h: h.scalar_tensor_tensor(on[0][:, :], on[1][:, :], lamt[:, 5:6], on[0][:, :], ALU.mult, ALU.add),
                              deps=[on_tok, tlam])
                    on_free[1] = ta
                    tsq = g.op("pool", lambda h: h.tensor_tensor(osq[:, :], on[0][:, :], on[0][:, :], ALU.mult), deps=[ta, osq_free[0]])
                    XB = RS
                    tmm = g.op("pe", lambda h, XB=XB: h.matmul(P.banks[XB][:, :], ones[:, :], osq[:, :], start=True, stop=True),
                               deps=[tsq, P.free[XB]])
                    osq_free[0] = tmm
                    t1_ = g.op("act", lambda h, XB=XB: h.activation(out=rr[:, :], in_=P.banks[XB][:, :], func=AF.Ln, bias=1e-5, scale=1.0 / 128.0),
                               deps=[tmm, rr_free[0]])
                    P.rel(XB, t1_)
                    t2_ = g.op("act", lambda h: h.activation(out=rr[:, :], in_=rr[:, :], func=AF.Exp, scale=-0.5), deps=[t1_])
                    mi = mo_i[0] % 2
                    mo_i[0] += 1
                    tf = g.op("dve", lambda h, mi=mi: h.scalar_tensor_tensor(mo[mi][:, :], on[0][:, :], dncol[:, 0:1], rr[:, :], ALU.mult, ALU.mult),
                              deps=[t2_, ta, mo_free[mi], tlam])
                    rr_free[0] = tf
                    on_free[0] = tf
                    ts = g.dma("sp", "b_out%d" % mi, mT_d[hi, :, q0:q0 + 512], mo[mi][:, :], deps=[tf])
                    mo_free[mi] = ts
                    outtoks.append(ts)
            kfree[hb] = last_pe
            qfree[hb] = last_pe
            vfree[hb] = last_pe
            if kind == "m":
                qrfree[hb] = last_pe
            else:
                bias_free[0] = lbu[0]
        g.barrier()
        g.flush("phB")
    return outtoks


def ln_block(g, st, A, lnp, gi, bi, dep, tln):
    t1 = g.op("dve", lambda h: h.bn_stats(out=st[:, 0:6], in_=A(0, 512)), deps=[dep])
    t2 = g.op("dve", lambda h: h.bn_stats(out=st[:, 6:12], in_=A(512, 1024)), deps=[dep])
    t3 = g.op("dve", lambda h: h.bn_aggr(out=st[:, 12:14], in_=st[:, 0:12]), deps=[t1, t2])
    t4a = g.op("act", lambda h: h.activation(out=st[:, 14:15], in_=st[:, 13:14], func=AF.Ln, bias=1e-5, scale=1.0), deps=[t3])
    t4 = g.op("act", lambda h: h.activation(out=st[:, 14:15], in_=st[:, 14:15], func=AF.Exp, scale=-0.5), deps=[t4a])
    t5 = g.op("dve", lambda h: h.tensor_scalar(A(0, 1024), A(0, 1024), st[:, 12:13], st[:, 14:15], ALU.subtract, ALU.mult), deps=[t4])
    t6 = g.op("dve", lambda h: h.tensor_tensor(A(0, 1024), A(0, 1024), lnp[:, gi, :], ALU.mult), deps=[t5, tln])
    t7 = g.op("dve", lambda h: h.tensor_tensor(A(0, 1024), A(0, 1024), lnp[:, bi, :], ALU.add), deps=[t6])
    return t7


def gates_block(g, lg, dep):
    L = lg[:, 0:8]
    m1, m2 = lg[:, 8:9], lg[:, 9:10]
    k1, k2 = lg[:, 16:24], lg[:, 24:32]
    L2 = lg[:, 32:40]
    e, g1, g2 = lg[:, 10:11], lg[:, 11:12], lg[:, 12:13]
    t = g.op("dve", lambda h: h.tensor_reduce(m1, L, AX.X, ALU.max), deps=[dep])
    t = g.op("dve", lambda h: h.tensor_scalar(k1, L, m1, None, ALU.is_equal), deps=[t])
    t = g.op("dve", lambda h: h.scalar_tensor_tensor(L2, k1, -1e30, L, ALU.mult, ALU.add), deps=[t])
    t = g.op("dve", lambda h: h.tensor_reduce(m2, L2, AX.X, ALU.max), deps=[t])
    t = g.op("dve", lambda h: h.tensor_scalar(k2, L2, m2, None, ALU.is_equal), deps=[t])
    t = g.op("dve", lambda h: h.tensor_tensor(e, m2, m1, ALU.subtract), deps=[t])
    t = g.op("act", lambda h: h.activation(out=e, in_=e, func=AF.Exp), deps=[t])
    t = g.op("dve", lambda h: h.tensor_scalar(g1, e, 1.0, None, ALU.add), deps=[t])
    t = g.op("dve", lambda h: h.reciprocal(g1, g1), deps=[t])
    t = g.op("dve", lambda h: h.tensor_tensor(g2, e, g1, ALU.mult), deps=[t])
    t = g.op("dve", lambda h: h.tensor_scalar(k1, k1, g1, None, ALU.mult), deps=[t])
    t = g.op("dve", lambda h: h.scalar_tensor_tensor(lg[:, 56:64], k2, g2, k1, ALU.mult, ALU.add), deps=[t])
    return t


def phase_C(g, nc, P, TOK, moe, x_d, mT_d, w, xo_d, cst, indeps=()):
    CT = min(TOK, 2048)
    NCH = TOK // CT
    NB = CT // 128
    NTT = CT // 512
    F = FE if moe else FD
    nfc_all = F // 128
    FT = 4
    outtoks = []
    ident = cst["ident"]
    with ExitStack() as es:
        sb = lambda n, s, d: es.enter_context(nc.sbuf_tensor(_uname(n), s, d))
        yacc = sb("c_yacc", [128, NB, 1024], F32)
        x1T = sb("c_x1T", [128, 8, CT], BF16)
        lnp = sb("c_lnp", [128, 2, 1024], F32)
        st = sb("c_st", [128, 16], F32)
        sg = [sb("c_sg%d" % i, [128, 512], F32) for i in range(2)]
        tt_ = [sb("c_tt%d" % i, [128, 512], F32) for i in range(2)]
        if moe:
            gT = sb("c_gT", [8, CT], F32)
            gbc = sb("c_gbc", [128, CT], F32)
            sel = cst["sel"]
            identf = cst["identf"]
        g.barrier()
        for ch in range(NCH):
            c0 = ch * CT
            with ExitStack() as es1:
                sb1 = lambda n, s, d: es1.enter_context(nc.sbuf_tensor(_uname(n), s, d))
                wo = sb1("c_wo", [128, 8, 1024], BF16)
                xin = [sb1("c_xin%d" % i, [128, 1024], F32) for i in range(2)]
                xbf = [sb1("c_xbf%d" % i, [128, 1024], BF16) for i in range(2)]
                mTs = [sb1("c_mT%d" % i, [128, 8, 128], BF16) for i in range(2)]
                if moe:
                    rw = sb1("c_rw", [128, 8, 8], F32)
                    x1Tf = [sb1("c_x1Tf%d" % i, [128, 8, 128], F32) for i in range(2)]
                    lg = sb1("c_lg", [128, 64], F32)
                g.barrier()
                two = [g.dma("pool", "c_wo", wo[:, fc, :], w["wo"][fc * 128:(fc + 1) * 128, :]) for fc in range(8)]
                tln = g.dma("sp", "c_ln", lnp[:, :, :], w["ln"][:, 0:2, :])
                trw = g.dma("sp", "c_rw", rw[:, :, :], w["router"]) if moe else None
                xin_free = [None, None]
                xbf_free = [None, None]
                mT_free = [None, None]
                x1Tf_free = [None, None]
                gate_ready = None
                for blk in range(NB):
                    b = blk % 2
                    r0 = c0 + blk * 128
                    tlx = g.dma("sp", "c_xin%d" % b, xin[b][:, :], x_d[r0:r0 + 128, :], deps=[xin_free[b], indeps])
                    tlm = g.dma("sp", "c_mT%d" % b, mTs[b][:, :, :], mT_d[:, :, r0:r0 + 128].rearrange("c p t -> p c t"),
                                deps=[mT_free[b], indeps])
                    ks = []
                    for half in range(2):
                        k = P.get()
                        tok = mm_group(g, P.banks[k][:, :], [(mTs[b][:, hc, :], wo[:, hc, half * 512:(half + 1) * 512]) for hc in range(8)],
                                       deps=[tlm, two, P.free[k]])
                        ks.append((k, tok))
                    mT_free[b] = ks[1][1]
                    ty = None
                    for half in range(2):
                        k, tok = ks[half]
                        ty = g.op("dve", lambda h, b=b, k=k, half=half: h.scalar_tensor_tensor(
                            xin[b][:, half * 512:(half + 1) * 512], xin[b][:, half * 512:(half + 1) * 512], float(DN_ALPHA),
                            P.banks[k][:, :], ALU.mult, ALU.add), deps=[tok, tlx, ty])
                        P.rel(k, ty)
                    tl = ln_block(g, st, (lambda a, c, b=b: xin[b][:, a:c]), lnp, 0, 1, ty, tln)
                    ta = g.op("act", lambda h, b=b, blk=blk: h.activation(out=yacc[:, blk, :], in_=xin[b][:, :], func=AF.Copy, scale=float(DN_ALPHA)),
                              deps=[tl])
                    tb_ = g.op("act", lambda h, b=b: h.activation(out=xbf[b][:, :], in_=xin[b][:, :], func=AF.Copy), deps=[tl, xbf_free[b]])
                    tp = None
                    for fc in range(8):
                        tp = g.op("pe", lambda h, b=b, fc=fc: h.transpose(P.tb[:, fc * 128:(fc + 1) * 128], xbf[b][:, fc * 128:(fc + 1) * 128], ident[:, :]),
                                  deps=[tb_, P.tbfree] if fc == 0 else (), sig=(fc == 7))
                    xbf_free[b] = tp
                    te = g.op("dve", lambda h, blk=blk: h.tensor_copy(out=x1T[:, :, blk * 128:(blk + 1) * 128],
                                                                      in_=P.tb[:, :].rearrange("p (c n) -> p c n", c=8)), deps=[tp])
                    P.tbfree = te
                    xfree = [ta, tb_]
                    if moe:
                        fb = blk % 2
                        parts = []
                        for half in range(2):
                            kk = P.get()
                            tpf = None
                            for j in range(4):
                                fc = half * 4 + j
                                tpf = g.op("pe", lambda h, b=b, fc=fc, j=j, kk=kk: h.transpose(
                                    P.banks[kk][:, j * 128:(j + 1) * 128], xin[b][:, fc * 128:(fc + 1) * 128], identf[:, :]),
                                    deps=[tl, P.free[kk]] if j == 0 else (), sig=(j == 3))
                            tcp = g.op("act", lambda h, fb=fb, half=half, kk=kk: h.activation(
                                out=x1Tf[fb][:, half * 4:(half + 1) * 4, :], in_=P.banks[kk][:, :].rearrange("p (c n) -> p c n", c=4), func=AF.Copy),
                                deps=[tpf, x1Tf_free[fb]])
                            P.rel(kk, tcp)
                            parts.append(tcp)
                            xfree.append(tpf)
                        kk = P.get()
                        tlg = mm_group(g, P.banks[kk][:, 0:8], [(x1Tf[fb][:, fc, :], rw[:, fc, :]) for fc in range(8)],
                                       deps=[parts, trw, P.free[kk]])
                        x1Tf_free[fb] = tlg
                        tlc = g.op("dve", lambda h, kk=kk: h.tensor_copy(out=lg[:, 0:8], in_=P.banks[kk][:, 0:8]), deps=[tlg, gate_ready])
                        P.rel(kk, tlc)
                        tgt = gates_block(g, lg, tlc)
                        kk = P.get()
                        ttp = g.op("pe", lambda h, kk=kk: h.transpose(P.banks[kk][0:8, 0:128], lg[:, 56:64], identf[:, :]), deps=[tgt, P.free[kk]])
                        tgc = g.op("dve", lambda h, kk=kk, blk=blk: h.tensor_copy(out=gT[:, blk * 128:(blk + 1) * 128], in_=P.banks[kk][0:8, 0:128]),
                                   deps=[ttp])
                        P.rel(kk, tgc)
                        gate_ready = tgc
                    xin_free[b] = xfree
                g.barrier()
                g.flush("phC1")
            with ExitStack() as es2:
                sb2 = lambda n, s, d: es2.enter_context(nc.sbuf_tensor(_uname(n), s, d))
                hT = sb2("c_hT", [128, FT, CT], BF16)
                wgu = [sb2("c_wgu%d" % i, [128, 2, 8, FT * 128], BF16) for i in range(2)]
                wd = [sb2("c_wd%d" % i, [128, FT, 1024], BF16) for i in range(2)]
                g.barrier()
                wfree_gu = [None, None]
                wfree_d = [None, None]
                hfree = None
                sgfree = [None, None]
                ttfree = [None, None]
                wt_i = 0
                yl = [None] * NB
                experts = list(range(NE)) if moe else [None]
                gb_free = None
                for e in experts:
                    tgb = None
                    if moe:
                        tgb = []
                        for tt in range(NTT):
                            kk = P.get()
                            tmm = g.op("pe", lambda h, kk=kk, e=e, tt=tt: h.matmul(P.banks[kk][:, :], sel[:, e, :], gT[:, tt * 512:(tt + 1) * 512],
                                                                                  start=True, stop=True), deps=[P.free[kk]])
                            tcp = g.op("act", lambda h, kk=kk, tt=tt: h.activation(out=gbc[:, tt * 512:(tt + 1) * 512], in_=P.banks[kk][:, :], func=AF.Copy),
                                       deps=[tmm, gb_free])
                            P.rel(kk, tcp)
                            tgb.append(tcp)
                    gb_last = []
                    nft = (nfc_all + FT - 1) // FT
                    for ft in range(nft):
                        nf = min(FT, nfc_all - ft * FT)
                        wb = wt_i % 2
                        wt_i += 1
                        f0 = ft * FT * 128
                        src_g = w["wg"][e] if moe else w["wg"]
                        src_u = w["wu"][e] if moe else w["wu"]
                        src_d = w["wd"][e] if moe else w["wd"]
                        twg = g.dma("pool", "c_wg%d" % wb, wgu[wb][:, 0, :, 0:nf * 128],
                                    src_g[:, f0:f0 + nf * 128].rearrange("(fc p) f -> p fc f", p=128), deps=[wfree_gu[wb]])
                        twu = g.dma("pool", "c_wu%d" % wb, wgu[wb][:, 1, :, 0:nf * 128],
                                    src_u[:, f0:f0 + nf * 128].rearrange("(fc p) f -> p fc f", p=128))
                        twd = g.dma("pool", "c_wd%d" % wb, wd[wb][:, 0:nf, :],
                                    src_d[f0:f0 + nf * 128, :].rearrange("(j p) d -> p j d", p=128), deps=[wfree_d[wb]])
                        h_toks = []
                        last_gu = None
                        for j in range(nf):
                            for tt in range(NTT):
                                kg = P.get()
                                tg_ = mm_group(g, P.banks[kg][:, :], [(wgu[wb][:, 0, fc, j * 128:(j + 1) * 128], x1T[:, fc, tt * 512:(tt + 1) * 512]) for fc in range(8)],
                                               deps=[twg, P.free[kg]])
                                ku = P.get()
                                tu_ = mm_group(g, P.banks[ku][:, :], [(wgu[wb][:, 1, fc, j * 128:(j + 1) * 128], x1T[:, fc, tt * 512:(tt + 1) * 512]) for fc in range(8)],
                                               deps=[twu, P.free[ku]])
                                last_gu = tu_
                                si = (j * NTT + tt) % 2
                                ts_ = g.op("act", lambda h, kg=kg, si=si: h.activation(out=sg[si][:, :], in_=P.banks[kg][:, :], func=AF.Silu),
                                           deps=[tg_, sgfree[si]])
                                P.rel(kg, ts_)
                                if moe:
                                    tm_ = g.op("dve", lambda h, ku=ku, si=si: h.tensor_tensor(tt_[si][:, :], sg[si][:, :], P.banks[ku][:, :], ALU.mult),
                                               deps=[ts_, tu_, ttfree[si]])
                                    P.rel(ku, tm_)
                                    sgfree[si] = tm_
                                    th_ = g.op("pool", lambda h, si=si, j=j, tt=tt: h.tensor_tensor(
                                        hT[:, j, tt * 512:(tt + 1) * 512], tt_[si][:, :], gbc[:, tt * 512:(tt + 1) * 512], ALU.mult),
                                        deps=[tm_, tgb, hfree])
                                    ttfree[si] = th_
                                    gb_last.append(th_)
                                else:
                                    th_ = g.op("dve", lambda h, ku=ku, si=si, j=j, tt=tt: h.tensor_tensor(
                                        hT[:, j, tt * 512:(tt + 1) * 512], sg[si][:, :], P.banks[ku][:, :], ALU.mult),
                                        deps=[ts_, tu_, hfree])
                                    P.rel(ku, th_)
                                    sgfree[si] = th_
                                h_toks.append(th_)
                        wfree_gu[wb] = last_gu
                        last_d = None
                        for blk in range(NB):
                            for half in range(2):
                                kk = P.get()
                                td_ = mm_group(g, P.banks[kk][:, :], [(hT[:, j, blk * 128:(blk + 1) * 128], wd[wb][:, j, half * 512:(half + 1) * 512]) for j in range(nf)],
                                               deps=[h_toks, twd, P.free[kk]])
                                last_d = td_
                                tacc = g.op("dve", lambda h, kk=kk, blk=blk, half=half: h.tensor_tensor(
                                    yacc[:, blk, half * 512:(half + 1) * 512], yacc[:, blk, half * 512:(half + 1) * 512], P.banks[kk][:, :], ALU.add),
                                    deps=[td_, yl[blk]])
                                P.rel(kk, tacc)
                                yl[blk] = tacc
                        wfree_d[wb] = last_d
                        hfree = last_d
                    if moe:
                        gb_free = gb_last
                g.barrier()
                g.flush("phC2")
            tln2 = g.dma("sp", "c_ln", lnp[:, :, :], w["ln"][:, 2:4, :])
            for blk in range(NB):
                tl = ln_block(g, st, (lambda a, c, blk=blk: yacc[:, blk, a:c]), lnp, 0, 1, None, tln2)
                ts = g.dma("sp", "c_out%d" % (blk % 4), xo_d[c0 + blk * 128:c0 + (blk + 1) * 128, :], yacc[:, blk, :], deps=[tl])
                outtoks.append(ts)
            g.barrier()
            g.flush("phC3")
    return outtoks


CONST_SPECS = [("c_ident", [128, 128], BF16), ("c_identf", [128, 128], F32), ("c_ones", [128, 128], BF16),
               ("c_tri", [128, 128], BF16), ("c_zero", [128, 1], F32), ("c_sel", [8, NE, 128], F32)]


def host_consts():
    ident = np.eye(128, dtype=np.float32)
    tri = (np.arange(128)[None, :] >= np.arange(128)[:, None]).astype(np.float32)
    sel = np.zeros((8, NE, 128), np.float32)
    for e in range(NE):
        sel[e, e, :] = 1.0
    return {"c_ident": ident.astype(NPBF), "c_identf": ident, "c_ones": np.ones((128, 128), NPBF),
            "c_tri": tri.astype(NPBF), "c_zero": np.zeros((128, 1), np.float32), "c_sel": sel}


def load_consts(g, nc, es):
    cst = {}
    for (name, shape, dt) in CONST_SPECS:
        d = nc.dram_tensor(name, shape, dt, kind="ExternalInput").ap()
        t = es.enter_context(nc.sbuf_tensor("s" + name, shape, dt))
        if len(shape) == 2:
            g.dma("sp", "cst", t[:, :], d)
        else:
            g.dma("sp", "cst", t[:, :, :], d)
        cst[name] = t
    return {"ident": cst["c_ident"], "identf": cst["c_identf"], "ones": cst["c_ones"], "tri": cst["c_tri"],
            "zero_c": cst["c_zero"], "sel": cst["c_sel"]}


A_IN = lambda TOK: [("win", [1024, INW], F32), ("wuq", [384, 1024], F32), ("wuk", [256, 512], F32), ("wuv", [256, 512], F32),
                    ("gq", [128, 3], F32), ("gkv", [128, 2], F32)]
A_OUT = lambda TOK: [("qnT", [4, 128, TOK], BF16), ("qrT", [4, 64, TOK], BF16), ("knT", [4, 128, TOK], BF16), ("krT", [64, TOK], BF16),
                     ("vm", [TOK, 512], BF16), ("dqT", [4, 128, TOK], BF16), ("dkT", [4, 128, TOK], BF16), ("dv", [TOK, 512], BF16)]


def B_IN(S, NM, ND):
    return [("qnT", [NM, 128, S], BF16), ("qrT", [NM, 64, S], BF16), ("knT", [NM, 128, S], BF16), ("krT", [64, S], BF16),
            ("vm", [S, NM * 128], BF16), ("dqT", [ND, 128, S], BF16), ("dkT", [ND, 128, S], BF16), ("dv", [S, ND * 128], BF16)]


B_W = lambda ND: [("bias", [ND, 5, 128, 512], F32), ("cfar", [128, ND], F32), ("lamp", [128, 256], F32), ("dnorm", [128, 128], F32)]


def C_W(moe):
    base = [("wo", [1024, 1024], F32), ("ln", [128, 4, 1024], F32)]
    if moe:
        return base + [("router", [128, 8, 8], F32), ("wg", [NE, 1024, FE], F32), ("wu", [NE, 1024, FE], F32), ("wd", [NE, FE, 1024], F32)]
    return base + [("wg", [1024, FD], F32), ("wu", [1024, FD], F32), ("wd", [FD, 1024], F32)]


PHASE_B = phase_B2


def lam_init_of(l):
    return 0.8 - 0.6 * math.exp(-0.3 * l)


def build(mode, S=4096, TOK=2048, NM=2, ND=2, moe=False, layers=(0,), lam_l=0):
    nc = bass.Bass("TRN2", target_bir_lowering=False)
    ext_in = lambda n, s, d: nc.dram_tensor(n, s, d, kind="ExternalInput").ap()
    ext_out = lambda n, s, d: nc.dram_tensor(n, s, d, kind="ExternalOutput").ap()
    internal = lambda n, s, d: nc.dram_tensor(n, s, d).ap()
    with ExitStack() as es:
        g = Gen(nc, es)
        P = Psum(g, nc, es)
        cst = load_consts(g, nc, es)
        if mode == "A":
            x = ext_in("x", [TOK, 1024], F32)
            w = {n: ext_in(n, s, d) for (n, s, d) in A_IN(TOK)}
            w["cos"] = ext_in("cos", [128, TOK], F32)
            w["sin"] = ext_in("sin", [128, TOK], F32)
            o = {n: ext_out("o_" + n, s, d) for (n, s, d) in A_OUT(TOK)}
            phase_A(g, nc, P, TOK, x, w, o, cst)
        elif mode == "B":
            i_ = {n: ext_in(n, s, d) for (n, s, d) in B_IN(S, NM, ND) + B_W(ND)}
            mT = ext_out("o_mT", [NM + ND, 128, S], BF16)
            PHASE_B(g, nc, P, S, NM, ND, i_, mT, cst, lam_init_of(lam_l))
        elif mode == "C":
            x = ext_in("x", [TOK, 1024], F32)
            mT = ext_in("mT", [8, 128, TOK], BF16)
            w = {n: ext_in(n, s, d) for (n, s, d) in C_W(moe)}
            xo = ext_out("o_x", [TOK, 1024], F32)
            phase_C(g, nc, P, TOK, moe, x, mT, w, xo, cst)
        elif mode == "F":
            x0 = ext_in("x", [S, 1024], F32)
            cos = ext_in("cos", [128, S], F32)
            sin = ext_in("sin", [128, S], F32)
            bias = ext_in("bias", [4, 5, 128, 512], F32)
            cfar = ext_in("cfar", [128, 4], F32)
            xout = ext_out("o_x", [S, 1024], F32)
            xs = [internal("xs0", [S, 1024], F32), internal("xs1", [S, 1024], F32)]
            sc = {n: internal("sc_" + n, s, d) for (n, s, d) in A_OUT(S)}
            mT = internal("sc_mT", [8, 128, S], BF16)
            xin = x0
            for li, l in enumerate(layers):
                moe_l = (l % 2 == 1)
                w = {n: ext_in("L%d_%s" % (l, n), s, d) for (n, s, d) in A_IN(S)}
                w["cos"], w["sin"] = cos, sin
                phase_A(g, nc, P, S, xin, w, sc, cst)
                i_ = dict(sc)
                i_["bias"], i_["cfar"] = bias, cfar
                i_["lamp"] = ext_in("L%d_lamp" % l, [128, 256], F32)
                i_["dnorm"] = ext_in("L%d_dnorm" % l, [128, 128], F32)
                PHASE_B(g, nc, P, S, 4, 4, i_, mT, cst, lam_init_of(l))
                wc = {n: ext_in("L%d_%s" % (l, n), s, d) for (n, s, d) in C_W(moe_l)}
                xo = xout if li == len(layers) - 1 else xs[li % 2]
                phase_C(g, nc, P, S, moe_l, xin, mT, wc, xo, cst)
                xin = xo
        g.barrier()
        g.flush()
    return nc


WIN_PERM = np.concatenate([np.arange(0, 704), np.arange(672, 704), np.arange(640, 672), np.arange(704, 2240)])
_uq = []
for _h in range(4):
    _uq.append(np.arange(_h * 192, _h * 192 + 128))
for _h in range(4):
    _uq.append(np.arange(_h * 192 + 128, _h * 192 + 192))
for _h in range(4):
    _uq.append(np.concatenate([np.arange(_h * 192 + 160, _h * 192 + 192), np.arange(_h * 192 + 128, _h * 192 + 160)]))
WUQ_PERM = np.concatenate(_uq)


def rope_tables(pos):
    inv = (1.0 / (np.float32(10000.0) ** (np.arange(0, 64, 2, dtype=np.float32) / np.float32(64)))).astype(np.float32)
    ang = pos.astype(np.float32)[None, :] * inv[:, None]
    c = np.cos(ang).astype(np.float32)
    s = np.sin(ang).astype(np.float32)
    cos_t = np.concatenate([c, c, c, c], 0)
    sin_t = np.concatenate([-s, s, -s, s], 0)
    return np.ascontiguousarray(cos_t), np.ascontiguousarray(sin_t)


def bucket_index(S):
    n = np.arange(S, dtype=np.int32)
    nf = np.maximum(n, 1).astype(np.float32)
    large = 16 + (np.log(nf / np.float32(16)) / np.float32(math.log(128 / 16)) * np.float32(16)).astype(np.int32)
    large = np.minimum(large, 31)
    return np.where(n < 16, n, large)


def bias_tiles(rel_bias, S):
    bidx = bucket_index(max(S, 1024))
    k = np.arange(128)[:, None]
    q = np.arange(512)[None, :]
    out = np.empty((4, 5, 128, 512), np.float32)
    for r in range(-1, 4):
        dist = np.clip(q - 128 * r - k, 0, len(bidx) - 1)
        out[:, r + 1] = np.transpose(rel_bias[bidx[dist]], (2, 0, 1))
    return out


def layer_A_weights(inp, l):
    return {"win": np.ascontiguousarray(inp["w_in"][l][:, WIN_PERM]),
            "wuq": np.ascontiguousarray(inp["mla_w_uq"][l][:, WUQ_PERM]),
            "wuk": np.ascontiguousarray(inp["mla_w_uk"][l]), "wuv": np.ascontiguousarray(inp["mla_w_uv"][l]),
            "gq": np.ascontiguousarray(inp["mla_q_norm"][l].reshape(3, 128).T),
            "gkv": np.ascontiguousarray(inp["mla_kv_norm"][l].reshape(2, 128).T)}


def layer_B_weights(inp, l):
    return {"lamp": np.ascontiguousarray(np.broadcast_to(inp["diff_lambda"][l].reshape(1, 256), (128, 256))),
            "dnorm": np.ascontiguousarray(np.broadcast_to(inp["diff_norm"][l].reshape(1, 128), (128, 128)))}


def layer_C_weights(inp, l):
    ln = np.stack([inp["ln1_g"][l], inp["ln1_b"][l], inp["ln2_g"][l], inp["ln2_b"][l]], 0)
    w = {"wo": np.ascontiguousarray(inp["w_o"][l]), "ln": np.ascontiguousarray(np.broadcast_to(ln[None], (128, 4, 1024)))}
    if l % 2 == 1:
        m = l // 2
        w["router"] = np.ascontiguousarray(inp["moe_router"][m].reshape(8, 128, 8).transpose(1, 0, 2))
        w["wg"], w["wu"], w["wd"] = inp["moe_w_gate"][m], inp["moe_w_up"][m], inp["moe_w_down"][m]
    else:
        m = l // 2
        w["wg"], w["wu"], w["wd"] = inp["ffn_w_gate"][m], inp["ffn_w_up"][m], inp["ffn_w_down"][m]
    return w


MODE = "fused4"
S_FULL = 4096
_PROG_CACHE = {}


def _prog(layers):
    key = tuple(layers)
    if key not in _PROG_CACHE:
        _PROG_CACHE[key] = build("F", S=S_FULL, layers=key)
    return _PROG_CACHE[key]


def _layer_inputs(inp, l):
    d = {}
    for part in (layer_A_weights(inp, l), layer_B_weights(inp, l), layer_C_weights(inp, l)):
        for k, v in part.items():
            d["L%d_%s" % (l, k)] = np.ascontiguousarray(v, dtype=np.float32)
    return d


def kernel(**inputs):
    inp = {k: np.asarray(v) for k, v in inputs.items()}
    x = np.ascontiguousarray(inp["x"], dtype=np.float32)
    B = x.shape[0]
    shared = dict(host_consts())
    shared["cos"], shared["sin"] = rope_tables(np.arange(S_FULL))
    shared["bias"] = bias_tiles(inp["rel_bias"].astype(np.float32), S_FULL)
    shared["cfar"] = np.ascontiguousarray(np.broadcast_to(inp["rel_bias"][31][None, :], (128, 4)), dtype=np.float32)
    groups = [tuple(range(DEPTH))] if MODE == "fused4" else [(l,) for l in range(DEPTH)]
    cur = [x[b] for b in range(B)]
    for layers in groups:
        nc = _prog(layers)
        base = dict(shared)
        for l in layers:
            base.update(_layer_inputs(inp, l))
        in_maps = []
        for b in range(B):
            m = dict(base)
            m["x"] = np.ascontiguousarray(cur[b])
            in_maps.append(m)
        res = run_bass_kernel_spmd(nc, in_maps, core_ids=list(range(B)))
        cur = [np.asarray(res.results[b]["o_x"], dtype=np.float32) for b in range(B)]
    return np.stack(cur, 0).astype(np.float32)
```

```python
import math
from contextlib import ExitStack

import numpy as np
import ml_dtypes

import concourse.bass as bass
import concourse.mybir as mybir
from concourse.bass_utils import run_bass_kernel_spmd

F32, BF16 = mybir.dt.float32, mybir.dt.bfloat16
AF = mybir.ActivationFunctionType
ALU = mybir.AluOpType
AX = mybir.AxisListType
NPBF = ml_dtypes.bfloat16

D = 1024
DEPTH = 4
NH = 4
INW = 2304
FD = 2816
FE = 3584
NE = 8
DN_ALPHA = (2 * DEPTH) ** 0.25
NEG = -30000.0
ENG = ["pe", "act", "dve", "pool", "sp"]


_UID = [0]


def _uname(n):
    _UID[0] += 1
    return "%s_u%d" % (n, _UID[0])


class Gen:
    def __init__(self, nc, es):
        self.nc = nc
        self.es = es
        self.ops = {e: [] for e in ENG}
        self.cnt = {e: 0 for e in ENG}
        self.seen = {e: {} for e in ENG}
        self.sem = {e: es.enter_context(nc.semaphore("s_" + e)) for e in ENG}
        self.dsem = {}
        self.nops = 0

    def _dsem(self, name):
        if name not in self.dsem:
            self.dsem[name] = [self.es.enter_context(self.nc.semaphore("d_" + name)), 0]
        return self.dsem[name]

    def _wait(self, eng, tok):
        if tok is None:
            return
        if isinstance(tok, (list, tuple)) and (len(tok) == 0 or not isinstance(tok[0], str)):
            for t in tok:
                self._wait(eng, t)
            return
        kind, key, v = tok
        if kind == "e":
            sem = self.sem[key]
            k = "e:" + key
        else:
            sem = self.dsem[key][0]
            k = "d:" + key
        if self.seen[eng].get(k, 0) >= v:
            return
        self.seen[eng][k] = v
        self.ops[eng].append(lambda h, sem=sem, v=v: h.wait_ge(sem, v))

    def _flat(self, tok, out):
        if tok is None:
            return
        if isinstance(tok, (list, tuple)) and (len(tok) == 0 or not isinstance(tok[0], str)):
            for t in tok:
                self._flat(t, out)
            return
        k = (tok[0], tok[1])
        if out.get(k, 0) < tok[2]:
            out[k] = tok[2]

    def _wait_all(self, eng, deps):
        d = {}
        self._flat(deps, d)
        for (kind, key), v in d.items():
            self._wait(eng, (kind, key, v))

    def op(self, eng, fn, deps=(), sig=True):
        self._wait_all(eng, deps)
        self.nops += 1
        if sig:
            self.cnt[eng] += 1
            sem = self.sem[eng]
            self.ops[eng].append(lambda h, fn=fn, sem=sem: fn(h).then_inc(sem, 1))
            return ("e", eng, self.cnt[eng])
        self.ops[eng].append(lambda h, fn=fn: fn(h))
        return None

    def dma(self, eng, name, out, in_, deps=()):
        self._wait_all(eng, deps)
        s = self._dsem(name)
        s[1] += 16
        sem = s[0]
        self.ops[eng].append(lambda h, out=out, in_=in_, sem=sem: h.dma_start(out=out, in_=in_).then_inc(sem, 16))
        return ("d", name, s[1])

    def all_tokens(self):
        toks = [("e", e, self.cnt[e]) for e in ENG if self.cnt[e] > 0]
        toks += [("d", n, s[1]) for n, s in self.dsem.items() if s[1] > 0]
        return toks

    def barrier(self):
        toks = self.all_tokens()
        for e in ENG:
            self._wait(e, toks)

    def flush(self, scope=None):
        nc = self.nc
        if scope is not None:
            with nc.named_scope(_uname(scope)):
                self.flush(None)
            return
        with nc.Block() as block:
            for e, deco in (("pe", block.tensor), ("act", block.scalar), ("dve", block.vector),
                            ("pool", block.gpsimd), ("sp", block.sync)):
                ops = self.ops[e]

                def run(h, ops=ops):
                    for f in ops:
                        f(h)
                deco(run)
        self.ops = {e: [] for e in ENG}


class Psum:
    def __init__(self, g, nc, es):
        self.g = g
        self.banks = [es.enter_context(nc.psum_tensor("ps%d" % i, [128, 512], F32)) for i in range(7)]
        self.tb = es.enter_context(nc.psum_tensor("pstb", [128, 1024], BF16))
        self.free = [None] * 7
        self.tbfree = None
        self.rr = 0

    def get(self, allowed=None):
        allowed = allowed if allowed is not None else list(range(7))
        k = allowed[self.rr % len(allowed)]
        self.rr += 1
        return k

    def rel(self, k, tok):
        self.free[k] = tok


def mm_group(g, out_ap, pairs, deps=(), sig_last=True):
    n = len(pairs)
    tok = None
    for i, (l, r) in enumerate(pairs):
        tok = g.op("pe", lambda h, l=l, r=r, i=i: h.matmul(out_ap, l, r, start=(i == 0), stop=(i == n - 1)),
                   deps=deps if i == 0 else (), sig=(sig_last and i == n - 1))
    return tok


def phase_A(g, nc, P, TOK, x_d, w, o, cst, xdeps=()):
    NT = TOK // 512
    outtoks = []
    with ExitStack() as es:
        sb = lambda n, s, d: es.enter_context(nc.sbuf_tensor(_uname(n), s, d))
        win = sb("a_win", [128, 8, INW], BF16)
        wuq = sb("a_wuq", [128, 3, 1024], BF16)
        wuk = sb("a_wuk", [128, 2, 512], BF16)
        wuv = sb("a_wuv", [128, 2, 512], BF16)
        wst = [sb("a_wst%d" % i, [128, 1024], F32) for i in range(2)]
        gq = sb("a_gq", [128, 3], F32)
        gkv = sb("a_gkv", [128, 2], F32)
        xin = [sb("a_xin%d" % i, [128, 1024], F32) for i in range(2)]
        xb = [sb("a_xb%d" % i, [128, 1024], BF16) for i in range(2)]
        xT = [sb("a_xT%d" % i, [128, 8, 512], BF16) for i in range(2)]
        craw = sb("a_craw", [128, 5, 512], F32)
        csq = sb("a_csq", [128, 5, 512], BF16)
        rbc = sb("a_rbc", [128, 2, 512], F32)
        cn = sb("a_cn", [128, 5, 512], BF16)
        cs = [sb("a_cs%d" % i, [128, 2, 512], F32) for i in range(2)]
        rt = [sb("a_rt%d" % i, [128, 512], F32) for i in range(3)]
        NST = 6
        stg = [sb("a_stg%d" % i, [128, 512], BF16) for i in range(NST)]
        ones = cst["ones"]
        ident = cst["ident"]

        g.barrier()
        wl = []
        for fc in range(8):
            wl.append(g.dma("pool", "a_win", win[:, fc, :], w["win"][fc * 128:(fc + 1) * 128, :]))
        tgq = g.dma("sp", "a_small", gq[:, :], w["gq"])
        tgkv = g.dma("sp", "a_small2", gkv[:, :], w["gkv"])
        wtok = {}
        k = 0
        wfree = [None, None]
        for (name, dst, src, nch, ncol, gt, gs) in (("wuq", wuq, w["wuq"], 3, 1024, tgq, gq),
                                                    ("wuk", wuk, w["wuk"], 2, 512, tgkv, gkv),
                                                    ("wuv", wuv, w["wuv"], 2, 512, tgkv, gkv)):
            for j in range(nch):
                b = k % 2
                k += 1
                t1 = g.dma("sp", "a_wst%d" % b, wst[b][:, 0:ncol], src[j * 128:(j + 1) * 128, :], deps=[wfree[b]])
                t2 = g.op("dve", lambda h, dst=dst, j=j, b=b, ncol=ncol, gs=gs: h.tensor_scalar(
                    dst[:, j, :], wst[b][:, 0:ncol], gs[:, j:j + 1], None, ALU.mult), deps=[t1, gt])
                wfree[b] = t2
                wtok[(name, j)] = t2
        wall = wl + list(wtok.values())

        import os
        DBG = int(os.environ.get("DBG_A", "99"))
        stg_free = [None] * NST
        stg_i = [0]

        cur_stage = [0]

        def stage():
            i = stg_i[0] % NST
            stg_i[0] += 1
            cur_stage[0] = i
            return i

        xin_free = [None, None]
        xb_free = [None, None]
        xT_free = [None, None]
        cs_free = [None, None]
        craw_free = None
        blk_i = 0
        for t in range(NT if DBG >= 1 else 0):
            tb = t % 2
            tok0 = t * 512
            tcs = g.dma("sp", "a_cs%d" % tb, cs[tb][:, 0, :], w["cos"][:, tok0:tok0 + 512], deps=[cs_free[tb]])
            tcs2 = g.dma("sp", "a_cs%d" % tb, cs[tb][:, 1, :], w["sin"][:, tok0:tok0 + 512])
            xT_ready = []
            for blk in range(4):
                b = blk_i % 2
                blk_i += 1
                r0 = tok0 + blk * 128
                tl = g.dma("sp", "a_xin%d" % b, xin[b][:, :], x_d[r0:r0 + 128, :], deps=[xin_free[b], xdeps])
                tc = g.op("act", lambda h, b=b: h.activation(out=xb[b][:, :], in_=xin[b][:, :], func=AF.Copy),
                          deps=[tl, xb_free[b]])
                xin_free[b] = tc
                tp = None
                for fc in range(8):
                    tp = g.op("pe", lambda h, b=b, fc=fc: h.transpose(
                        P.tb[:, fc * 128:(fc + 1) * 128], xb[b][:, fc * 128:(fc + 1) * 128], ident[:, :]),
                        deps=[tc, P.tbfree] if fc == 0 else (), sig=(fc == 7))
                xb_free[b] = tp
                te = g.op("dve", lambda h, tb=tb, blk=blk: h.tensor_copy(
                    out=xT[tb][:, :, blk * 128:(blk + 1) * 128],
                    in_=P.tb[:, :].rearrange("p (c n) -> p c n", c=8)), deps=[tp, xT_free[tb]])
                P.tbfree = te
                xT_ready.append(te)
            xTt = xT[tb]
            if DBG < 2:
                continue

            def fm_chunk(c0, ncols, deps):
                k = P.get()
                tok = mm_group(g, P.banks[k][0:ncols, :],
                               [(win[:, fc, c0:c0 + ncols], xTt[:, fc, :]) for fc in range(8)],
                               deps=[deps, P.free[k], wl])
                return k, tok

            sq_toks = []
            raw_toks = []
            for j in range(5):
                k, tok = fm_chunk(j * 128, 128, xT_ready)
                t2 = g.op("dve", lambda h, k=k, j=j: h.tensor_copy(out=craw[:, j, :], in_=P.banks[k][:, :]),
                          deps=[tok, craw_free])
                t1 = g.op("act", lambda h, j=j: h.activation(out=csq[:, j, :], in_=craw[:, j, :], func=AF.Square),
                          deps=[t2, craw_free])
                P.rel(k, t2)
                sq_toks.append(t1)
                raw_toks.append(t2)
            if DBG < 3:
                continue
            rtoks = []
            for (which, js, n) in ((0, (0, 1, 2), 384.0), (1, (3, 4), 256.0)):
                k = P.get()
                tok = mm_group(g, P.banks[k][:, :], [(ones[:, :], csq[:, j, :]) for j in js],
                               deps=[[sq_toks[j] for j in js], P.free[k]])
                t1 = g.op("act", lambda h, k=k, which=which, n=n: h.activation(
                    out=rbc[:, which, :], in_=P.banks[k][:, :], func=AF.Ln, bias=1e-6, scale=1.0 / n), deps=[tok, craw_free])
                P.rel(k, t1)
                t2 = g.op("act", lambda h, which=which: h.activation(
                    out=rbc[:, which, :], in_=rbc[:, which, :], func=AF.Exp, scale=-0.5), deps=[t1])
                rtoks.append(t2)
            cn_toks = []
            for j in range(5):
                which = 0 if j < 3 else 1
                eng = "pool" if j % 2 == 0 else "dve"
                cn_toks.append(g.op(eng, lambda h, j=j, which=which: h.tensor_tensor(
                    cn[:, j, :], craw[:, j, :], rbc[:, which, :], ALU.mult), deps=[raw_toks[j], rtoks[which], craw_free]))
            craw_last = list(cn_toks)

            if DBG < 4:
                continue
            def up_chunk(wt, nk, c0, js0, deps):
                k = P.get()
                tok = mm_group(g, P.banks[k][:, :], [(wt[:, j, c0:c0 + 128], cn[:, js0 + j, :]) for j in range(nk)],
                               deps=[deps, P.free[k], wall])
                return k, tok

            def store(src_ap, dst_ap, dep):
                return g.dma("sp", "a_out%d" % cur_stage[0], dst_ap, src_ap, deps=[dep])

            def evac_store(k, tok, dst_ap, eng, scale=None, rows=128):
                i = stage()
                if eng == "act":
                    te = g.op("act", lambda h, k=k, i=i: h.activation(
                        out=stg[i][0:rows, :], in_=P.banks[k][0:rows, :], func=AF.Copy,
                        scale=(1.0 if scale is None else scale)), deps=[tok, stg_free[i]])
                else:
                    te = g.op("dve", lambda h, k=k, i=i: h.tensor_copy(out=stg[i][0:rows, :], in_=P.banks[k][0:rows, :]),
                              deps=[tok, stg_free[i]])
                P.rel(k, te)
                ts = store(stg[i][0:rows, :], dst_ap, te)
                stg_free[i] = ts
                outtoks.append(ts)
                return ts

            for hh in range(4):
                k, tok = up_chunk(wuq, 3, hh * 128, 0, cn_toks[0:3])
                craw_last.append(tok)
                evac_store(k, tok, o["qnT"][hh, :, tok0:tok0 + 512], "act" if hh % 2 == 0 else "dve")

            def rope(kA, tokA, kB, tokB, rows, dsts):
                t1 = g.op("dve", lambda h, tb=tb: h.tensor_tensor(rt[0][0:rows, :], P.banks[kA][0:rows, :], cs[tb][0:rows, 0, :], ALU.mult),
                          deps=[tokA, tcs, tcs2, stg_free_rt[0]])
                t2 = g.op("dve", lambda h, tb=tb: h.tensor_tensor(rt[1][0:rows, :], P.banks[kB][0:rows, :], cs[tb][0:rows, 1, :], ALU.mult),
                          deps=[tokB, stg_free_rt[0]])
                P.rel(kA, t1)
                P.rel(kB, t2)
                i = stage()
                t3 = g.op("pool", lambda h, i=i: h.tensor_tensor(stg[i][0:rows, :], rt[0][0:rows, :], rt[1][0:rows, :], ALU.add),
                          deps=[t1, t2, stg_free[i]])
                stg_free_rt[0] = t3
                last = None
                for (r0, r1, dst) in dsts:
                    last = store(stg[i][r0:r1, :], dst, t3)
                    outtoks.append(last)
                stg_free[i] = last
                return t3

            stg_free_rt = [None]
            for rc in range(2):
                kA, tokA = up_chunk(wuq, 3, 512 + rc * 128, 0, cn_toks[0:3])
                kB, tokB = up_chunk(wuq, 3, 768 + rc * 128, 0, cn_toks[0:3])
                craw_last += [tokA, tokB]
                rope(kA, tokA, kB, tokB, 128, [(0, 64, o["qrT"][2 * rc, :, tok0:tok0 + 512]),
                                                (64, 128, o["qrT"][2 * rc + 1, :, tok0:tok0 + 512])])
            kA, tokA = fm_chunk(640, 64, xT_ready)
            kB, tokB = fm_chunk(704, 64, xT_ready)
            trope = rope(kA, tokA, kB, tokB, 64, [(0, 64, o["krT"][:, tok0:tok0 + 512])])
            cs_free[tb] = trope
            for hh in range(4):
                k, tok = up_chunk(wuk, 2, hh * 128, 3, cn_toks[3:5])
                craw_last.append(tok)
                evac_store(k, tok, o["knT"][hh, :, tok0:tok0 + 512], "act" if hh % 2 == 0 else "dve")
            for blk in range(4):
                k = P.get()
                tok = mm_group(g, P.banks[k][:, :], [(cn[:, 3 + j, blk * 128:(blk + 1) * 128], wuv[:, j, :]) for j in range(2)],
                               deps=[cn_toks[3:5], P.free[k], wall])
                craw_last.append(tok)
                evac_store(k, tok, o["vm"][tok0 + blk * 128: tok0 + (blk + 1) * 128, :], "act" if blk % 2 == 0 else "dve")
            craw_free = craw_last
            for hh in range(4):
                k, tok = fm_chunk(768 + hh * 128, 128, xT_ready)
                evac_store(k, tok, o["dqT"][hh, :, tok0:tok0 + 512], "act", scale=0.125)
            for hh in range(4):
                k, tok = fm_chunk(1280 + hh * 128, 128, xT_ready)
                evac_store(k, tok, o["dkT"][hh, :, tok0:tok0 + 512], "dve" if hh % 2 == 0 else "act")
            last_x = None
            for blk in range(4):
                k = P.get()
                tok = mm_group(g, P.banks[k][:, :], [(xTt[:, fc, blk * 128:(blk + 1) * 128], win[:, fc, 1792:2304]) for fc in range(8)],
                               deps=[xT_ready, P.free[k], wl])
                last_x = tok
                evac_store(k, tok, o["dv"][tok0 + blk * 128: tok0 + (blk + 1) * 128, :], "act" if blk % 2 == 0 else "dve")
            xT_free[tb] = last_x
        g.barrier()
        g.flush("phA")
    return outtoks


def phase_B(g, nc, P, S, NM, ND, i_, mT_d, cst, lam_init, indeps=()):
    NKB = S // 128
    NG = S // 512
    mla_scale = (128 + 64) ** -0.5
    outtoks = []
    with ExitStack() as es:
        sb = lambda n, s, d: es.enter_context(nc.sbuf_tensor(_uname(n), s, d))
        kT = [sb("b_kT%d" % i, [128, S], BF16) for i in range(2)]
        qT = [sb("b_qT%d" % i, [128, S], BF16) for i in range(2)]
        vv = [sb("b_v%d" % i, [128, NKB, 132], BF16) for i in range(2)]
        krT = sb("b_krT", [64, S], BF16)
        qrT = [sb("b_qrT%d" % i, [64, S], BF16) for i in range(2)]
        bias_f = sb("b_biasf", [128, 5, 512], F32)
        bias_t = sb("b_biast", [128, 512], F32)
        bias_hl = sb("b_biashl", [128, 2, 5, 512], BF16)
        cfar = sb("b_cfar", [128, max(ND, 1)], F32)
        lamp = sb("b_lamp", [128, 256], F32)
        lamt = sb("b_lamt", [128, 8], F32)
        dnorm = sb("b_dnorm", [128, 128], F32)
        NPT = 4
        pt = [sb("b_pt%d" % i, [128, 512], BF16) for i in range(NPT)]
        osb = [sb("b_osb%d" % i, [128, 4, 128], F32) for i in range(2)]
        rs = sb("b_rs", [128, 32], F32)
        obf = sb("b_obf", [128, 4, 128], BF16)
        osq = sb("b_osq", [128, 128], F32)
        mo = [sb("b_mo%d" % i, [128, 512], BF16) for i in range(2)]
        ones = cst["ones"]
        ident = cst["ident"]
        tri = cst["tri"]
        zero_c = cst["zero_c"]

        g.barrier()
        vinit = [g.op("pool", lambda h, i=i: h.memset(vv[i][:, :, 128:132], 1.0)) for i in range(2)]
        tkr = g.dma("sp", "b_kr", krT[:, :], i_["krT"], deps=[indeps]) if NM > 0 else None
        tlam = None
        if ND > 0:
            t0 = g.dma("sp", "b_small", lamp[:, :], i_["lamp"])
            t0b = g.dma("sp", "b_small2", dnorm[:, :], i_["dnorm"])
            t0c = g.dma("sp", "b_small3", cfar[:, :], i_["cfar"])
            t1 = g.op("dve", lambda h: h.tensor_tensor(lamp[:, 0:64], lamp[:, 0:64], lamp[:, 64:128], ALU.mult), deps=[t0])
            t2 = g.op("dve", lambda h: h.tensor_tensor(lamp[:, 128:192], lamp[:, 128:192], lamp[:, 192:256], ALU.mult), deps=[t1])
            t3 = g.op("dve", lambda h: h.tensor_reduce(lamt[:, 0:1], lamp[:, 0:64], AX.X, ALU.add), deps=[t2])
            t4 = g.op("dve", lambda h: h.tensor_reduce(lamt[:, 1:2], lamp[:, 128:192], AX.X, ALU.add), deps=[t3])
            t5 = g.op("act", lambda h: h.activation(out=lamt[:, 2:4], in_=lamt[:, 0:2], func=AF.Exp), deps=[t4])
            t6 = g.op("dve", lambda h: h.tensor_tensor(lamt[:, 4:5], lamt[:, 2:3], lamt[:, 3:4], ALU.subtract), deps=[t5])
            t7 = g.op("dve", lambda h: h.tensor_scalar(lamt[:, 5:6], lamt[:, 4:5], -1.0, -float(lam_init), ALU.mult, ALU.add), deps=[t6])
            t8 = g.op("dve", lambda h: h.tensor_scalar(dnorm[:, :], dnorm[:, :], float(1.0 - lam_init), None, ALU.mult), deps=[t0b, t7])
            tlam = [t7, t8, t0c]

        kfree = [None, None]
        qfree = [None, None]
        vfree = [vinit[0], vinit[1]]
        qrfree = [None, None]
        ptfree = [None] * NPT
        pt_i = [0]
        osb_free = [None, None]
        obf_free = [None]
        mo_free = [None, None]
        mo_i = [0]
        bias_free = [None]
        SBANKS = [0, 1, 2]
        lbu = [None]
        OBANKS = [3, 4, 5, 6]
        s_rr = [0]

        def oacc(qb):
            return P.banks[OBANKS[qb]][:, 0:129]

        heads = [("m", h) for h in range(NM)] + [("d", h) for h in range(ND)]
        for hi, (kind, hh) in enumerate(heads):
            hb = hi % 2
            if kind == "m":
                tk = g.dma("sp", "b_k%d" % hb, kT[hb][:, :], i_["knT"][hh, :, :], deps=[kfree[hb], indeps])
                tq = g.dma("sp", "b_q%d" % hb, qT[hb][:, :], i_["qnT"][hh, :, :], deps=[qfree[hb]])
                tqr = g.dma("sp", "b_qr%d" % hb, qrT[hb][:, :], i_["qrT"][hh, :, :], deps=[qrfree[hb]])
                tv = g.dma("sp", "b_v%d" % hb, vv[hb][:, :, 0:128],
                           i_["vm"][:, hh * 128:(hh + 1) * 128].rearrange("(kb p) d -> p kb d", p=128), deps=[vfree[hb]])
                ld = [tk, tq, tqr, tv, tkr]
                comps = [0]
            else:
                tk = g.dma("sp", "b_k%d" % hb, kT[hb][:, :], i_["dkT"][hh, :, :], deps=[kfree[hb], indeps])
                tq = g.dma("sp", "b_q%d" % hb, qT[hb][:, :], i_["dqT"][hh, :, :], deps=[qfree[hb]])
                tv = g.dma("sp", "b_v%d" % hb, vv[hb][:, :, 0:128],
                           i_["dv"][:, hh * 128:(hh + 1) * 128].rearrange("(kb p) d -> p kb d", p=128), deps=[vfree[hb]])
                tb0 = g.dma("sp", "b_bias", bias_f[:, :, :], i_["bias"][hh].rearrange("r p q -> p r q"), deps=[bias_free[0]])
                tb1 = g.op("dve", lambda h: h.tensor_copy(out=bias_hl[:, 0, :, :], in_=bias_f[:, :, :]), deps=[tb0, bias_free[0]])
                tbl = tb1
                for r in range(5):
                    ta = g.op("dve", lambda h, r=r: h.tensor_tensor(bias_t[:, :], bias_f[:, r, :], bias_hl[:, 0, r, :], ALU.subtract), deps=[tbl])
                    tbl = g.op("dve", lambda h, r=r: h.tensor_copy(out=bias_hl[:, 1, r, :], in_=bias_t[:, :]), deps=[ta])
                ld = [tk, tq, tv, tbl, tlam]
                comps = [0, 1]
            last_pe = None
            last_bias_use = None
            for G in range(NG):
                q0 = G * 512
                comp_done = []
                for c in comps:
                    nkb = 4 * G + 4
                    pv_last = [None] * 4
                    def issue_S(kb, c=c, G=G, q0=q0):
                        r = kb - 4 * G
                        qlo = 128 * r if r > 0 else 0
                        ncol = 512 - qlo
                        sk = SBANKS[s_rr[0] % len(SBANKS)]
                        s_rr[0] += 1
                        S_ps = P.banks[sk][:, 0:ncol]
                        pairs = []
                        if kind == "m":
                            pairs.append((kT[hb][:, kb * 128:(kb + 1) * 128], qT[hb][:, q0 + qlo:q0 + 512]))
                            pairs.append((krT[:, kb * 128:(kb + 1) * 128], qrT[hb][:, q0 + qlo:q0 + 512]))
                        else:
                            pairs.append((kT[hb][c * 64:(c + 1) * 64, kb * 128:(kb + 1) * 128],
                                          qT[hb][c * 64:(c + 1) * 64, q0 + qlo:q0 + 512]))
                            if r >= -1:
                                pairs.append((ident[:, :], bias_hl[:, 0, r + 1, qlo:512]))
                                pairs.append((ident[:, :], bias_hl[:, 1, r + 1, qlo:512]))
                        tS = mm_group(g, S_ps, pairs, deps=[ld, P.free[sk]])
                        if kind == "d" and r >= -1:
                            lbu[0] = tS
                        pi = pt_i[0] % NPT
                        pt_i[0] += 1
                        if kind == "m":
                            bias_arg = zero_c[:, 0:1]
                            sc = mla_scale
                        else:
                            bias_arg = cfar[:, hh:hh + 1] if r < -1 else zero_c[:, 0:1]
                            sc = 1.0
                        tE = g.op("act", lambda h, pi=pi, S_ps=S_ps, ncol=ncol, bias_arg=bias_arg, sc=sc: h.activation(
                            out=pt[pi][:, 0:ncol], in_=S_ps, func=AF.Exp, bias=bias_arg, scale=sc),
                            deps=[tS, ptfree[pi], tlam])
                        P.rel(sk, tE)
                        if r >= 0:
                            tE = g.op("pool", lambda h, pi=pi: h.tensor_tensor(pt[pi][:, 0:128], pt[pi][:, 0:128], tri[:, :], ALU.mult),
                                      deps=[tE])
                        return (r, qlo, pi, tE)

                    def issue_PV(kb, rec, G=G):
                        r, qlo, pi, tE = rec
                        qb0 = r if r > 0 else 0
                        tpv = None
                        for qb in range(qb0, 4):
                            lastkb = 4 * G + qb
                            dep = [tE]
                            if kb == 0:
                                dep.append(P.free[OBANKS[qb]])
                            tpv = g.op("pe", lambda h, qb=qb, pi=pi, kb=kb, qlo=qlo, lastkb=lastkb, hb=hb: h.matmul(
                                oacc(qb), pt[pi][:, qb * 128 - qlo:(qb + 1) * 128 - qlo], vv[hb][:, kb, 0:129],
                                start=(kb == 0), stop=(kb == lastkb)), deps=dep if qb == qb0 else ([] if kb > 0 else dep),
                                sig=(qb == 3 or kb == lastkb))
                            if kb == lastkb:
                                pv_last[qb] = tpv
                        ptfree[pi] = tpv
                        return tpv

                    LA = 2
                    recs = {}
                    for kb in range(min(LA, nkb)):
                        recs[kb] = issue_S(kb)
                    for kb in range(nkb):
                        if kb + LA < nkb:
                            recs[kb + LA] = issue_S(kb + LA)
                        last_pe = issue_PV(kb, recs.pop(kb))
                    if lbu[0] is not None:
                        last_bias_use = lbu[0]
                    slot = c
                    tr = None
                    tn = []
                    for qb in range(4):
                        bk = OBANKS[qb]
                        col = 0
                        ci = (G * 2 + c) % 4 * 4 + qb
                        tr = g.op("dve", lambda h, bk=bk, col=col, ci=ci: h.reciprocal(rs[:, ci:ci + 1], P.banks[bk][:, col + 128:col + 129]),
                                  deps=[pv_last[qb]])
                        t_ = g.op("dve", lambda h, bk=bk, col=col, ci=ci, qb=qb, slot=slot: h.tensor_scalar(
                            osb[slot][:, qb, :], P.banks[bk][:, col:col + 128], rs[:, ci:ci + 1], None, ALU.mult),
                            deps=[tr, osb_free[slot]])
                        tn.append(t_)
                        P.rel(bk, t_)
                    comp_done.append(tn)
                if kind == "m":
                    tfin = g.op("act", lambda h: h.activation(out=obf[:, :, :], in_=osb[0][:, :, :], func=AF.Copy),
                                deps=[comp_done[0], obf_free[0]])
                    osb_free[0] = tfin
                    fin = [tfin] * 4
                else:
                    fin = []
                    for qb in range(4):
                        ta = g.op("dve", lambda h, qb=qb: h.scalar_tensor_tensor(
                            osb[0][:, qb, :], osb[1][:, qb, :], lamt[:, 5:6], osb[0][:, qb, :], ALU.mult, ALU.add),
                            deps=[comp_done[0][qb], comp_done[1][qb], tlam])
                        ci = 12 + qb
                        tb0_ = g.op("dve", lambda h, qb=qb: h.tensor_tensor(osq[:, :], osb[0][:, qb, :], osb[0][:, qb, :], ALU.mult), deps=[ta])
                        tb_ = g.op("dve", lambda h, ci=ci: h.tensor_reduce(rs[:, ci + 4:ci + 5], osq[:, :], AX.X, ALU.add), deps=[tb0_])
                        tc_ = g.op("act", lambda h, ci=ci: h.activation(out=rs[:, ci + 4:ci + 5], in_=rs[:, ci + 4:ci + 5], func=AF.Ln, bias=1e-5, scale=1.0 / 128.0), deps=[tb_])
                        td_ = g.op("act", lambda h, ci=ci: h.activation(out=rs[:, ci + 4:ci + 5], in_=rs[:, ci + 4:ci + 5], func=AF.Exp, scale=-0.5), deps=[tc_])
                        te_ = g.op("dve", lambda h, qb=qb, ci=ci: h.scalar_tensor_tensor(
                            obf[:, qb, :], osb[0][:, qb, :], rs[:, ci + 4:ci + 5], dnorm[:, :], ALU.mult, ALU.mult),
                            deps=[td_, obf_free[0]])
                        fin.append(te_)
                    osb_free[0] = fin
                    osb_free[1] = fin
                mi = mo_i[0] % 2
                mo_i[0] += 1
                tp = None
                for qb in range(4):
                    tp = g.op("pe", lambda h, qb=qb: h.transpose(P.tb[:, qb * 128:(qb + 1) * 128], obf[:, qb, :], ident[:, :]),
                              deps=[fin[qb], P.tbfree] if qb == 0 else [fin[qb]], sig=(qb == 3))
                obf_free[0] = tp
                te = g.op("act", lambda h, mi=mi: h.activation(out=mo[mi][:, :], in_=P.tb[:, 0:512], func=AF.Copy), deps=[tp, mo_free[mi]])
                P.tbfree = te
                ts = g.dma("sp", "b_out%d" % mi, mT_d[hi, :, q0:q0 + 512], mo[mi][:, :], deps=[te])
                mo_free[mi] = ts
                outtoks.append(ts)
            kfree[hb] = last_pe
            qfree[hb] = last_pe
            vfree[hb] = last_pe
            if kind == "m":
                qrfree[hb] = last_pe
            else:
                bias_free[0] = last_bias_use
        g.barrier()
        g.flush("phB")
    return outtoks


def phase_B2(g, nc, P, S, NM, ND, i_, mT_d, cst, lam_init, indeps=()):
    NKB = S // 128
    NG = S // 512
    mla_scale = (128 + 64) ** -0.5
    outtoks = []
    with ExitStack() as es:
        sb = lambda n, s, d: es.enter_context(nc.sbuf_tensor(_uname(n), s, d))
        kT = [sb("b_kT%d" % i, [128, S], BF16) for i in range(2)]
        qT = [sb("b_qT%d" % i, [128, S], BF16) for i in range(2)]
        vv = [sb("b_v%d" % i, [128, NKB, 128], BF16) for i in range(2)]
        kz = [[sb("b_kz%d_%d" % (i, c), [128, S], BF16) for c in range(2)] for i in range(2)]
        krT = sb("b_krT", [128, S], BF16)
        qrT = [sb("b_qrT%d" % i, [128, S], BF16) for i in range(2)]
        pacc = [[sb("b_pacc%d_%d" % (i, e), [128, 512], F32) for e in range(2)] for i in range(2)]
        onesf = sb("b_onesf", [128, 128], F32)
        bias_f = sb("b_biasf", [128, 5, 512], F32)
        bias_t = sb("b_biast", [128, 512], F32)
        bias_hl = sb("b_biashl", [128, 2, 5, 512], BF16)
        cfar = sb("b_cfar", [128, max(ND, 1)], F32)
        lamp = sb("b_lamp", [128, 256], F32)
        lamt = sb("b_lamt", [128, 8], F32)
        dncol = sb("b_dncol", [128, 1], F32)
        NPT = 6
        pt = [sb("b_pt%d" % i, [128, 512], BF16) for i in range(NPT)]
        rinv2 = [sb("b_rinv%d" % i, [128, 512], F32) for i in range(2)]
        on = [sb("b_on%d" % i, [128, 512], F32) for i in range(2)]
        osq = sb("b_osq", [128, 512], BF16)
        rr = sb("b_rr", [128, 512], F32)
        mo = [sb("b_mo%d" % i, [128, 512], BF16) for i in range(2)]
        ones = cst["ones"]
        ident = cst["ident"]
        tri = cst["tri"]
        zero_c = cst["zero_c"]

        g.barrier()
        zinit = [g.op("pool", lambda h: h.memset(onesf[:, :], 1.0)),
                 g.op("pool", lambda h: h.memset(krT[64:128, :], 0.0))]
        for i in range(2):
            zinit.append(g.op("pool", lambda h, i=i: h.memset(qrT[i][64:128, :], 0.0)))
            zinit.append(g.op("pool", lambda h, i=i: h.memset(kz[i][0][64:128, :], 0.0)))
            zinit.append(g.op("pool", lambda h, i=i: h.memset(kz[i][1][0:64, :], 0.0)))
        tkr = g.dma("sp", "b_kr", krT[0:64, :], i_["krT"], deps=[indeps, zinit]) if NM > 0 else None
        tlam = None
        if ND > 0:
            t0 = g.dma("sp", "b_small", lamp[:, :], i_["lamp"])
            t0b = g.dma("sp", "b_small2", dncol[:, :], i_["dnorm"][0:1, :].rearrange("o d -> d o"))
            t0c = g.dma("sp", "b_small3", cfar[:, :], i_["cfar"])
            t1 = g.op("dve", lambda h: h.tensor_tensor(lamp[:, 0:64], lamp[:, 0:64], lamp[:, 64:128], ALU.mult), deps=[t0])
            t2 = g.op("dve", lambda h: h.tensor_tensor(lamp[:, 128:192], lamp[:, 128:192], lamp[:, 192:256], ALU.mult), deps=[t1])
            t3 = g.op("dve", lambda h: h.tensor_reduce(lamt[:, 0:1], lamp[:, 0:64], AX.X, ALU.add), deps=[t2])
            t4 = g.op("dve", lambda h: h.tensor_reduce(lamt[:, 1:2], lamp[:, 128:192], AX.X, ALU.add), deps=[t3])
            t5 = g.op("act", lambda h: h.activation(out=lamt[:, 2:4], in_=lamt[:, 0:2], func=AF.Exp), deps=[t4])
            t6 = g.op("dve", lambda h: h.tensor_tensor(lamt[:, 4:5], lamt[:, 2:3], lamt[:, 3:4], ALU.subtract), deps=[t5])
            t7 = g.op("dve", lambda h: h.tensor_scalar(lamt[:, 5:6], lamt[:, 4:5], -1.0, -float(lam_init), ALU.mult, ALU.add), deps=[t6])
            t8 = g.op("dve", lambda h: h.tensor_scalar(dncol[:, :], dncol[:, :], float(1.0 - lam_init), None, ALU.mult), deps=[t0b, t7])
            tlam = [t7, t8, t0c]

        kfree = [None, None]
        qfree = [None, None]
        vfree = [None, None]
        qrfree = [None, None]
        ptfree = [None] * NPT
        pt_i = [0]
        mo_free = [None, None]
        mo_i = [0]
        bias_free = [None]
        lbu = [None]
        SBANKS = [0, 1, 2, 3]
        OTB, RSB = [4, 6], [5, 5]
        grp_i = [0]
        s_rr = [0]
        rinv_free = [None, None]
        on_free = [None, None]
        osq_free = [None]
        rr_free = [None]
        acc_i = [0]
        pacc_free = [None, None]
        lastA = [None, None]

        heads = [("m", h) for h in range(NM)] + [("d", h) for h in range(ND)]
        for hi, (kind, hh) in enumerate(heads):
            hb = hi % 2
            if kind == "m":
                tk = g.dma("sp", "b_k%d" % hb, kT[hb][:, :], i_["knT"][hh, :, :], deps=[kfree[hb], indeps])
                tq = g.dma("sp", "b_q%d" % hb, qT[hb][:, :], i_["qnT"][hh, :, :], deps=[qfree[hb]])
                tqr = g.dma("sp", "b_qr%d" % hb, qrT[hb][0:64, :], i_["qrT"][hh, :, :], deps=[qrfree[hb], zinit])
                tv = g.dma("sp", "b_v%d" % hb, vv[hb][:, :, :],
                           i_["vm"][:, hh * 128:(hh + 1) * 128].rearrange("(kb p) d -> p kb d", p=128), deps=[vfree[hb]])
                ld = [tk, tq, tqr, tv, tkr]
                comps = [0]
            else:
                tk0 = g.dma("sp", "b_k%d" % hb, kz[hb][0][0:64, :], i_["dkT"][hh, 0:64, :], deps=[kfree[hb], indeps, zinit])
                tk = g.dma("sp", "b_k%d" % hb, kz[hb][1][64:128, :], i_["dkT"][hh, 64:128, :])
                tq = g.dma("sp", "b_q%d" % hb, qT[hb][:, :], i_["dqT"][hh, :, :], deps=[qfree[hb]])
                tv = g.dma("sp", "b_v%d" % hb, vv[hb][:, :, :],
                           i_["dv"][:, hh * 128:(hh + 1) * 128].rearrange("(kb p) d -> p kb d", p=128), deps=[vfree[hb]])
                tb0 = g.dma("sp", "b_bias", bias_f[:, :, :], i_["bias"][hh].rearrange("r p q -> p r q"), deps=[bias_free[0]])
                tb1 = g.op("dve", lambda h: h.tensor_copy(out=bias_hl[:, 0, :, :], in_=bias_f[:, :, :]), deps=[tb0, bias_free[0]])
                tbl = tb1
                for r in range(5):
                    ta = g.op("dve", lambda h, r=r: h.tensor_tensor(bias_t[:, :], bias_f[:, r, :], bias_hl[:, 0, r, :], ALU.subtract), deps=[tbl])
                    tbl = g.op("dve", lambda h, r=r: h.tensor_copy(out=bias_hl[:, 1, r, :], in_=bias_t[:, :]), deps=[ta])
                ld = [tk0, tk, tq, tv, tbl, tlam]
                comps = [0, 1]
            last_pe = None
            lbu[0] = None
            for G in range(NG):
                q0 = G * 512
                nkb = 4 * G + 4
                on_tok = []
                for c in comps:
                    gi = grp_i[0] % 2
                    grp_i[0] += 1
                    OT, RS = OTB[gi], RSB[gi]
                    rinv = rinv2[gi]

                    def issue_S(kb, c=c, G=G, q0=q0):
                        r = kb - 4 * G
                        qlo = 128 * r if r > 0 else 0
                        ncol = 512 - qlo
                        sk = SBANKS[s_rr[0] % len(SBANKS)]
                        s_rr[0] += 1
                        S_ps = P.banks[sk][:, 0:ncol]
                        pairs = []
                        if kind == "m":
                            pairs.append((kT[hb][:, kb * 128:(kb + 1) * 128], qT[hb][:, q0 + qlo:q0 + 512]))
                            pairs.append((krT[:, kb * 128:(kb + 1) * 128], qrT[hb][:, q0 + qlo:q0 + 512]))
                        else:
                            pairs.append((kz[hb][c][:, kb * 128:(kb + 1) * 128], qT[hb][:, q0 + qlo:q0 + 512]))
                            if r >= -1:
                                pairs.append((ident[:, :], bias_hl[:, 0, r + 1, qlo:512]))
                                pairs.append((ident[:, :], bias_hl[:, 1, r + 1, qlo:512]))
                        tS = mm_group(g, S_ps, pairs, deps=[ld, P.free[sk]])
                        if kind == "d" and r >= -1:
                            lbu[0] = tS
                        pi = pt_i[0] % NPT
                        pt_i[0] += 1
                        if kind == "m":
                            bias_arg = zero_c[:, 0:1]
                            sc = mla_scale
                        else:
                            bias_arg = cfar[:, hh:hh + 1] if r < -1 else zero_c[:, 0:1]
                            sc = 1.0
                        tE = g.op("act", lambda h, pi=pi, S_ps=S_ps, ncol=ncol, bias_arg=bias_arg, sc=sc: h.activation(
                            out=pt[pi][:, 0:ncol], in_=S_ps, func=AF.Exp, bias=bias_arg, scale=sc),
                            deps=[tS, ptfree[pi], tlam])
                        P.rel(sk, tE)
                        if r >= 0:
                            tE = g.op("pool", lambda h, pi=pi: h.tensor_tensor(pt[pi][:, 0:128], pt[pi][:, 0:128], tri[:, :], ALU.mult),
                                      deps=[tE])
                        ai = acc_i[0] % 2
                        if kb == 0:
                            tA = g.op("dve", lambda h, pi=pi, ai=ai: h.tensor_copy(out=pacc[ai][0][:, :], in_=pt[pi][:, :]),
                                      deps=[tE, pacc_free[ai]])
                            tZ = g.op("pool", lambda h, ai=ai: h.memset(pacc[ai][1][:, :], 0.0), deps=[pacc_free[ai]])
                            lastA[0] = tA
                            lastA[1] = tZ
                        elif kb % 2 == 0:
                            tA = g.op("dve", lambda h, pi=pi, ai=ai, qlo=qlo, ncol=ncol: h.tensor_tensor(
                                pacc[ai][0][:, qlo:512], pacc[ai][0][:, qlo:512], pt[pi][:, 0:ncol], ALU.add), deps=[tE, lastA[0]])
                            lastA[0] = tA
                        else:
                            tA = g.op("pool", lambda h, pi=pi, ai=ai, qlo=qlo, ncol=ncol: h.tensor_tensor(
                                pacc[ai][1][:, qlo:512], pacc[ai][1][:, qlo:512], pt[pi][:, 0:ncol], ALU.add), deps=[tE, lastA[1]])
                            lastA[1] = tA
                        return (qlo, ncol, pi, tE, tA)

                    def issue_PV(kb, rec, nkb=nkb, OT=OT):
                        qlo, ncol, pi, tE, tA = rec
                        first = (kb == 0)
                        last = (kb == nkb - 1)
                        t = g.op("pe", lambda h, pi=pi, kb=kb, qlo=qlo, ncol=ncol, hb=hb, first=first, last=last, OT=OT: h.matmul(
                            P.banks[OT][:, qlo:512], vv[hb][:, kb, :], pt[pi][:, 0:ncol], start=first, stop=last),
                            deps=[tE, P.free[OT]] if first else [tE], sig=True)
                        ptfree[pi] = [t, tA]
                        return t

                    recs = {}
                    for kb in range(min(2, nkb)):
                        recs[kb] = issue_S(kb)
                    tpv = None
                    for kb0 in range(0, nkb, 2):
                        for kb in (kb0 + 2, kb0 + 3):
                            if kb < nkb:
                                recs[kb] = issue_S(kb)
                        for kb in (kb0, kb0 + 1):
                            if kb < nkb:
                                tpv = issue_PV(kb, recs.pop(kb))
                    last_pe = tpv
                    ai = acc_i[0] % 2
                    acc_i[0] += 1
                    g.op("pe", lambda h, ai=ai, RS=RS: h.matmul(P.banks[RS][:, :], onesf[:, :], pacc[ai][0][:, :], start=True, stop=False),
                         deps=[lastA[0], lastA[1], P.free[RS]], sig=False)
                    trs = g.op("pe", lambda h, ai=ai, RS=RS: h.matmul(P.banks[RS][:, :], onesf[:, :], pacc[ai][1][:, :], start=False, stop=True))
                    pacc_free[ai] = trs
                    last_pe = trs
                    tr = g.op("dve", lambda h, RS=RS, rinv=rinv: h.reciprocal(rinv[:, :], P.banks[RS][:, :]), deps=[trs, rinv_free[gi]])
                    P.rel(RS, tr)
                    if kind == "m":
                        mi = mo_i[0] % 2
                        mo_i[0] += 1
                        tn = g.op("dve", lambda h, mi=mi, OT=OT, rinv=rinv: h.tensor_tensor(mo[mi][:, :], P.banks[OT][:, :], rinv[:, :], ALU.mult),
                                  deps=[tr, mo_free[mi]])
                        P.rel(OT, tn)
                        rinv_free[gi] = tn
                        ts = g.dma("sp", "b_out%d" % mi, mT_d[hi, :, q0:q0 + 512], mo[mi][:, :], deps=[tn])
                        mo_free[mi] = ts
                        outtoks.append(ts)
                    else:
                        tn = g.op("dve", lambda h, c=c, OT=OT, rinv=rinv: h.tensor_tensor(on[c][:, :], P.banks[OT][:, :], rinv[:, :], ALU.mult),
                                  deps=[tr, on_free[c]])
                        P.rel(OT, tn)
                        rinv_free[gi] = tn
                        on_tok.append(tn)
                if kind == "d":
                    ta = g.op("dve", lambda h: h.scalar_tensor_tensor(on[0][:, :], on[1][:, :], lamt[:, 5:6], on[0][:, :], ALU.mult, ALU.add),
                              deps=[on_tok, tlam])
                    on_free[1] = ta
                    tsq = g.op("pool", lambda h: h.tensor_tensor(osq[:, :], on[0][:, :], on[0][:, :], ALU.mult), deps=[ta, osq_free[0]])
                    XB = RS
                    tmm = g.op("pe", lambda h, XB=XB: h.matmul(P.banks[XB][:, :], ones[:, :], osq[:, :], start=True, stop=True),
                               deps=[tsq, P.free[XB]])
                    osq_free[0] = tmm
                    t1_ = g.op("act", lambda h, XB=XB: h.activation(out=rr[:, :], in_=P.banks[XB][:, :], func=AF.Ln, bias=1e-5, scale=1.0 / 128.0),
                               deps=[tmm, rr_free[0]])
                    P.rel(XB, t1_)
                    t2_ = g.op("act", lambda h: h.activation(out=rr[:, :], in_=rr[:, :], func=AF.Exp, scale=-0.5), deps=[t1_])
                    mi = mo_i[0] % 2
                    mo_i[0] += 1
                    tf = g.op("dve", lambda h, mi=mi: h.scalar_tensor_tensor(mo[mi][:, :], on[0][:, :], dncol[:, 0:1], rr[:, :], ALU.mult, ALU.mult),
                              deps=[t2_, ta, mo_free[mi], tlam])
                    rr_free[0] = tf
                    on_free[0] = tf
                    ts = g.dma("sp", "b_out%d" % mi, mT_d[hi, :, q0:q0 + 512], mo[mi][:, :], deps=[tf])
                    mo_free[mi] = ts
                    outtoks.append(ts)
            kfree[hb] = last_pe
            qfree[hb] = last_pe
            vfree[hb] = last_pe
            if kind == "m":
                qrfree[hb] = last_pe
            else:
                bias_free[0] = lbu[0]
        g.barrier()
        g.flush("phB")
    return outtoks


def ln_block(g, st, A, lnp, gi, bi, dep, tln):
    t1 = g.op("dve", lambda h: h.bn_stats(out=st[:, 0:6], in_=A(0, 512)), deps=[dep])
    t2 = g.op("dve", lambda h: h.bn_stats(out=st[:, 6:12], in_=A(512, 1024)), deps=[dep])
    t3 = g.op("dve", lambda h: h.bn_aggr(out=st[:, 12:14], in_=st[:, 0:12]), deps=[t1, t2])
    t4a = g.op("act", lambda h: h.activation(out=st[:, 14:15], in_=st[:, 13:14], func=AF.Ln, bias=1e-5, scale=1.0), deps=[t3])
    t4 = g.op("act", lambda h: h.activation(out=st[:, 14:15], in_=st[:, 14:15], func=AF.Exp, scale=-0.5), deps=[t4a])
    t5 = g.op("dve", lambda h: h.tensor_scalar(A(0, 1024), A(0, 1024), st[:, 12:13], st[:, 14:15], ALU.subtract, ALU.mult), deps=[t4])
    t6 = g.op("dve", lambda h: h.tensor_tensor(A(0, 1024), A(0, 1024), lnp[:, gi, :], ALU.mult), deps=[t5, tln])
    t7 = g.op("dve", lambda h: h.tensor_tensor(A(0, 1024), A(0, 1024), lnp[:, bi, :], ALU.add), deps=[t6])
    return t7


def gates_block(g, lg, dep):
    L = lg[:, 0:8]
    m1, m2 = lg[:, 8:9], lg[:, 9:10]
    k1, k2 = lg[:, 16:24], lg[:, 24:32]
    L2 = lg[:, 32:40]
    e, g1, g2 = lg[:, 10:11], lg[:, 11:12], lg[:, 12:13]
    t = g.op("dve", lambda h: h.tensor_reduce(m1, L, AX.X, ALU.max), deps=[dep])
    t = g.op("dve", lambda h: h.tensor_scalar(k1, L, m1, None, ALU.is_equal), deps=[t])
    t = g.op("dve", lambda h: h.scalar_tensor_tensor(L2, k1, -1e30, L, ALU.mult, ALU.add), deps=[t])
    t = g.op("dve", lambda h: h.tensor_reduce(m2, L2, AX.X, ALU.max), deps=[t])
    t = g.op("dve", lambda h: h.tensor_scalar(k2, L2, m2, None, ALU.is_equal), deps=[t])
    t = g.op("dve", lambda h: h.tensor_tensor(e, m2, m1, ALU.subtract), deps=[t])
    t = g.op("act", lambda h: h.activation(out=e, in_=e, func=AF.Exp), deps=[t])
    t = g.op("dve", lambda h: h.tensor_scalar(g1, e, 1.0, None, ALU.add), deps=[t])
    t = g.op("dve", lambda h: h.reciprocal(g1, g1), deps=[t])
    t = g.op("dve", lambda h: h.tensor_tensor(g2, e, g1, ALU.mult), deps=[t])
    t = g.op("dve", lambda h: h.tensor_scalar(k1, k1, g1, None, ALU.mult), deps=[t])
    t = g.op("dve", lambda h: h.scalar_tensor_tensor(lg[:, 56:64], k2, g2, k1, ALU.mult, ALU.add), deps=[t])
    return t


def phase_C(g, nc, P, TOK, moe, x_d, mT_d, w, xo_d, cst, indeps=()):
    CT = min(TOK, 2048)
    NCH = TOK // CT
    NB = CT // 128
    NTT = CT // 512
    F = FE if moe else FD
    nfc_all = F // 128
    FT = 4
    outtoks = []
    ident = cst["ident"]
    with ExitStack() as es:
        sb = lambda n, s, d: es.enter_context(nc.sbuf_tensor(_uname(n), s, d))
        yacc = sb("c_yacc", [128, NB, 1024], F32)
        x1T = sb("c_x1T", [128, 8, CT], BF16)
        lnp = sb("c_lnp", [128, 2, 1024], F32)
        st = sb("c_st", [128, 16], F32)
        sg = [sb("c_sg%d" % i, [128, 512], F32) for i in range(2)]
        tt_ = [sb("c_tt%d" % i, [128, 512], F32) for i in range(2)]
        if moe:
            gT = sb("c_gT", [8, CT], F32)
            gbc = sb("c_gbc", [128, CT], F32)
            sel = cst["sel"]
            identf = cst["identf"]
        g.barrier()
        for ch in range(NCH):
            c0 = ch * CT
            with ExitStack() as es1:
                sb1 = lambda n, s, d: es1.enter_context(nc.sbuf_tensor(_uname(n), s, d))
                wo = sb1("c_wo", [128, 8, 1024], BF16)
                xin = [sb1("c_xin%d" % i, [128, 1024], F32) for i in range(2)]
                xbf = [sb1("c_xbf%d" % i, [128, 1024], BF16) for i in range(2)]
                mTs = [sb1("c_mT%d" % i, [128, 8, 128], BF16) for i in range(2)]
                if moe:
                    rw = sb1("c_rw", [128, 8, 8], F32)
                    x1Tf = [sb1("c_x1Tf%d" % i, [128, 8, 128], F32) for i in range(2)]
                    lg = sb1("c_lg", [128, 64], F32)
                g.barrier()
                two = [g.dma("pool", "c_wo", wo[:, fc, :], w["wo"][fc * 128:(fc + 1) * 128, :]) for fc in range(8)]
                tln = g.dma("sp", "c_ln", lnp[:, :, :], w["ln"][:, 0:2, :])
                trw = g.dma("sp", "c_rw", rw[:, :, :], w["router"]) if moe else None
                xin_free = [None, None]
                xbf_free = [None, None]
                mT_free = [None, None]
                x1Tf_free = [None, None]
                gate_ready = None
                for blk in range(NB):
                    b = blk % 2
                    r0 = c0 + blk * 128
                    tlx = g.dma("sp", "c_xin%d" % b, xin[b][:, :], x_d[r0:r0 + 128, :], deps=[xin_free[b], indeps])
                    tlm = g.dma("sp", "c_mT%d" % b, mTs[b][:, :, :], mT_d[:, :, r0:r0 + 128].rearrange("c p t -> p c t"),
                                deps=[mT_free[b], indeps])
                    ks = []
                    for half in range(2):
                        k = P.get()
                        tok = mm_group(g, P.banks[k][:, :], [(mTs[b][:, hc, :], wo[:, hc, half * 512:(half + 1) * 512]) for hc in range(8)],
                                       deps=[tlm, two, P.free[k]])
                        ks.append((k, tok))
                    mT_free[b] = ks[1][1]
                    ty = None
                    for half in range(2):
                        k, tok = ks[half]
                        ty = g.op("dve", lambda h, b=b, k=k, half=half: h.scalar_tensor_tensor(
                            xin[b][:, half * 512:(half + 1) * 512], xin[b][:, half * 512:(half + 1) * 512], float(DN_ALPHA),
                            P.banks[k][:, :], ALU.mult, ALU.add), deps=[tok, tlx, ty])
                        P.rel(k, ty)
                    tl = ln_block(g, st, (lambda a, c, b=b: xin[b][:, a:c]), lnp, 0, 1, ty, tln)
                    ta = g.op("act", lambda h, b=b, blk=blk: h.activation(out=yacc[:, blk, :], in_=xin[b][:, :], func=AF.Copy, scale=float(DN_ALPHA)),
                              deps=[tl])
                    tb_ = g.op("act", lambda h, b=b: h.activation(out=xbf[b][:, :], in_=xin[b][:, :], func=AF.Copy), deps=[tl, xbf_free[b]])
                    tp = None
                    for fc in range(8):
                        tp = g.op("pe", lambda h, b=b, fc=fc: h.transpose(P.tb[:, fc * 128:(fc + 1) * 128], xbf[b][:, fc * 128:(fc + 1) * 128], ident[:, :]),
                                  deps=[tb_, P.tbfree] if fc == 0 else (), sig=(fc == 7))
                    xbf_free[b] = tp
                    te = g.op("dve", lambda h, blk=blk: h.tensor_copy(out=x1T[:, :, blk * 128:(blk + 1) * 128],
                                                                      in_=P.tb[:, :].rearrange("p (c n) -> p c n", c=8)), deps=[tp])
                    P.tbfree = te
                    xfree = [ta, tb_]
                    if moe:
                        fb = blk % 2
                        parts = []
                        for half in range(2):
                            kk = P.get()
                            tpf = None
                            for j in range(4):
                                fc = half * 4 + j
                                tpf = g.op("pe", lambda h, b=b, fc=fc, j=j, kk=kk: h.transpose(
                                    P.banks[kk][:, j * 128:(j + 1) * 128], xin[b][:, fc * 128:(fc + 1) * 128], identf[:, :]),
                                    deps=[tl, P.free[kk]] if j == 0 else (), sig=(j == 3))
                            tcp = g.op("act", lambda h, fb=fb, half=half, kk=kk: h.activation(
                                out=x1Tf[fb][:, half * 4:(half + 1) * 4, :], in_=P.banks[kk][:, :].rearrange("p (c n) -> p c n", c=4), func=AF.Copy),
                                deps=[tpf, x1Tf_free[fb]])
                            P.rel(kk, tcp)
                            parts.append(tcp)
                            xfree.append(tpf)
                        kk = P.get()
                        tlg = mm_group(g, P.banks[kk][:, 0:8], [(x1Tf[fb][:, fc, :], rw[:, fc, :]) for fc in range(8)],
                                       deps=[parts, trw, P.free[kk]])
                        x1Tf_free[fb] = tlg
                        tlc = g.op("dve", lambda h, kk=kk: h.tensor_copy(out=lg[:, 0:8], in_=P.banks[kk][:, 0:8]), deps=[tlg, gate_ready])
                        P.rel(kk, tlc)
                        tgt = gates_block(g, lg, tlc)
                        kk = P.get()
                        ttp = g.op("pe", lambda h, kk=kk: h.transpose(P.banks[kk][0:8, 0:128], lg[:, 56:64], identf[:, :]), deps=[tgt, P.free[kk]])
                        tgc = g.op("dve", lambda h, kk=kk, blk=blk: h.tensor_copy(out=gT[:, blk * 128:(blk + 1) * 128], in_=P.banks[kk][0:8, 0:128]),
                                   deps=[ttp])
                        P.rel(kk, tgc)
                        gate_ready = tgc
                    xin_free[b] = xfree
                g.barrier()
                g.flush("phC1")
            with ExitStack() as es2:
                sb2 = lambda n, s, d: es2.enter_context(nc.sbuf_tensor(_uname(n), s, d))
                hT = sb2("c_hT", [128, FT, CT], BF16)
                wgu = [sb2("c_wgu%d" % i, [128, 2, 8, FT * 128], BF16) for i in range(2)]
                wd = [sb2("c_wd%d" % i, [128, FT, 1024], BF16) for i in range(2)]
                g.barrier()
                wfree_gu = [None, None]
                wfree_d = [None, None]
                hfree = None
                sgfree = [None, None]
                ttfree = [None, None]
                wt_i = 0
                yl = [None] * NB
                experts = list(range(NE)) if moe else [None]
                gb_free = None
                for e in experts:
                    tgb = None
                    if moe:
                        tgb = []
                        for tt in range(NTT):
                            kk = P.get()
                            tmm = g.op("pe", lambda h, kk=kk, e=e, tt=tt: h.matmul(P.banks[kk][:, :], sel[:, e, :], gT[:, tt * 512:(tt + 1) * 512],
                                                                                  start=True, stop=True), deps=[P.free[kk]])
                            tcp = g.op("act", lambda h, kk=kk, tt=tt: h.activation(out=gbc[:, tt * 512:(tt + 1) * 512], in_=P.banks[kk][:, :], func=AF.Copy),
                                       deps=[tmm, gb_free])
                            P.rel(kk, tcp)
                            tgb.append(tcp)
                    gb_last = []
                    nft = (nfc_all + FT - 1) // FT
                    for ft in range(nft):
                        nf = min(FT, nfc_all - ft * FT)
                        wb = wt_i % 2
                        wt_i += 1
                        f0 = ft * FT * 128
                        src_g = w["wg"][e] if moe else w["wg"]
                        src_u = w["wu"][e] if moe else w["wu"]
                        src_d = w["wd"][e] if moe else w["wd"]
                        twg = g.dma("pool", "c_wg%d" % wb, wgu[wb][:, 0, :, 0:nf * 128],
                                    src_g[:, f0:f0 + nf * 128].rearrange("(fc p) f -> p fc f", p=128), deps=[wfree_gu[wb]])
                        twu = g.dma("pool", "c_wu%d" % wb, wgu[wb][:, 1, :, 0:nf * 128],
                                    src_u[:, f0:f0 + nf * 128].rearrange("(fc p) f -> p fc f", p=128))
                        twd = g.dma("pool", "c_wd%d" % wb, wd[wb][:, 0:nf, :],
                                    src_d[f0:f0 + nf * 128, :].rearrange("(j p) d -> p j d", p=128), deps=[wfree_d[wb]])
                        h_toks = []
                        last_gu = None
                        for j in range(nf):
                            for tt in range(NTT):
                                kg = P.get()
                                tg_ = mm_group(g, P.banks[kg][:, :], [(wgu[wb][:, 0, fc, j * 128:(j + 1) * 128], x1T[:, fc, tt * 512:(tt + 1) * 512]) for fc in range(8)],
                                               deps=[twg, P.free[kg]])
                                ku = P.get()
                                tu_ = mm_group(g, P.banks[ku][:, :], [(wgu[wb][:, 1, fc, j * 128:(j + 1) * 128], x1T[:, fc, tt * 512:(tt + 1) * 512]) for fc in range(8)],
                                               deps=[twu, P.free[ku]])
                                last_gu = tu_
                                si = (j * NTT + tt) % 2
                                ts_ = g.op("act", lambda h, kg=kg, si=si: h.activation(out=sg[si][:, :], in_=P.banks[kg][:, :], func=AF.Silu),
                                           deps=[tg_, sgfree[si]])
                                P.rel(kg, ts_)
                                if moe:
                                    tm_ = g.op("dve", lambda h, ku=ku, si=si: h.tensor_tensor(tt_[si][:, :], sg[si][:, :], P.banks[ku][:, :], ALU.mult),
                                               deps=[ts_, tu_, ttfree[si]])
                                    P.rel(ku, tm_)
                                    sgfree[si] = tm_
                                    th_ = g.op("pool", lambda h, si=si, j=j, tt=tt: h.tensor_tensor(
                                        hT[:, j, tt * 512:(tt + 1) * 512], tt_[si][:, :], gbc[:, tt * 512:(tt + 1) * 512], ALU.mult),
                                        deps=[tm_, tgb, hfree])
                                    ttfree[si] = th_
                                    gb_last.append(th_)
                                else:
                                    th_ = g.op("dve", lambda h, ku=ku, si=si, j=j, tt=tt: h.tensor_tensor(
                                        hT[:, j, tt * 512:(tt + 1) * 512], sg[si][:, :], P.banks[ku][:, :], ALU.mult),
                                        deps=[ts_, tu_, hfree])
                                    P.rel(ku, th_)
                                    sgfree[si] = th_
                                h_toks.append(th_)
                        wfree_gu[wb] = last_gu
                        last_d = None
                        for blk in range(NB):
                            for half in range(2):
                                kk = P.get()
                                td_ = mm_group(g, P.banks[kk][:, :], [(hT[:, j, blk * 128:(blk + 1) * 128], wd[wb][:, j, half * 512:(half + 1) * 512]) for j in range(nf)],
                                               deps=[h_toks, twd, P.free[kk]])
                                last_d = td_
                                tacc = g.op("dve", lambda h, kk=kk, blk=blk, half=half: h.tensor_tensor(
                                    yacc[:, blk, half * 512:(half + 1) * 512], yacc[:, blk, half * 512:(half + 1) * 512], P.banks[kk][:, :], ALU.add),
                                    deps=[td_, yl[blk]])
                                P.rel(kk, tacc)
                                yl[blk] = tacc
                        wfree_d[wb] = last_d
                        hfree = last_d
                    if moe:
                        gb_free = gb_last
                g.barrier()
                g.flush("phC2")
            tln2 = g.dma("sp", "c_ln", lnp[:, :, :], w["ln"][:, 2:4, :])
            for blk in range(NB):
                tl = ln_block(g, st, (lambda a, c, blk=blk: yacc[:, blk, a:c]), lnp, 0, 1, None, tln2)
                ts = g.dma("sp", "c_out%d" % (blk % 4), xo_d[c0 + blk * 128:c0 + (blk + 1) * 128, :], yacc[:, blk, :], deps=[tl])
                outtoks.append(ts)
            g.barrier()
            g.flush("phC3")
    return outtoks


CONST_SPECS = [("c_ident", [128, 128], BF16), ("c_identf", [128, 128], F32), ("c_ones", [128, 128], BF16),
               ("c_tri", [128, 128], BF16), ("c_zero", [128, 1], F32), ("c_sel", [8, NE, 128], F32)]


def host_consts():
    ident = np.eye(128, dtype=np.float32)
    tri = (np.arange(128)[None, :] >= np.arange(128)[:, None]).astype(np.float32)
    sel = np.zeros((8, NE, 128), np.float32)
    for e in range(NE):
        sel[e, e, :] = 1.0
    return {"c_ident": ident.astype(NPBF), "c_identf": ident, "c_ones": np.ones((128, 128), NPBF),
            "c_tri": tri.astype(NPBF), "c_zero": np.zeros((128, 1), np.float32), "c_sel": sel}


def load_consts(g, nc, es):
    cst = {}
    for (name, shape, dt) in CONST_SPECS:
        d = nc.dram_tensor(name, shape, dt, kind="ExternalInput").ap()
        t = es.enter_context(nc.sbuf_tensor("s" + name, shape, dt))
        if len(shape) == 2:
            g.dma("sp", "cst", t[:, :], d)
        else:
            g.dma("sp", "cst", t[:, :, :], d)
        cst[name] = t
    return {"ident": cst["c_ident"], "identf": cst["c_identf"], "ones": cst["c_ones"], "tri": cst["c_tri"],
            "zero_c": cst["c_zero"], "sel": cst["c_sel"]}


A_IN = lambda TOK: [("win", [1024, INW], F32), ("wuq", [384, 1024], F32), ("wuk", [256, 512], F32), ("wuv", [256, 512], F32),
                    ("gq", [128, 3], F32), ("gkv", [128, 2], F32)]
A_OUT = lambda TOK: [("qnT", [4, 128, TOK], BF16), ("qrT", [4, 64, TOK], BF16), ("knT", [4, 128, TOK], BF16), ("krT", [64, TOK], BF16),
                     ("vm", [TOK, 512], BF16), ("dqT", [4, 128, TOK], BF16), ("dkT", [4, 128, TOK], BF16), ("dv", [TOK, 512], BF16)]


def B_IN(S, NM, ND):
    return [("qnT", [NM, 128, S], BF16), ("qrT", [NM, 64, S], BF16), ("knT", [NM, 128, S], BF16), ("krT", [64, S], BF16),
            ("vm", [S, NM * 128], BF16), ("dqT", [ND, 128, S], BF16), ("dkT", [ND, 128, S], BF16), ("dv", [S, ND * 128], BF16)]


B_W = lambda ND: [("bias", [ND, 5, 128, 512], F32), ("cfar", [128, ND], F32), ("lamp", [128, 256], F32), ("dnorm", [128, 128], F32)]


def C_W(moe):
    base = [("wo", [1024, 1024], F32), ("ln", [128, 4, 1024], F32)]
    if moe:
        return base + [("router", [128, 8, 8], F32), ("wg", [NE, 1024, FE], F32), ("wu", [NE, 1024, FE], F32), ("wd", [NE, FE, 1024], F32)]
    return base + [("wg", [1024, FD], F32), ("wu", [1024, FD], F32), ("wd", [FD, 1024], F32)]


PHASE_B = phase_B2


def lam_init_of(l):
    return 0.8 - 0.6 * math.exp(-0.3 * l)


def build(mode, S=4096, TOK=2048, NM=2, ND=2, moe=False, layers=(0,), lam_l=0):
    nc = bass.Bass("TRN2", target_bir_lowering=False)
    ext_in = lambda n, s, d: nc.dram_tensor(n, s, d, kind="ExternalInput").ap()
    ext_out = lambda n, s, d: nc.dram_tensor(n, s, d, kind="ExternalOutput").ap()
    internal = lambda n, s, d: nc.dram_tensor(n, s, d).ap()
    with ExitStack() as es:
        g = Gen(nc, es)
        P = Psum(g, nc, es)
        cst = load_consts(g, nc, es)
        if mode == "A":
            x = ext_in("x", [TOK, 1024], F32)
            w = {n: ext_in(n, s, d) for (n, s, d) in A_IN(TOK)}
            w["cos"] = ext_in("cos", [128, TOK], F32)
            w["sin"] = ext_in("sin", [128, TOK], F32)
            o = {n: ext_out("o_" + n, s, d) for (n, s, d) in A_OUT(TOK)}
            phase_A(g, nc, P, TOK, x, w, o, cst)
        elif mode == "B":
            i_ = {n: ext_in(n, s, d) for (n, s, d) in B_IN(S, NM, ND) + B_W(ND)}
            mT = ext_out("o_mT", [NM + ND, 128, S], BF16)
            PHASE_B(g, nc, P, S, NM, ND, i_, mT, cst, lam_init_of(lam_l))
        elif mode == "C":
            x = ext_in("x", [TOK, 1024], F32)
            mT = ext_in("mT", [8, 128, TOK], BF16)
            w = {n: ext_in(n, s, d) for (n, s, d) in C_W(moe)}
            xo = ext_out("o_x", [TOK, 1024], F32)
            phase_C(g, nc, P, TOK, moe, x, mT, w, xo, cst)
        elif mode == "F":
            x0 = ext_in("x", [S, 1024], F32)
            cos = ext_in("cos", [128, S], F32)
            sin = ext_in("sin", [128, S], F32)
            bias = ext_in("bias", [4, 5, 128, 512], F32)
            cfar = ext_in("cfar", [128, 4], F32)
            xout = ext_out("o_x", [S, 1024], F32)
            xs = [internal("xs0", [S, 1024], F32), internal("xs1", [S, 1024], F32)]
            sc = {n: internal("sc_" + n, s, d) for (n, s, d) in A_OUT(S)}
            mT = internal("sc_mT", [8, 128, S], BF16)
            xin = x0
            for li, l in enumerate(layers):
                moe_l = (l % 2 == 1)
                w = {n: ext_in("L%d_%s" % (l, n), s, d) for (n, s, d) in A_IN(S)}
                w["cos"], w["sin"] = cos, sin
                phase_A(g, nc, P, S, xin, w, sc, cst)
                i_ = dict(sc)
                i_["bias"], i_["cfar"] = bias, cfar
                i_["lamp"] = ext_in("L%d_lamp" % l, [128, 256], F32)
                i_["dnorm"] = ext_in("L%d_dnorm" % l, [128, 128], F32)
                PHASE_B(g, nc, P, S, 4, 4, i_, mT, cst, lam_init_of(l))
                wc = {n: ext_in("L%d_%s" % (l, n), s, d) for (n, s, d) in C_W(moe_l)}
                xo = xout if li == len(layers) - 1 else xs[li % 2]
                phase_C(g, nc, P, S, moe_l, xin, mT, wc, xo, cst)
                xin = xo
        g.barrier()
        g.flush()
    return nc


WIN_PERM = np.concatenate([np.arange(0, 704), np.arange(672, 704), np.arange(640, 672), np.arange(704, 2240)])
_uq = []
for _h in range(4):
    _uq.append(np.arange(_h * 192, _h * 192 + 128))
for _h in range(4):
    _uq.append(np.arange(_h * 192 + 128, _h * 192 + 192))
for _h in range(4):
    _uq.append(np.concatenate([np.arange(_h * 192 + 160, _h * 192 + 192), np.arange(_h * 192 + 128, _h * 192 + 160)]))
WUQ_PERM = np.concatenate(_uq)


def rope_tables(pos):
    inv = (1.0 / (np.float32(10000.0) ** (np.arange(0, 64, 2, dtype=np.float32) / np.float32(64)))).astype(np.float32)
    ang = pos.astype(np.float32)[None, :] * inv[:, None]
    c = np.cos(ang).astype(np.float32)
    s = np.sin(ang).astype(np.float32)
    cos_t = np.concatenate([c, c, c, c], 0)
    sin_t = np.concatenate([-s, s, -s, s], 0)
    return np.ascontiguousarray(cos_t), np.ascontiguousarray(sin_t)


def bucket_index(S):
    n = np.arange(S, dtype=np.int32)
    nf = np.maximum(n, 1).astype(np.float32)
    large = 16 + (np.log(nf / np.float32(16)) / np.float32(math.log(128 / 16)) * np.float32(16)).astype(np.int32)
    large = np.minimum(large, 31)
    return np.where(n < 16, n, large)


def bias_tiles(rel_bias, S):
    bidx = bucket_index(max(S, 1024))
    k = np.arange(128)[:, None]
    q = np.arange(512)[None, :]
    out = np.empty((4, 5, 128, 512), np.float32)
    for r in range(-1, 4):
        dist = np.clip(q - 128 * r - k, 0, len(bidx) - 1)
        out[:, r + 1] = np.transpose(rel_bias[bidx[dist]], (2, 0, 1))
    return out


def layer_A_weights(inp, l):
    return {"win": np.ascontiguousarray(inp["w_in"][l][:, WIN_PERM]),
            "wuq": np.ascontiguousarray(inp["mla_w_uq"][l][:, WUQ_PERM]),
            "wuk": np.ascontiguousarray(inp["mla_w_uk"][l]), "wuv": np.ascontiguousarray(inp["mla_w_uv"][l]),
            "gq": np.ascontiguousarray(inp["mla_q_norm"][l].reshape(3, 128).T),
            "gkv": np.ascontiguousarray(inp["mla_kv_norm"][l].reshape(2, 128).T)}


def layer_B_weights(inp, l):
    return {"lamp": np.ascontiguousarray(np.broadcast_to(inp["diff_lambda"][l].reshape(1, 256), (128, 256))),
            "dnorm": np.ascontiguousarray(np.broadcast_to(inp["diff_norm"][l].reshape(1, 128), (128, 128)))}


def layer_C_weights(inp, l):
    ln = np.stack([inp["ln1_g"][l], inp["ln1_b"][l], inp["ln2_g"][l], inp["ln2_b"][l]], 0)
    w = {"wo": np.ascontiguousarray(inp["w_o"][l]), "ln": np.ascontiguousarray(np.broadcast_to(ln[None], (128, 4, 1024)))}
    if l % 2 == 1:
        m = l // 2
        w["router"] = np.ascontiguousarray(inp["moe_router"][m].reshape(8, 128, 8).transpose(1, 0, 2))
        w["wg"], w["wu"], w["wd"] = inp["moe_w_gate"][m], inp["moe_w_up"][m], inp["moe_w_down"][m]
    else:
        m = l // 2
        w["wg"], w["wu"], w["wd"] = inp["ffn_w_gate"][m], inp["ffn_w_up"][m], inp["ffn_w_down"][m]
    return w


MODE = "unfused8"
S_FULL = 4096
_PROG_CACHE = {}


def _prog(layers):
    key = tuple(layers)
    if key not in _PROG_CACHE:
        _PROG_CACHE[key] = build("F", S=S_FULL, layers=key)
    return _PROG_CACHE[key]


def _layer_inputs(inp, l):
    d = {}
    for part in (layer_A_weights(inp, l), layer_B_weights(inp, l), layer_C_weights(inp, l)):
        for k, v in part.items():
            d["L%d_%s" % (l, k)] = np.ascontiguousarray(v, dtype=np.float32)
    return d


def _kernel_unfused8(inp):
    x = np.ascontiguousarray(inp["x"], dtype=np.float32)
    B = x.shape[0]
    H = S_FULL // 2
    consts = host_consts()
    cs = [rope_tables(np.arange(h * H, (h + 1) * H)) for h in range(2)]
    bias_all = bias_tiles(inp["rel_bias"].astype(np.float32), S_FULL)
    cfar_all = np.ascontiguousarray(np.broadcast_to(inp["rel_bias"][31][None, :], (128, 4)), dtype=np.float32)
    key = ("A8",)
    if key not in _PROG_CACHE:
        _PROG_CACHE[key] = build("A", TOK=H)
    progA = _PROG_CACHE[key]
    cur = [[x[b, h * H:(h + 1) * H] for h in range(2)] for b in range(B)]
    cores = [(b, p) for b in range(B) for p in range(2)]
    for l in range(DEPTH):
        moe = (l % 2 == 1)
        wa = {k: np.ascontiguousarray(v, dtype=np.float32) for k, v in layer_A_weights(inp, l).items()}
        in_maps = []
        for (b, h) in cores:
            m = dict(consts)
            m.update(wa)
            m["x"] = np.ascontiguousarray(cur[b][h])
            m["cos"], m["sin"] = cs[h]
            in_maps.append(m)
        ra = run_bass_kernel_spmd(progA, in_maps, core_ids=list(range(8))).results
        A = {(b, h): ra[2 * b + h] for (b, h) in cores}
        key = ("B8", l)
        if key not in _PROG_CACHE:
            _PROG_CACHE[key] = build("B", S=S_FULL, NM=2, ND=2, lam_l=l)
        progB = _PROG_CACHE[key]
        wb = layer_B_weights(inp, l)
        in_maps = []
        for (b, p) in cores:
            def cat_t(name, sl):
                return np.ascontiguousarray(np.concatenate([np.asarray(A[(b, h)]["o_" + name])[sl] for h in range(2)], axis=-1))
            def cat_r(name, c0):
                return np.ascontiguousarray(np.concatenate([np.asarray(A[(b, h)]["o_" + name])[:, c0:c0 + 256] for h in range(2)], axis=0))
            hs = slice(2 * p, 2 * p + 2)
            m = dict(consts)
            m["qnT"], m["qrT"], m["knT"] = cat_t("qnT", hs), cat_t("qrT", hs), cat_t("knT", hs)
            m["krT"] = cat_t("krT", slice(None))
            m["dqT"], m["dkT"] = cat_t("dqT", hs), cat_t("dkT", hs)
            m["vm"], m["dv"] = cat_r("vm", 256 * p), cat_r("dv", 256 * p)
            m["bias"] = np.ascontiguousarray(bias_all[2 * p:2 * p + 2])
            m["cfar"] = np.ascontiguousarray(cfar_all[:, 2 * p:2 * p + 2])
            m["lamp"] = np.ascontiguousarray(wb["lamp"], dtype=np.float32)
            m["dnorm"] = np.ascontiguousarray(wb["dnorm"], dtype=np.float32)
            in_maps.append(m)
        rb = run_bass_kernel_spmd(progB, in_maps, core_ids=list(range(8))).results
        Bm = {(b, p): np.asarray(rb[2 * b + p]["o_mT"]) for (b, p) in cores}
        key = ("C8", moe)
        if key not in _PROG_CACHE:
            _PROG_CACHE[key] = build("C", TOK=H, moe=moe)
        progC = _PROG_CACHE[key]
        wc = {k: np.ascontiguousarray(v, dtype=np.float32) for k, v in layer_C_weights(inp, l).items()}
        in_maps = []
        for (b, h) in cores:
            ts = slice(h * H, (h + 1) * H)
            chunks = [Bm[(b, hd // 2)][hd % 2][:, ts] for hd in range(4)] + [Bm[(b, hd // 2)][2 + hd % 2][:, ts] for hd in range(4)]
            m = dict(consts)
            m.update(wc)
            m["x"] = np.ascontiguousarray(cur[b][h])
            m["mT"] = np.ascontiguousarray(np.stack(chunks, 0))
            in_maps.append(m)
        rc = run_bass_kernel_spmd(progC, in_maps, core_ids=list(range(8))).results
        cur = [[np.asarray(rc[2 * b + h]["o_x"], dtype=np.float32) for h in range(2)] for b in range(B)]
    return np.stack([np.concatenate(cur[b], 0) for b in range(B)], 0).astype(np.float32)


def kernel(**inputs):
    if MODE == "unfused8":
        return _kernel_unfused8({k: np.asarray(v) for k, v in inputs.items()})
    inp = {k: np.asarray(v) for k, v in inputs.items()}
    x = np.ascontiguousarray(inp["x"], dtype=np.float32)
    B = x.shape[0]
    shared = dict(host_consts())
    shared["cos"], shared["sin"] = rope_tables(np.arange(S_FULL))
    shared["bias"] = bias_tiles(inp["rel_bias"].astype(np.float32), S_FULL)
    shared["cfar"] = np.ascontiguousarray(np.broadcast_to(inp["rel_bias"][31][None, :], (128, 4)), dtype=np.float32)
    groups = [tuple(range(DEPTH))] if MODE == "fused4" else [(l,) for l in range(DEPTH)]
    cur = [x[b] for b in range(B)]
    for layers in groups:
        nc = _prog(layers)
        base = dict(shared)
        for l in layers:
            base.update(_layer_inputs(inp, l))
        in_maps = []
        for b in range(B):
            m = dict(base)
            m["x"] = np.ascontiguousarray(cur[b])
            in_maps.append(m)
        res = run_bass_kernel_spmd(nc, in_maps, core_ids=list(range(B)))
        cur = [np.asarray(res.results[b]["o_x"], dtype=np.float32) for b in range(B)]
    return np.stack(cur, 0).astype(np.float32)
```
